# Optimizing a Trainium2 kernel written in Bass

```python
import jax
import jax.numpy as jnp
from jax import lax
import numpy as np

D_MODEL = 1024
BATCH = 8
SEQ = 2048
DEPTH = 1
DEC_BATCH = 128
DEC_SEQ = 4
PAST_LEN = 16384
PAGE_SIZE = 128

N_META = 16
CONV_WIDTH = 3
CONV_DIM = D_MODEL // 2
CONV_GROUPS = 8
RET_HEADS = 8
RET_HEAD_DIM = (D_MODEL // 2) // RET_HEADS
RET_DIM = RET_HEADS * RET_HEAD_DIM
MIX_DIM = CONV_DIM + RET_DIM
IN_COLS = 3 * CONV_DIM + 4 * RET_DIM
D_FF = -(-8 * D_MODEL // (3 * 256)) * 256
CHUNK = 128
ROPE_BASE = 10000.0
EPS = 1e-6
GN_EPS = 1e-5

kernel_name = 'hymba_conv_retnet_step'


def _log_gamma():
    return jnp.asarray(np.log1p(-(2.0 ** (-5.0 - np.arange(RET_HEADS)))), dtype=jnp.float32)


def _rmsnorm(x, g):
    xf = x.astype(jnp.float32)
    y = xf * lax.rsqrt(jnp.mean(xf * xf, axis=-1, keepdims=True) + EPS)
    return (y * g.astype(jnp.float32)).astype(x.dtype)


def _rope(t, pos):
    half = RET_HEAD_DIM // 2
    inv = ROPE_BASE ** (-jnp.arange(half, dtype=jnp.float32) / half)
    ang = pos[:, None] * inv[None, :]
    cos = jnp.cos(ang)[None, :, None, :]
    sin = jnp.sin(ang)[None, :, None, :]
    t1, t2 = t[..., :half], t[..., half:]
    return jnp.concatenate([t1 * cos - t2 * sin, t1 * sin + t2 * cos], axis=-1)


def _retention_block(S, q, k, v, log_gamma):
    L = q.shape[1]
    idx = jnp.arange(L, dtype=jnp.float32)
    diff = idx[:, None] - idx[None, :]
    decay = jnp.where(diff[None] >= 0,
                      jnp.exp(jnp.maximum(diff, 0.0)[None] * log_gamma[:, None, None]), 0.0)
    scores = jnp.einsum('nihd,njhd->nhij', q, k) * decay[None]
    intra = jnp.einsum('nhij,njhe->nihe', scores, v)
    q_decay = jnp.exp((idx + 1.0)[:, None] * log_gamma[None, :])
    cross = jnp.einsum('nihd,nhde->nihe', q * q_decay[None, :, :, None], S)
    k_decay = jnp.exp((L - 1.0 - idx)[:, None] * log_gamma[None, :])
    S_new = (jnp.exp(L * log_gamma)[None, :, None, None] * S
             + jnp.einsum('njhd,njhe->nhde', k * k_decay[None, :, :, None], v))
    return intra + cross, S_new


def _retention(q, k, v, S0, log_gamma):
    N, L, H, D = q.shape
    lead = L % CHUNK
    outs = []
    S = S0
    if lead:
        o, S = _retention_block(S, q[:, :lead], k[:, :lead], v[:, :lead], log_gamma)
        outs.append(o)
    n_chunks = (L - lead) // CHUNK
    if n_chunks:
        def split(t):
            return t[:, lead:].reshape(N, n_chunks, CHUNK, H, D).swapaxes(0, 1)

        def step(S_c, qkv):
            o_c, S_c = _retention_block(S_c, qkv[0], qkv[1], qkv[2], log_gamma)
            return S_c, o_c

        S, o = lax.scan(step, S, (split(q), split(k), split(v)))
        outs.append(o.swapaxes(0, 1).reshape(N, n_chunks * CHUNK, H, D))
    return jnp.concatenate(outs, axis=1), S


def _layer(x, pos, conv_buf, ret_state, log_gamma, norm1_g, w_in, w_conv, ret_norm_g, w_out,
           norm2_g, w_gate, w_up, w_down):
    N, L, _ = x.shape
    xn = _rmsnorm(x, norm1_g)
    z = xn @ w_in
    cuts = [CONV_DIM, 2 * CONV_DIM, 3 * CONV_DIM, 3 * CONV_DIM + RET_DIM,
            3 * CONV_DIM + 2 * RET_DIM, 3 * CONV_DIM + 3 * RET_DIM]
    b_gate, c_gate, h, q, k, v, g = jnp.split(z, cuts, axis=-1)

    u = c_gate * h
    ext = jnp.concatenate([conv_buf.astype(u.dtype), u], axis=1)
    conv = sum(w_conv[j] * ext[:, j:j + L] for j in range(CONV_WIDTH))
    conv_out = b_gate * conv
    new_conv_buf = ext[:, L:]

    qh = _rope(q.astype(jnp.float32).reshape(N, L, RET_HEADS, RET_HEAD_DIM), pos)
    kh = _rope(k.astype(jnp.float32).reshape(N, L, RET_HEADS, RET_HEAD_DIM), pos) * (RET_HEAD_DIM ** -0.5)
    vh = v.astype(jnp.float32).reshape(N, L, RET_HEADS, RET_HEAD_DIM)
    o, S_new = _retention(qh, kh, vh, ret_state.astype(jnp.float32), log_gamma)
    mu = jnp.mean(o, axis=-1, keepdims=True)
    var = jnp.mean(jnp.square(o - mu), axis=-1, keepdims=True)
    o = ((o - mu) * lax.rsqrt(var + GN_EPS)).reshape(N, L, RET_DIM) * ret_norm_g.astype(jnp.float32)
    ret_out = (jax.nn.silu(g.astype(jnp.float32)) * o).astype(x.dtype)

    x = x + jnp.concatenate([conv_out, ret_out], axis=-1) @ w_out
    xn = _rmsnorm(x, norm2_g)
    x = x + (jax.nn.silu(xn @ w_gate) * (xn @ w_up)) @ w_down
    return x, new_conv_buf, S_new.astype(x.dtype)


def setup_inputs(seed: int = 0) -> dict:
    key = jax.random.key(seed)
    ks = jax.random.split(key, 16)
    f32 = jnp.float32

    def nrm(k, shape, scale):
        return jax.random.normal(k, shape, f32) * scale

    return {
        'x_prompt': nrm(ks[0], (BATCH, SEQ, D_MODEL), 1.0),
        'x_sample': nrm(ks[1], (DEC_BATCH, DEC_SEQ, D_MODEL), 1.0),
        'state_conv': nrm(ks[2], (DEPTH, DEC_BATCH, CONV_WIDTH - 1, CONV_DIM), 1.0),
        'state_ret': nrm(ks[3], (DEPTH, DEC_BATCH, RET_HEADS, RET_HEAD_DIM, RET_HEAD_DIM), 1.0),
        'meta_tokens': nrm(ks[4], (N_META, D_MODEL), 1.0),
        'norm1_g': 1.0 + nrm(ks[5], (DEPTH, D_MODEL), 0.02),
        'w_in': nrm(ks[6], (DEPTH, D_MODEL, IN_COLS), D_MODEL ** -0.5),
        'w_conv': nrm(ks[7], (DEPTH, CONV_WIDTH, CONV_DIM), CONV_WIDTH ** -0.5),
        'ret_norm_g': 1.0 + nrm(ks[8], (DEPTH, RET_DIM), 0.02),
        'w_out': nrm(ks[9], (DEPTH, MIX_DIM, D_MODEL), MIX_DIM ** -0.5),
        'norm2_g': 1.0 + nrm(ks[10], (DEPTH, D_MODEL), 0.02),
        'w_gate': nrm(ks[11], (DEPTH, D_MODEL, D_FF), D_MODEL ** -0.5),
        'w_up': nrm(ks[12], (DEPTH, D_MODEL, D_FF), D_MODEL ** -0.5),
        'w_down': nrm(ks[13], (DEPTH, D_FF, D_MODEL), D_FF ** -0.5),
        'final_norm_g': 1.0 + nrm(ks[14], (D_MODEL,), 0.02),
    }


def reference(x_prompt, x_sample, state_conv, state_ret, meta_tokens, norm1_g, w_in, w_conv,
              ret_norm_g, w_out, norm2_g, w_gate, w_up, w_down, final_norm_g):
    log_gamma = _log_gamma()
    n_p = x_prompt.shape[0]
    meta = jnp.broadcast_to(meta_tokens.astype(x_prompt.dtype)[None], (n_p, N_META, D_MODEL))
    xp = jnp.concatenate([meta, x_prompt], axis=1)
    xs = x_sample
    pos_p = jnp.arange(xp.shape[1], dtype=jnp.float32)
    pos_s = PAST_LEN + jnp.arange(xs.shape[1], dtype=jnp.float32)
    conv_p, ret_p, conv_s, ret_s = [], [], [], []
    for l in range(DEPTH):
        params = (norm1_g[l], w_in[l], w_conv[l], ret_norm_g[l], w_out[l], norm2_g[l],
                  w_gate[l], w_up[l], w_down[l])
        zero_conv = jnp.zeros((n_p, CONV_WIDTH - 1, CONV_DIM), xp.dtype)
        zero_ret = jnp.zeros((n_p, RET_HEADS, RET_HEAD_DIM, RET_HEAD_DIM), jnp.float32)
        xp, cb_p, rs_p = _layer(xp, pos_p, zero_conv, zero_ret, log_gamma, *params)
        xs, cb_s, rs_s = _layer(xs, pos_s, state_conv[l], state_ret[l], log_gamma, *params)
        conv_p.append(cb_p)
        ret_p.append(rs_p)
        conv_s.append(cb_s)
        ret_s.append(rs_s)
    y_prompt = _rmsnorm(xp, final_norm_g)[:, N_META:]
    y_sample = _rmsnorm(xs, final_norm_g)
    return (y_prompt, y_sample, jnp.stack(conv_p), jnp.stack(ret_p), jnp.stack(conv_s), jnp.stack(ret_s))
```

```python
import contextlib
import os
import numpy as np
import concourse.bass as bass
import concourse.mybir as mybir
from concourse.bass_utils import run_bass_kernel_spmd

F32 = mybir.dt.float32
BF16 = mybir.dt.bfloat16
AF = mybir.ActivationFunctionType
ALU = mybir.AluOpType
AX = mybir.AxisListType

D = 1024
SEQ = 2048
NMETA = 16
DFF = 2816
NFF = DFF // 128
INC = 3584
EPS = 1e-6
GN_EPS = 1e-5
PAST = 16384
ENGS = ("pe", "act", "dve", "pool", "sp")
SCHEDULE = True
A3_AFTER_RET = False
PE_COLD = float(os.environ.get("MK_PE_COLD", "1.0"))
HOP = float(os.environ.get("MK_HOP", "0.15"))
PE_BASE = float(os.environ.get("MK_PE_BASE", "0.03"))
ACT_BASE = float(os.environ.get("MK_ACT_BASE", "0.25"))
ACT_RATE = float(os.environ.get("MK_ACT_RATE", "1200"))
POOL_BASE = float(os.environ.get("MK_POOL_BASE", "1.5"))
DMA_LAT = float(os.environ.get("MK_DMA_LAT", "2.0"))
LATE_PE = os.environ.get("MK_LATE_PE", "1") == "1"
BOOST = os.environ.get("MK_BOOST", "0") == "1"
WIDE_PV = os.environ.get("MK_WIDE_PV", "0") == "1"
YQ = os.environ.get("MK_YQ", "act")
NF2 = os.environ.get("MK_NF2", "1") == "1"
JITTER = float(os.environ.get("MK_JITTER", "0.0"))
_RNG = np.random.RandomState(int(os.environ.get("MK_SEED", "0")))
STRICT = os.environ.get("MK_STRICT", "1") == "1"


class _Op:
    __slots__ = ("eng", "fn", "dma", "key", "deps", "odeps", "signal", "count", "idx", "cost", "start", "boost")


class _Fake:
    def __init__(self):
        self.rec = None

    def __getattr__(self, name):
        def f(*a, **k):
            self.rec = (name, a, k)
            return self
        return f


def _prod(t):
    r = 1
    for x in t:
        r *= int(x)
    return r


class Sched:
    def __init__(self, nc):
        self.nc = nc
        self.ops = []
        self.last_w = {}
        self.readers = {}
        self.excl = set()
        self.dma_keys = {}
        self.last_dma = {}
        self.auto = None
        self.pe_scale = 1.0

    def _cost(self, eng, fn, dma):
        fk = _Fake()
        try:
            fn(fk)
            name, a, k = fk.rec
            out = k.get("out", a[0] if a else None)
            shp = tuple(out.shape)
            F = _prod(shp[1:]) if len(shp) > 1 else 1
            if dma:
                c = _prod(shp) * 4 / 300e3
                if k.get("allow_slow_non_contiguous"):
                    c += _prod(shp) * 0.004
                return c
            if eng == "pe":
                if name == "matmul":
                    M = _prod(tuple(k["lhsT"].shape)[1:])
                    N = _prod(tuple(k["rhs"].shape)[1:])
                    return PE_BASE + max(N / 2400.0, M / 1200.0)
                return 0.1
            if eng == "dve":
                return 0.13 + F / 960.0
            if eng == "act":
                return ACT_BASE + F / ACT_RATE
            if eng == "pool":
                return POOL_BASE + F / 600.0
        except Exception:
            pass
        return 0.5

    def op(self, eng, fn, reads=(), writes=(), dma=False, key=None, nofence=False, boost=False):
        o = _Op()
        reads = list(reads)
        writes = list(writes)
        late = False
        if self.auto is not None and not nofence and self.auto not in writes:
            if eng == "pe" and LATE_PE:
                late = True
            else:
                reads.append(self.auto)
        o.eng, o.fn, o.dma, o.key = eng, fn, dma, key
        o.boost = boost and BOOST
        o.idx = len(self.ops)
        o.signal = False
        o.count = None
        o.cost = self._cost(eng, fn, dma) * (self.pe_scale if eng == "pe" else 1.0)
        if JITTER > 0:
            o.cost *= 1.0 + JITTER * (2.0 * _RNG.random() - 1.0)
        deps = {}
        for r in reads:
            w = self.last_w.get(r)
            if w is not None:
                deps[w] = True
            if r in self.excl:
                for x in self.readers.get(r, ()):
                    if self.ops[x].eng != eng:
                        deps.setdefault(x, False)
        for w_ in writes:
            w = self.last_w.get(w_)
            if w is not None:
                deps.setdefault(w, False)
            for x in self.readers.get(w_, ()):
                deps.setdefault(x, False)
        fin = []
        for d, raw in deps.items():
            Dd = self.ops[d]
            if Dd.dma:
                fin.append(d)
            elif Dd.eng == eng and not dma:
                if eng in ("act", "dve", "pool") and (raw or STRICT):
                    fin.append(d)
            else:
                fin.append(d)
        o.deps = fin
        o.odeps = set(deps.keys())
        if dma and key in self.last_dma:
            o.odeps.add(self.last_dma[key])
        for r in reads:
            self.readers.setdefault(r, []).append(o.idx)
        if late:
            self.readers.setdefault(self.auto, []).append(o.idx)
        for w_ in writes:
            self.last_w[w_] = o.idx
            self.readers[w_] = []
        if dma:
            self.dma_keys[key] = self.dma_keys.get(key, 0) + 16
            o.count = self.dma_keys[key]
            self.last_dma[key] = o.idx
        self.ops.append(o)
        return o

    def schedule(self, K=48):
        import heapq
        ops = self.ops
        n = len(ops)
        succ = [[] for _ in range(n)]
        npred = [0] * n
        for o in ops:
            npred[o.idx] = len(o.odeps)
            for d in o.odeps:
                succ[d].append(o.idx)
        ready_t = [0.0] * n
        fin_t = [0.0] * n
        rank = [0.0] * n
        for i in range(n - 1, -1, -1):
            m = 0.0
            for j in succ[i]:
                if rank[j] > m:
                    m = rank[j]
            rank[i] = ops[i].cost + m + (DMA_LAT if ops[i].dma else 0.0) + HOP + (1e6 if ops[i].boost else 0.0)
        cand = {e: [] for e in ENGS}
        for o in ops:
            if npred[o.idx] == 0:
                cand[o.eng].append(o.idx)
        free = {e: 0.0 for e in ENGS}
        dma_free = 0.0
        order = {e: [] for e in ENGS}
        done = 0
        while done < n:
            best = None
            for e in ENGS:
                h = cand[e]
                if not h:
                    continue
                T = free[e]
                rd = [i for i in h if ready_t[i] <= T]
                if rd:
                    pick = min(rd, key=lambda i: (-rank[i], i))
                else:
                    pick = min(h, key=lambda i: (ready_t[i], -rank[i], i))
                st = max(T, ready_t[pick])
                if best is None or st < best[0]:
                    best = (st, e, pick)
            st, e, i = best
            cand[e].remove(i)
            o = ops[i]
            o.start = st
            if o.dma:
                t0 = max(st, dma_free)
                dma_free = t0 + o.cost
                free[e] = st + (o.cost if e == "pool" else 0.1)
                fin_t[i] = dma_free + DMA_LAT
            else:
                free[e] = st + o.cost
                fin_t[i] = free[e]
            order[e].append(o)
            done += 1
            for j in succ[i]:
                f_ = fin_t[i] + (HOP if ops[j].eng != e else 0.0)
                if f_ > ready_t[j]:
                    ready_t[j] = f_
                npred[j] -= 1
                if npred[j] == 0:
                    cand[ops[j].eng].append(j)
        self.sim_time = max(fin_t) if n else 0.0
        return order

    def emit(self, final_wait_keys=(), schedule=True):
        nc = self.nc
        ops = self.ops
        for o in ops:
            for d in o.deps:
                if not ops[d].dma:
                    ops[d].signal = True
        if schedule:
            per_eng = self.schedule()
        else:
            per_eng = {e: [o for o in ops if o.eng == e] for e in ENGS}
        for e in ENGS:
            c = 0
            for o in per_eng[e]:
                if not o.dma and o.signal:
                    c += 1
                    o.count = c
        with contextlib.ExitStack() as st:
            esem = {e: st.enter_context(nc.semaphore("s_" + e)) for e in ENGS}
            dsem = {k: st.enter_context(nc.semaphore("d_%d" % i)) for i, k in enumerate(self.dma_keys)}
            block = st.enter_context(nc.Block())

            def run(engname, eng):
                waited = {}
                for o in per_eng[engname]:
                    need = {}
                    for d in o.deps:
                        Dd = ops[d]
                        s = ("d", Dd.key) if Dd.dma else ("e", Dd.eng)
                        if Dd.count > need.get(s, 0):
                            need[s] = Dd.count
                    for s, v in need.items():
                        if waited.get(s, 0) >= v:
                            continue
                        waited[s] = v
                        eng.wait_ge(dsem[s[1]] if s[0] == "d" else esem[s[1]], v)
                    ins = o.fn(eng)
                    if o.dma:
                        ins.then_inc(dsem[o.key], 16)
                    elif o.signal:
                        ins.then_inc(esem[engname], 1)
                if engname == "sp":
                    for k in final_wait_keys:
                        eng.wait_ge(dsem[k], self.dma_keys[k])

            block.sync(lambda e: run("sp", e))
            block.tensor(lambda e: run("pe", e))
            block.scalar(lambda e: run("act", e))
            block.vector(lambda e: run("dve", e))
            block.gpsimd(lambda e: run("pool", e))


def _tables():
    half = 32
    inv = 10000.0 ** (-np.arange(half, dtype=np.float64) / half)
    lg = np.log1p(-(2.0 ** (-5.0 - np.arange(8)))).astype(np.float32).astype(np.float64)
    cs = np.zeros((128, 18, 2, 32), np.float32)
    for t in range(18):
        i = np.arange(128)
        if t == 0:
            pos = i.astype(np.float64)
        elif t <= 16:
            pos = (16 + 128 * (t - 1) + i).astype(np.float64)
        else:
            pos = (PAST + (i % 4)).astype(np.float64)
        ang = pos[:, None] * inv[None, :]
        cs[:, t, 0] = np.cos(ang)
        cs[:, t, 1] = np.sin(ang)
    tabs = np.zeros((128, 336), np.float32)
    i = np.arange(128)
    for kind in range(2):
        il = i if kind == 0 else (i % 4)
        tabs[:, kind * 8:kind * 8 + 8] = np.exp(-(il[:, None] + 1.0) * lg[None, :]) * 0.125
        tabs[:, 16 + kind * 8:16 + kind * 8 + 8] = GN_EPS * np.exp(-2.0 * (il[:, None] + 1.0) * lg[None, :])
    for k, L in enumerate((128.0, 16.0, 4.0)):
        tabs[:, 32 + k * 8:32 + k * 8 + 8] = np.exp(L * lg)[None, :]
    tabs[:, 56:72] = (i[:, None] // 4 == np.arange(16)[None, :]).astype(np.float32)
    tabs[:, 72:80] = -0.5
    m0 = (i[None, :] >= i[:, None]).astype(np.float32)
    m1 = m0 * (i[None, :] // 4 == i[:, None] // 4)
    tabs[:, 80:208] = m0
    tabs[:, 208:336] = m1
    smT = np.zeros((64, 16, 64), np.float32)
    for s in range(16):
        smT[:, s, 4 * s:4 * s + 4] = 1.0
    return cs.reshape(128, 18 * 64), tabs, np.eye(128, dtype=np.float32), smT.reshape(64, 1024)


def build_nc():
    nc = bass.Bass("TRN2", target_bir_lowering=False)
    S = Sched(nc)

    def din(name, shape):
        return nc.dram_tensor(name, list(shape), F32, kind="ExternalInput").ap()

    def dout(name, shape):
        return nc.dram_tensor(name, list(shape), F32, kind="ExternalOutput").ap()

    xp = din("xp", [SEQ, D]); meta = din("meta", [NMETA, D]); xsamp = din("xsamp", [64, D])
    sconv = din("sconv", [16, 2, 512]); sret = din("sret", [16, 8, 64, 64])
    w_conv = din("w_conv", [3, 512]); rng_d = din("rng", [512])
    n1 = din("n1", [D]); n2 = din("n2", [D]); nf = din("nf", [D])
    cwb = din("cwb", [4, 128, 8 * 384])
    wretb = din("wretb", [128, 8 * 2048])
    woutb = din("woutb", [128, 8 * D])
    gub = din("gub", [NFF, 128, 8 * 256])
    wdb = din("wdb", [4, 128, 11 * 512])
    cs_d = din("cs", [128, 18 * 64]); tabs_d = din("tabs", [128, 336]); ident_d = din("identf", [128, 128])
    smT_d = din("smT", [64, 1024])
    yp = dout("yp", [SEQ, D]); ys = dout("ys", [64, D]); ncp = dout("ncp", [2, 512]); nrp = dout("nrp", [8, 64, 64])
    ncs = dout("ncs", [16, 2, 512]); nrs = dout("nrs", [16, 8, 64, 64])

    def A(name, shape, dt):
        return nc.alloc_sbuf_tensor("sb_" + name, shape, dt)

    X = A("X", [128, 9, D], F32)
    XT = A("XT", [128, 8, 1088], BF16)
    AL = A("AL", [128, 25088], BF16)
    WRET = AL[:, 0:16384].rearrange("p (k n) -> p k n", k=8)
    MIXT = AL[:, 16384:16384 + 8704].rearrange("p (k n) -> p k n", k=8)
    HT = AL[:, 0:NFF * 1088].rearrange("p (f n) -> p f n", f=NFF)
    AL2 = A("AL2", [128, 43584], BF16)
    _off = [0]

    def carve(nelem_bf16):
        o = _off[0]
        _off[0] += nelem_bf16
        assert _off[0] <= 43584, _off[0]
        return AL2[:, o:o + nelem_bf16]

    def carve_f32(nelem):
        return carve(2 * nelem).bitcast(F32)

    CW = carve(2 * 8 * 384).rearrange("p (s k n) -> p s k n", s=2, k=8)
    WOUT = carve(8 * D).rearrange("p (k n) -> p k n", k=8)
    r1 = _off[0]
    hsb = carve_f32(2 * 512).rearrange("p (s n) -> p s n", s=2)
    ubuf = carve_f32(520)
    cacc = carve_f32(512)
    ext = carve_f32(96).rearrange("p (s t) -> p s t", s=16)
    conv_end = _off[0]
    _off[0] = r1

    def ret_set():
        d = {}
        d["qrot"] = carve(512); d["ktil"] = carve(512); d["vbf"] = carve(512)
        d["sg"] = carve_f32(512)
        d["QT"] = carve(1024).rearrange("p (h n) -> p h n", h=8)
        d["KT"] = carve(1024).rearrange("p (h n) -> p h n", h=8)
        d["PT"] = carve(1024).rearrange("p (h n) -> p h n", h=8)
        d["osq"] = carve_f32(512)
        d["ro"] = carve(512)
        return d

    RS = [None, None]
    RS[1] = ret_set()
    _off[0] = max(_off[0], conv_end)
    RS[0] = ret_set()
    t1 = carve_f32(512); t2 = carve_f32(512)
    on = carve_f32(512)
    Stmp = carve_f32(512)
    Sbf = carve(512).rearrange("p (h e) -> p h e", h=8)
    QTm = carve(2 * 512).rearrange("p (s h n) -> p s h n", s=2, h=8)
    km = carve(2 * 512).rearrange("p (s n) -> p s n", s=2)
    S0b = carve_f32(3 * 512).rearrange("p (s h e) -> p s h e", s=3, h=8)
    S0bf = carve(2 * 512).rearrange("p (s h e) -> p s h e", s=2, h=8)
    S0t = carve_f32(2 * 512).rearrange("p (s h e) -> p s h e", s=2, h=8)
    So = carve_f32(2 * 512).rearrange("p (s h e) -> p s h e", s=2, h=8)
    endA = _off[0]
    _off[0] = 0
    GU = carve(3 * 8 * 256).rearrange("p (s k n) -> p s k n", s=3, k=8)
    WD = carve(3 * 11 * 512).rearrange("p (s f n) -> p s f n", s=3, f=11)
    sgate = carve_f32(2 * 512).rearrange("p (s n) -> p s n", s=2)
    YO = carve_f32(2 * D).rearrange("p (s n) -> p s n", s=2)
    gf = carve_f32(D)
    endB = _off[0]
    xs2 = A("xs", [128, 2, D], BF16)
    stat = A("stat", [128, 18, 12], F32)
    gst = A("gst", [128, 2, 48], F32)
    S32 = A("S32", [64, 8, 64], F32)
    ucar = A("ucar", [128, 4, 2], F32)
    cst = A("cst", [128, 4, 2], F32)
    cso = A("cso", [128, 4, 16, 2], F32)
    csi = A("csi", [128, 4, 16, 2], F32)
    idf = A("idf", [128, 128], F32)
    rowb = A("rowb", [32, 512], F32)
    rowc = A("rowc", [8, 3, 128], F32)
    cs_t = A("cs_t", [128, 18, 2, 32], F32)
    tabs = A("tabs", [128, 336], F32)
    ident = A("ident", [128, 128], BF16)
    smT = A("smT", [64, 16, 64], BF16)
    wc = A("wc", [128, 4, 3], F32)
    g1 = A("g1", [128, 8], F32); g2 = A("g2", [128, 8], F32)
    rngt = A("rngt", [128, 4], F32)
    fsc = A("fsc", [128, 8], F32)
    mk = A("mk", [128, 8], F32)
    kdec = tabs[:, 0:16].rearrange("p (k h) -> p k h", k=2)
    epsq = tabs[:, 16:32].rearrange("p (k h) -> p k h", k=2)
    gl = tabs[:, 32:56].rearrange("p (k h) -> p k h", k=3)
    seqmask = tabs[:, 56:72]
    neghalf = tabs[:, 72:80]
    maskt = tabs[:, 80:336].rearrange("p (k n) -> p k n", k=2)
    PB = [nc.alloc_psum_tensor("pb%d" % i, [128, 512], F32) for i in range(8)]
    PBn = ["pb%d" % i for i in range(8)]
    S.excl.update(PBn)

    def pbf(i):
        return PB[i][:, :].bitcast(BF16)

    def ld(q, dst, src, res, boost=True, **kw):
        S.op(q, lambda e: e.dma_start(out=dst, in_=src, **kw), writes=[res], dma=True, key=res, nofence=True,
             boost=boost)

    ld("sp", tabs[:, :], tabs_d, "tabs")
    ld("pool", ident[:, :], ident_d, "ident")
    ld("sp", idf[:, :], ident_d, "idf")
    ld("sp", rowc[:, 0, :], n1.rearrange("(k p) -> k p", p=128), "rowc")
    ld("sp", rowc[:, 1, :], n2.rearrange("(k p) -> k p", p=128), "rowc")
    ld("sp", rowc[0:4, 2, :], rng_d.rearrange("(k p) -> k p", p=128), "rowc")
    ld("sp", rowb[0:3, :], w_conv, "rowb")

    def tr32(out_ps, lhsT, kk, reads):
        S.op("pe", lambda e: e.matmul(out_ps, lhsT=lhsT, rhs=idf[:kk, :kk], start=True, stop=True),
             reads=list(reads) + ["idf"], writes=["pb7"], nofence=True, boost=True)

    tr32(PB[7][:, 0:8], rowc[:8, 0, :], 8, ["rowc"])
    tr32(PB[7][:, 8:16], rowc[:8, 1, :], 8, ["rowc"])
    tr32(PB[7][:, 16:20], rowc[:4, 2, :], 4, ["rowc"])
    for cc_ in range(4):
        tr32(PB[7][:, 20 + 3 * cc_:23 + 3 * cc_], rowb[:3, cc_ * 128:(cc_ + 1) * 128], 3, ["rowb"])
    S.op("dve", lambda e: e.tensor_copy(out=g1[:, :], in_=PB[7][:, 0:8]), reads=["pb7"], writes=["g1"], nofence=True,
         boost=True)
    S.op("dve", lambda e: e.tensor_copy(out=g2[:, :], in_=PB[7][:, 8:16]), reads=["pb7"], writes=["g2"], nofence=True,
         boost=True)
    S.op("dve", lambda e: e.tensor_copy(out=rngt[:, :], in_=PB[7][:, 16:20]), reads=["pb7"], writes=["rngt"],
         nofence=True, boost=True)
    S.op("dve", lambda e: e.tensor_copy(out=wc[:, :, :].rearrange("p c j -> p (c j)"), in_=PB[7][:, 20:32]),
         reads=["pb7"], writes=["wc"], nofence=True, boost=True)

    def late_tables():
        ld("sp", cs_t[:, :, :, :], cs_d.rearrange("p (t k d) -> p t k d", t=18, k=2), "cs_t", boost=False)
        ld("pool", smT[:, :, :], smT_d.rearrange("p (s n) -> p s n", s=16), "smT", boost=False)

    S.op("dve", lambda e: e.memset(S32[:, :, :], 0.0), writes=["S32"], nofence=True)
    S.op("dve", lambda e: e.memset(fsc[:, :], 0.0), writes=["PH"], nofence=True)
    S.auto = "PH"

    def fence():
        S.op("dve", lambda e: e.memset(fsc[:, :], 0.0), writes=["PH"])

    def tiles_of(hf):
        if hf == 0:
            return [(0, 16, 0, 0)] + [(t, 128, t, 16 + 128 * (t - 1)) for t in range(1, 9)]
        return [(t, 128, t - 9, 128 * (t - 9)) for t in range(9, 17)] + [(17, 64, 8, 1024)]

    def split(c0, c1, k):
        b = [c0 + (c1 - c0) * i // k for i in range(k + 1)]
        return [(b[i], b[i + 1]) for i in range(k)]

    def overlapping(tl, c0, c1):
        return [j for (t, n, j, col) in tl if col < c1 and col + n > c0]

    pt_rot = [0]

    def XR(j):
        return ["X%da" % j, "X%db" % j]

    def norm_stats(t, n, j, sc, xsl, nf=False, src=None, src_res=None):
        src = X[:n, j, :] if src is None else src
        src_res = XR(j) if src_res is None else src_res
        S.op("act", lambda e: e.activation(out=xs2[:n, xsl, :], in_=src, func=AF.Square,
                                           accum_out=stat[:n, t, sc:sc + 1]),
             reads=src_res, writes=["xs%d" % xsl, "st%d.%d" % (t, sc)], nofence=nf)
        S.op("dve", lambda e: e.tensor_scalar(out=stat[:n, t, sc + 1:sc + 2], in0=stat[:n, t, sc:sc + 1],
                                              scalar1=1.0 / D, scalar2=EPS, op0=ALU.mult, op1=ALU.add),
             reads=["st%d.%d" % (t, sc)], writes=["st%d.%d" % (t, sc + 1)], nofence=nf)
        S.op("pool", lambda e: e.tensor_tensor(out=stat[:n, t, sc + 2:sc + 3], in0=stat[:n, t, sc + 1:sc + 2],
                                               in1=neghalf[:n, 0:1], op=ALU.pow),
             reads=["st%d.%d" % (t, sc + 1), "tabs"], writes=["st%d.%d" % (t, sc + 2)], nofence=nf)

    def norm_apply(t, n, j, col, gain, sc, banks, xsl, nf=False, on_dve=False, src=None, src_res=None):
        src = X[:n, j, :] if src is None else src
        src_res = XR(j) if src_res is None else src_res
        bank = banks[pt_rot[0] % len(banks)]
        xs = xs2[:, xsl, :]
        xsn = "xs%d" % xsl
        pt_rot[0] += 1
        if on_dve:
            S.op("dve", lambda e: e.tensor_scalar(out=xs[:n, :], in0=src, scalar1=stat[:n, t, sc + 2:sc + 3],
                                                  scalar2=None, op0=ALU.mult),
                 reads=src_res + ["st%d.%d" % (t, sc + 2)], writes=[xsn], nofence=nf)
        else:
            S.op("act", lambda e: e.activation(out=xs[:n, :], in_=src, func=AF.Copy,
                                               scale=stat[:n, t, sc + 2:sc + 3]),
                 reads=src_res + ["st%d.%d" % (t, sc + 2)], writes=[xsn], nofence=nf)
        pv = pbf(bank).rearrange("p (k n) -> p k n", k=8)
        for kc in range(8):
            S.op("pe", lambda e, kc=kc: e.transpose(out=pv[:, kc, :n], in_=xs[:n, kc * 128:(kc + 1) * 128],
                                                    identity=ident[:n, :n]),
                 reads=[xsn, "ident"], writes=[PBn[bank]], nofence=nf)
        S.op("dve", lambda e: e.tensor_tensor(out=XT[:, :, col:col + n], in0=pv[:, :, :n],
                                              in1=gain[:, :].unsqueeze(2).to_broadcast([128, 8, n]), op=ALU.mult),
             reads=[PBn[bank], "g1", "g2"], writes=["XT%d" % j], nofence=nf)

    out_keys = []
    CWR = {0: ["GU0", "GU1a"], 1: ["GU1b", "GU2"]}
    GUR = {0: ["GU0"], 1: ["GU1a", "GU1b"], 2: ["GU2"]}

    for hf in range(2):
        tl = tiles_of(hf)
        if hf == 1:
            fence()
        for (t, n, j, col) in tl:
            if t == 0:
                src = meta
            elif t <= 16:
                src = xp[128 * (t - 1):128 * t, :]
            else:
                src = xsamp
            S.op("sp", lambda e, src=src, n=n, j=j: e.dma_start(out=X[:n, j, :], in_=src),
                 reads=(["X2a"] if (hf == 0 and j > 2 and os.environ.get("MK_XPRIO", "1") == "1") else []),
                 writes=XR(j), dma=True, key="X%d" % j, nofence=True)
        def load_cw(cc):
            sl = cc % 2
            S.op("pool", lambda e: e.dma_start(out=CW[:, sl, :, :].rearrange("p k n -> p (k n)"), in_=cwb[cc]),
                 reads=(["X6a"] if (hf == 0 and cc == 1) else []), writes=CWR[sl], dma=True, key="CW%d" % sl,
                 nofence=True)

        load_cw(0)
        load_cw(1)
        if hf == 0:
            late_tables()
        S.op("act", lambda e: e.copy(out=Sbf[:64, :, :], in_=S32[:, :, :]), reads=["S32"], writes=["Sbf"])
        if hf == 0:
            for i_, (t, n, j, col) in enumerate(tl):
                norm_stats(t, n, j, 0, i_ % 2, nf=True)
                if i_ >= 1:
                    (t_, n_, j_, col_) = tl[i_ - 1]
                    norm_apply(t_, n_, j_, col_, g1, 0, [0, 1], (i_ - 1) % 2, nf=True, on_dve=True)
            (t_, n_, j_, col_) = tl[-1]
            norm_apply(t_, n_, j_, col_, g1, 0, [0, 1], (len(tl) - 1) % 2, nf=True, on_dve=True)
        def load_wret(blk):
            S.op("pool", lambda e: e.dma_start(
                out=WRET[:, 2 * blk:2 * blk + 2, :].rearrange("p k n -> p (k n)"),
                in_=wretb[:, blk * 4096:(blk + 1) * 4096]), writes=["WRET"], dma=True, key="WRET")

        def load_wout(cb):
            S.op("pool", lambda e: e.dma_start(
                out=WOUT[:, 4 * cb:4 * cb + 4, :].rearrange("p k n -> p (k n)"),
                in_=woutb[:, cb * 4096:(cb + 1) * 4096]), writes=["WOUT"], dma=True, key="WOUT")

        if hf == 0:
            if os.environ.get("MK_UNEVEN", "0") == "1":
                pch = [(0, 144, "p"), (144, 592, "p"), (592, 1040, "p")]
            else:
                pch = [(a, b, "p") for (a, b) in split(0, 1040, 3)]
        else:
            pch = [(a, b, "p") for (a, b) in split(0, 1024, 3)] + [(1024, 1088, "s")]
        n_p = len([x for x in pch if x[2] == "p"])

        def conv_chunk(cc, ci, c0, c1, kind, sl, bks, hs):
            N = c1 - c0
            rd = ["XT%d" % j for j in overlapping(tl, c0, c1)] + CWR[sl]
            for br in range(3):
                for kc in range(8):
                    S.op("pe", lambda e, br=br, kc=kc: e.matmul(
                        PB[bks[br]][:, :N], lhsT=CW[:, sl, kc, br * 128:(br + 1) * 128], rhs=XT[:, kc, c0:c1],
                        start=(kc == 0), stop=(kc == 7)), reads=rd, writes=[PBn[bks[br]]])
            pb_, pc_, ph_ = bks
            S.op("act", lambda e: e.copy(out=hsb[:, hs, :N], in_=PB[ph_][:, :N]),
                 reads=[PBn[ph_]], writes=["hsb%d" % hs])
            mxw = ["MXc%d.%d" % (j, cc) for j in overlapping(tl, c0, c1)]
            if kind == "p":
                if ci == 0:
                    if hf == 0:
                        S.op("dve", lambda e: e.memset(ubuf[:, 0:2], 0.0), writes=["ubuf"])
                    else:
                        S.op("dve", lambda e: e.tensor_copy(out=ubuf[:, 0:2], in_=ucar[:, cc, :]),
                             reads=["ucar%d" % cc], writes=["ubuf"])
                S.op("dve", lambda e: e.tensor_tensor(
                    out=ubuf[:, 2:2 + N], in0=PB[pc_][:, :N], in1=hsb[:, hs, :N], op=ALU.mult),
                    reads=[PBn[pc_], "hsb%d" % hs], writes=["ubuf"])
                S.op("act", lambda e: e.activation(out=cacc[:, :N], in_=ubuf[:, 2:2 + N], func=AF.Copy,
                                                   scale=wc[:, cc, 2:3]),
                     reads=["ubuf", "wc"], writes=["cacc"])
                S.op("dve", lambda e: e.scalar_tensor_tensor(
                    out=cacc[:, :N], in0=ubuf[:, 1:1 + N], scalar=wc[:, cc, 1:2], in1=cacc[:, :N],
                    op0=ALU.mult, op1=ALU.add), reads=["ubuf", "wc", "cacc"], writes=["cacc"])
                S.op("dve", lambda e: e.scalar_tensor_tensor(
                    out=cacc[:, :N], in0=ubuf[:, 0:N], scalar=wc[:, cc, 0:1], in1=cacc[:, :N],
                    op0=ALU.mult, op1=ALU.add), reads=["ubuf", "wc", "cacc"], writes=["cacc"])
                S.op("dve", lambda e: e.tensor_tensor(
                    out=MIXT[:, cc, c0:c1], in0=PB[pb_][:, :N], in1=cacc[:, :N], op=ALU.mult),
                    reads=[PBn[pb_], "cacc"], writes=mxw)
                last_p = (ci == n_p - 1)
                if last_p and hf == 0:
                    S.op("dve", lambda e: e.tensor_copy(out=ucar[:, cc, :], in_=ubuf[:, N:N + 2]),
                         reads=["ubuf"], writes=["ucar%d" % cc])
                elif last_p:
                    S.op("dve", lambda e: e.tensor_copy(out=cst[:, cc, :], in_=ubuf[:, N:N + 2]),
                         reads=["ubuf"], writes=["cst%d" % cc])
                else:
                    S.op("dve", lambda e: e.tensor_copy(out=ubuf[:, 0:2], in_=ubuf[:, N:N + 2]),
                         reads=["ubuf"], writes=["ubuf"])
            else:
                S.op("dve", lambda e: e.tensor_copy(out=ext[:, :, 0:2], in_=csi[:, cc, :, :]), reads=["csi"],
                     writes=["ext"])
                pc3 = PB[pc_][:, :64].rearrange("p (s t) -> p s t", t=4)
                hs3 = hsb[:, hs, :64].rearrange("p (s t) -> p s t", t=4)
                ca3 = cacc[:, :64].rearrange("p (s t) -> p s t", t=4)
                S.op("dve", lambda e: e.tensor_tensor(out=ext[:, :, 2:6], in0=pc3, in1=hs3, op=ALU.mult),
                     reads=[PBn[pc_], "hsb%d" % hs], writes=["ext"])
                S.op("dve", lambda e: e.tensor_scalar(
                    out=ca3, in0=ext[:, :, 2:6], scalar1=wc[:, cc, 2:3], scalar2=None, op0=ALU.mult),
                    reads=["ext", "wc"], writes=["cacc"])
                S.op("dve", lambda e: e.scalar_tensor_tensor(
                    out=ca3, in0=ext[:, :, 1:5], scalar=wc[:, cc, 1:2], in1=ca3, op0=ALU.mult, op1=ALU.add),
                    reads=["ext", "wc", "cacc"], writes=["cacc"])
                S.op("dve", lambda e: e.scalar_tensor_tensor(
                    out=ca3, in0=ext[:, :, 0:4], scalar=wc[:, cc, 0:1], in1=ca3, op0=ALU.mult, op1=ALU.add),
                    reads=["ext", "wc", "cacc"], writes=["cacc"])
                S.op("dve", lambda e: e.tensor_tensor(
                    out=MIXT[:, cc, c0:c1], in0=PB[pb_][:, :64], in1=cacc[:, :64], op=ALU.mult),
                    reads=[PBn[pb_], "cacc"], writes=mxw)
                S.op("dve", lambda e: e.tensor_copy(out=cso[:, cc, :, :], in_=ext[:, :, 4:6]),
                     reads=["ext"], writes=["cso%d" % cc])

        cset = 0
        for cc in range(4):
            for ci, (c0, c1, kind) in enumerate(pch):
                conv_chunk(cc, ci, c0, c1, kind, cc % 2, (1, 2, 3) if cset % 2 == 0 else (4, 5, 6), cset % 2)
                cset += 1
            if cc + 2 < 4:
                load_cw(cc + 2)
            if cc == 0:
                load_wret(0)
                load_wret(1)
            elif cc == 1:
                load_wret(2)
                load_wret(3)
        if hf == 1:
            for cc in range(4):
                S.op("pe", lambda e, cc=cc: e.matmul(PB[7][:32, cc * 128:(cc + 1) * 128],
                                                     lhsT=cso[:, cc, :, :].rearrange("p s t -> p (s t)"), rhs=idf[:, :],
                                                     start=True, stop=True),
                     reads=["cso%d" % cc, "idf"], writes=["pb7"], nofence=True)
            S.op("dve", lambda e: e.tensor_copy(out=rowb[:32, :], in_=PB[7][:32, :]), reads=["pb7"], writes=["rowb"],
                 nofence=True)
            S.op("sp", lambda e: e.dma_start(out=ncs.rearrange("s t c -> (s t) c"), in_=rowb[:32, :]), reads=["rowb"],
                 dma=True, key="o_ncs", nofence=True)
            out_keys.append("o_ncs")
            for cc in range(4):
                S.op("pe", lambda e, cc=cc: e.matmul(PB[7][:2, cc * 128:(cc + 1) * 128], lhsT=cst[:, cc, :],
                                                     rhs=idf[:, :], start=True, stop=True),
                     reads=["cst%d" % cc, "idf"], writes=["pb7"], nofence=True)
            S.op("dve", lambda e: e.tensor_copy(out=rowb[:2, :], in_=PB[7][:2, :]), reads=["pb7"], writes=["rowb"],
                 nofence=True)
            S.op("sp", lambda e: e.dma_start(out=ncp, in_=rowb[:2, :]), reads=["rowb"], dma=True, key="o_ncp",
                 nofence=True)
            out_keys.append("o_ncp")

        fence()
        bsets = [(0, 1, 2, 3), (4, 5, 6, 7)]

        def alpha_gen(tile, p):
            (t, n, j, col) = tile
            bset = bsets[p]
            for blk in range(4):
                for kc in range(8):
                    S.op("pe", lambda e, blk=blk, kc=kc: e.matmul(
                        PB[bset[blk]][:n, :], lhsT=XT[:, kc, col:col + n], rhs=WRET[:, kc, blk * 512:(blk + 1) * 512],
                        start=(kc == 0), stop=(kc == 7)), reads=["XT%d" % j, "WRET"], writes=[PBn[bset[blk]]])
                    if kc % 4 == 3:
                        yield

        def alpha(tile, p):
            for _ in alpha_gen(tile, p):
                pass

        def step(g):
            if g is not None:
                next(g, None)

        def beta(tile, p):
            (t, n, j, col) = tile
            bA, bB, bC, bD = bsets[p]
            R = RS[p]
            kind = 1 if t == 17 else 0
            sfx = str(p)
            qrot, ktil, vbf, sg, QT, KT = R["qrot"], R["ktil"], R["vbf"], R["sg"], R["QT"], R["KT"]
            for h in range(8):
                S.op("act", lambda e, h=h: e.activation(out=vbf[:n, h * 64:(h + 1) * 64],
                                                        in_=PB[bC][:n, h * 64:(h + 1) * 64], func=AF.Copy,
                                                        scale=kdec[:n, kind, h:h + 1]),
                     reads=[PBn[bC], "tabs"], writes=["vbf%s.%d" % (sfx, h)])
            S.op("act", lambda e: e.activation(out=sg[:n, :], in_=PB[bD][:n, :], func=AF.Silu),
                 reads=[PBn[bD]], writes=["sg" + sfx])
            cosb = cs_t[:n, t, 0, :].unsqueeze(1).unsqueeze(1).to_broadcast([n, 8, 2, 32])
            sinb = cs_t[:n, t, 1, :].unsqueeze(1).to_broadcast([n, 8, 32])
            t14 = t1[:n, :].rearrange("p (h t d) -> p h t d", h=8, t=2)
            t24 = t2[:n, :].rearrange("p (h t d) -> p h t d", h=8, t=2)
            pq = pbf(bC).rearrange("p (h n) -> p h n", h=8)
            pk = pbf(bD).rearrange("p (h n) -> p h n", h=8)
            for which, bk in (("q", bA), ("k", bB)):
                p4 = PB[bk][:n, :].rearrange("p (h t d) -> p h t d", h=8, t=2)
                S.op("dve", lambda e, p4=p4: e.tensor_tensor(out=t14, in0=p4, in1=cosb, op=ALU.mult),
                     reads=[PBn[bk], "cs_t"], writes=["t1"])
                S.op("dve", lambda e, p4=p4: e.scalar_tensor_tensor(out=t24[:, :, 0, :], in0=p4[:, :, 1, :],
                                                                   scalar=-1.0, in1=sinb, op0=ALU.mult, op1=ALU.mult),
                     reads=[PBn[bk], "cs_t"], writes=["t2a"])
                S.op("dve", lambda e, p4=p4: e.tensor_tensor(out=t24[:, :, 1, :], in0=p4[:, :, 0, :], in1=sinb,
                                                            op=ALU.mult),
                     reads=[PBn[bk], "cs_t"], writes=["t2b"])
                if which == "q":
                    S.op("dve", lambda e: e.tensor_tensor(out=qrot[:n, :], in0=t1[:n, :], in1=t2[:n, :], op=ALU.add),
                         reads=["t1", "t2a", "t2b"], writes=["qrot" + sfx])
                    for h in range(8):
                        S.op("pe", lambda e, h=h: e.transpose(out=pq[:64, h, :n], in_=qrot[:n, h * 64:(h + 1) * 64],
                                                              identity=ident[:n, :n]),
                             reads=["qrot" + sfx, "ident"], writes=[PBn[bC]])
                    S.op("act", lambda e: e.copy(out=QT[:64, :, :n], in_=pq[:64, :, :n]), reads=[PBn[bC]],
                         writes=["QT" + sfx])
                else:
                    S.op("dve", lambda e: e.tensor_tensor(out=ktil[:n, :], in0=t1[:n, :], in1=t2[:n, :], op=ALU.add),
                         reads=["t1", "t2a", "t2b"], writes=["ktil" + sfx])
                    for h in range(8):
                        S.op("pe", lambda e, h=h: e.transpose(out=pk[:64, h, :n], in_=ktil[:n, h * 64:(h + 1) * 64],
                                                              identity=ident[:n, :n]),
                             reads=["ktil" + sfx, "ident"], writes=[PBn[bD]])
                    S.op("act", lambda e: e.copy(out=KT[:64, :, :n], in_=pk[:64, :, :n]), reads=[PBn[bD]],
                         writes=["KT" + sfx])
            for h in range(8):
                bk = bA if h < 4 else bB
                S.op("pe", lambda e, h=h, bk=bk: e.matmul(
                    PB[bk][:n, (h % 4) * 128:(h % 4) * 128 + n], lhsT=KT[:64, h, :n], rhs=QT[:64, h, :n],
                    start=True, stop=True), reads=["QT" + sfx, "KT" + sfx], writes=[PBn[bk]])

        def gamma1(tile, p, ag=None):
            (t, n, j, col) = tile
            bA, bB, bC, bD = bsets[p]
            R = RS[p]
            kind = 1 if t == 17 else 0
            sfx = str(p)
            ktil, vbf, QT, PT = R["ktil"], R["vbf"], R["QT"], R["PT"]
            for hb, bk in ((0, bA), (1, bB)):
                S.op("dve", lambda e, hb=hb, bk=bk: e.tensor_tensor(
                    out=PT[:n, hb * 4:hb * 4 + 4, :n],
                    in0=PB[bk][:n, :].rearrange("p (h i) -> p h i", h=4)[:, :, :n],
                    in1=maskt[:n, kind, :n].unsqueeze(1).to_broadcast([n, 4, n]), op=ALU.mult),
                    reads=[PBn[bk], "tabs"], writes=["PT%s.%d" % (sfx, hb)])
            if kind == 0:
                for h in range(8):
                    w_ = 128 if (WIDE_PV and h < 7) else 64
                    S.op("pe", lambda e, h=h, w_=w_: e.matmul(PB[bC][:n, h * 64:h * 64 + w_], lhsT=PT[:n, h, :n],
                                                              rhs=vbf[:n, h * 64:h * 64 + w_], start=True, stop=False,
                                                              skip_group_check=WIDE_PV),
                         reads=["PT%s.%d" % (sfx, h // 4), "vbf%s.%d" % (sfx, h)]
                         + (["vbf%s.%d" % (sfx, h + 1)] if w_ == 128 else []), writes=[PBn[bC]])
                    S.op("pe", lambda e, h=h: e.matmul(PB[bC][:n, h * 64:(h + 1) * 64], lhsT=QT[:64, h, :n],
                                                       rhs=Sbf[:64, h, :], start=False, stop=True,
                                                       skip_group_check=WIDE_PV),
                         reads=["QT" + sfx, "Sbf"], writes=[PBn[bC]])
                    step(ag)
                for h in range(8):
                    S.op("pe", lambda e, h=h: e.matmul(PB[bD][:64, h * 64:(h + 1) * 64],
                                                       lhsT=ktil[:n, h * 64:(h + 1) * 64],
                                                       rhs=vbf[:n, h * 64:(h + 1) * 64], start=True, stop=True),
                         reads=["ktil" + sfx, "vbf%s.%d" % (sfx, h)], writes=[PBn[bD]])
            else:
                for h in range(8):
                    S.op("pe", lambda e, h=h: e.matmul(PB[bC][:n, h * 64:(h + 1) * 64], lhsT=PT[:n, h, :n],
                                                       rhs=vbf[:n, h * 64:(h + 1) * 64], start=(h == 0), stop=False,
                                                       skip_group_check=True),
                         reads=["PT%s.%d" % (sfx, h // 4), "vbf%s.%d" % (sfx, h)], writes=[PBn[bC]])

                def s0_load(s):
                    s4 = s % 3
                    S.op("sp", lambda e: e.dma_start(out=S0b[:64, s4, :, :], in_=sret[s].rearrange("h d e -> d h e")),
                         writes=["S0b%d" % s4], dma=True, key="S0b%d" % s4)

                s0_load(0)
                s0_load(1)
                for s in range(16):
                    sl = s % 2
                    s4 = s % 3
                    if s + 2 < 16:
                        s0_load(s + 2)
                    S.op("act", lambda e, sl=sl, s4=s4: e.copy(out=S0bf[:64, sl, :, :], in_=S0b[:64, s4, :, :]),
                         reads=["S0b%d" % s4], writes=["S0bf%d" % sl])
                    S.op("dve", lambda e, s=s, sl=sl: e.tensor_tensor(
                        out=QTm[:64, sl, :, :64], in0=QT[:64, :, :64],
                        in1=smT[:64, s, :].unsqueeze(1).to_broadcast([64, 8, 64]), op=ALU.mult),
                        reads=["QT" + sfx, "smT"], writes=["QTm%d" % sl])
                    S.op("act", lambda e, s=s, sl=sl: e.activation(out=km[:64, sl, :], in_=ktil[:64, :], func=AF.Copy,
                                                                   scale=seqmask[:64, s:s + 1]),
                         reads=["ktil" + sfx, "tabs"], writes=["km%d" % sl])
                    for h in range(8):
                        S.op("pe", lambda e, h=h, sl=sl, s=s: e.matmul(
                            PB[bC][:n, h * 64:(h + 1) * 64], lhsT=QTm[:64, sl, h, :64], rhs=S0bf[:64, sl, h, :],
                            start=False, stop=(s == 15), skip_group_check=True),
                            reads=["QTm%d" % sl, "S0bf%d" % sl], writes=[PBn[bC]])
                    bk = bA if sl == 0 else bB
                    for h in range(8):
                        S.op("pe", lambda e, h=h, sl=sl, bk=bk: e.matmul(
                            PB[bk][:64, h * 64:(h + 1) * 64], lhsT=km[:64, sl, h * 64:(h + 1) * 64],
                            rhs=vbf[:64, h * 64:(h + 1) * 64], start=True, stop=True),
                            reads=["km%d" % sl, "vbf%s.%d" % (sfx, h)], writes=[PBn[bk]])
                    S.op("dve", lambda e, sl=sl, bk=bk, s4=s4: e.tensor_tensor(
                        out=S0t[:64, sl, :, :], in0=S0b[:64, s4, :, :],
                        in1=PB[bk][:64, :].rearrange("p (h e) -> p h e", h=8), op=ALU.add),
                        reads=["S0b%d" % s4, PBn[bk]], writes=["S0t%d" % sl])
                    S.op("pool" if s % 2 == 1 else "dve", lambda e, sl=sl: e.tensor_tensor(
                        out=So[:64, sl, :, :], in0=S0t[:64, sl, :, :],
                        in1=gl[:64, 2, :].unsqueeze(2).to_broadcast([64, 8, 64]), op=ALU.mult),
                        reads=["S0t%d" % sl, "tabs"], writes=["So%d" % sl])
                    S.op("sp", lambda e, s=s, sl=sl: e.dma_start(out=nrs[s].rearrange("h d e -> d h e"),
                                                                 in_=So[:64, sl, :, :]),
                         reads=["So%d" % sl], dma=True, key="o_nrs%d" % sl)
                out_keys.extend(["o_nrs0", "o_nrs1"])

        def gamma2a(tile, p):
            (t, n, j, col) = tile
            bA, bB, bC, bD = bsets[p]
            R = RS[p]
            kind = 1 if t == 17 else 0
            sfx = str(p)
            sg, osq, ro = R["sg"], R["osq"], R["ro"]
            gs = p
            o3 = PB[bC][:n, :].rearrange("p (h e) -> p h e", h=8)
            G = lambda a, b: gst[:n, gs, a:b]
            gA, gB, gC, gD, gE = ["g%d%s" % (gs, c_) for c_ in "abcde"]
            S.op("dve", lambda e: e.tensor_reduce(out=G(0, 8), in_=o3, axis=AX.X, op=ALU.add),
                 reads=[PBn[bC]], writes=[gA])
            S.op("act", lambda e: e.activation(out=osq[:n, :], in_=PB[bC][:n, :], func=AF.Square),
                 reads=[PBn[bC]], writes=["osq" + sfx])
            S.op("dve", lambda e: e.tensor_reduce(out=G(8, 16), in_=osq[:n, :].rearrange("p (h e) -> p h e", h=8),
                                                  axis=AX.X, op=ALU.add), reads=["osq" + sfx], writes=[gB])
            S.op("dve", lambda e: e.tensor_scalar(out=G(16, 24), in0=G(0, 8), scalar1=1.0 / 64, scalar2=None,
                                                  op0=ALU.mult), reads=[gA], writes=[gC])
            S.op("dve", lambda e: e.tensor_tensor(out=G(24, 32), in0=G(16, 24), in1=G(16, 24), op=ALU.mult),
                 reads=[gC], writes=[gD])
            S.op("dve", lambda e: e.scalar_tensor_tensor(out=G(32, 40), in0=G(8, 16), scalar=1.0 / 64, in1=G(24, 32),
                                                         op0=ALU.mult, op1=ALU.subtract), reads=[gB, gD], writes=[gE])
            S.op("dve", lambda e: e.tensor_tensor(out=G(32, 40), in0=G(32, 40), in1=epsq[:n, kind, :], op=ALU.add),
                 reads=[gE, "tabs"], writes=[gE])
            S.op("pool", lambda e: e.tensor_tensor(out=G(40, 48), in0=G(32, 40), in1=neghalf[:n, :], op=ALU.pow),
                 reads=[gE, "tabs"], writes=["gsr%d" % gs])
            S.op("dve", lambda e: e.scalar_tensor_tensor(out=G(24, 32), in0=G(16, 24), scalar=-1.0, in1=G(40, 48),
                                                         op0=ALU.mult, op1=ALU.mult),
                 reads=[gC, gD, "gsr%d" % gs], writes=["gsn%d" % gs])
            for h in range(8):
                S.op("act", lambda e, h=h: e.activation(out=on[:n, h * 64:(h + 1) * 64],
                                                        in_=PB[bC][:n, h * 64:(h + 1) * 64], func=AF.Identity,
                                                        scale=gst[:n, gs, 40 + h:41 + h], bias=gst[:n, gs, 24 + h:25 + h]),
                     reads=[PBn[bC], "gsr%d" % gs, "gsn%d" % gs], writes=["on.%d" % h])
            S.op("dve", lambda e: e.tensor_tensor(out=ro[:n, :], in0=on[:n, :], in1=sg[:n, :], op=ALU.mult),
                 reads=["on.%d" % h_ for h_ in range(8)] + ["sg" + sfx], writes=["ro" + sfx])
            if kind == 0:
                gk = 1 if t == 0 else 0
                S.op("dve", lambda e: e.tensor_tensor(out=Stmp[:64, :], in0=S32[:, :, :].rearrange("p h e -> p (h e)"),
                                                      in1=PB[bD][:64, :], op=ALU.add),
                     reads=["S32", PBn[bD]], writes=["Stmp"])
                S.op("dve", lambda e: e.tensor_tensor(
                    out=S32[:, :, :], in0=Stmp[:64, :].rearrange("p (h e) -> p h e", h=8),
                    in1=gl[:64, gk, :].unsqueeze(2).to_broadcast([64, 8, 64]), op=ALU.mult),
                    reads=["Stmp", "tabs"], writes=["S32"])
                S.op("act", lambda e: e.copy(out=Sbf[:64, :, :], in_=S32[:, :, :]), reads=["S32"], writes=["Sbf"])
                if t == 16:
                    S.op("sp", lambda e: e.dma_start(out=nrp.rearrange("h d e -> d h e"), in_=S32[:, :, :]),
                         reads=["S32"], dma=True, key="o_nrp")
                    out_keys.append("o_nrp")
            pr = pbf(bD).rearrange("p (k n) -> p k n", k=8)
            for pair in range(4):
                S.op("pe", lambda e, pair=pair: e.transpose(out=pr[:, pair, :n],
                                                            in_=ro[:n, pair * 128:(pair + 1) * 128],
                                                            identity=ident[:n, :n]),
                     reads=["ro" + sfx, "ident"], writes=[PBn[bD]])

        def gamma2b(tile, p):
            (t, n, j, col) = tile
            bD = bsets[p][3]
            pr = pbf(bD).rearrange("p (k n) -> p k n", k=8)
            for pair in range(4):
                S.op("act", lambda e, pair=pair: e.activation(out=MIXT[:, 4 + pair, col:col + n], in_=pr[:, pair, :n],
                                                              func=AF.Copy, scale=rngt[:, pair:pair + 1]),
                     reads=[PBn[bD], "rngt"], writes=["MXr%d.%d" % (j, pair)])

        S.pe_scale = PE_COLD
        alpha(tl[0], 0)
        beta(tl[0], 0)
        for i, tile in enumerate(tl):
            p = i % 2
            ag = alpha_gen(tl[i + 1], 1 - p) if i + 1 < len(tl) else None
            gamma1(tile, p, ag)
            if ag is not None:
                for _ in ag:
                    pass
            gamma2a(tile, p)
            if i + 1 < len(tl):
                beta(tl[i + 1], 1 - p)
            gamma2b(tile, p)
            if i == len(tl) - 1:
                S.op("act", lambda e: e.copy(out=mk[:, 4:8], in_=tabs[:, 0:4]), reads=["MXr%d.3" % tile[2], "tabs"],
                     writes=["RETDONE"])
            if i == 1:
                load_wout(0)
                load_wout(1)

        tl3 = [x for x in tl if x[0] != 0]

        def a3_proj(tile, k3):
            (t, n, j, col) = tile
            bks = (0, 1) if k3 % 2 == 0 else (2, 3)
            for cb in range(2):
                for kc in range(8):
                    S.op("pe", lambda e, cb=cb, kc=kc: e.matmul(
                        PB[bks[cb]][:n, :], lhsT=MIXT[:, kc, col:col + n], rhs=WOUT[:, kc, cb * 512:(cb + 1) * 512],
                        start=(kc == 0), stop=(kc == 7)),
                        reads=["MXc%d.%d" % (j, c_) for c_ in range(4)] + ["MXr%d.%d" % (j, c_) for c_ in range(4)]
                        + ["WOUT"] + (["RETDONE"] if A3_AFTER_RET else []), writes=[PBn[bks[cb]]])

        def a3_add(tile, k3):
            (t, n, j, col) = tile
            bks = (0, 1) if k3 % 2 == 0 else (2, 3)
            for cb in range(2):
                S.op("dve", lambda e, cb=cb: e.tensor_tensor(
                    out=X[:n, j, cb * 512:(cb + 1) * 512], in0=X[:n, j, cb * 512:(cb + 1) * 512],
                    in1=PB[bks[cb]][:n, :], op=ALU.add), reads=[XR(j)[cb], PBn[bks[cb]]], writes=[XR(j)[cb]],
                    nofence=NF2)

        a3_proj(tl3[0], 0)
        for k3, tile in enumerate(tl3):
            (t, n, j, col) = tile
            if k3 + 1 < len(tl3):
                a3_proj(tl3[k3 + 1], k3 + 1)
            a3_add(tile, k3)
            norm_stats(t, n, j, 3, k3 % 2, nf=NF2)
            if k3 >= 1:
                (t_, n_, j_, col_) = tl3[k3 - 1]
                norm_apply(t_, n_, j_, col_, g2, 3, [4, 5], (k3 - 1) % 2, nf=NF2)
        (t_, n_, j_, col_) = tl3[-1]
        norm_apply(t_, n_, j_, col_, g2, 3, [4, 5], (len(tl3) - 1) % 2, nf=NF2)

        S.pe_scale = 1.0
        fence()
        tlb = [x for x in tl if x[0] != 0]
        if hf == 0:
            gch = split(16, 1040, 2)
        else:
            gch = split(0, 1088, 3)

        def load_wd(g):
            sl = g % 3
            S.op("pool", lambda e: e.dma_start(out=WD[:, sl, :, :].rearrange("p f n -> p (f n)"), in_=wdb[g]),
                 writes=["WD%d" % sl], dma=True, key="WD%d" % sl)

        def load_gu(f):
            sl = f % 3
            S.op("pool", lambda e: e.dma_start(out=GU[:, sl, :, :].rearrange("p k n -> p (k n)"), in_=gub[f]),
                 writes=GUR[sl], dma=True, key="GU%d" % sl, nofence=True)

        S.op("sp", lambda e: e.dma_start(out=gf[:, :], in_=nf.partition_broadcast(128)), writes=["gf"], dma=True,
             key="gf")
        if hf == 0:
            ld("sp", rowb[:, :], sconv.rearrange("s t c -> (s t) c"), "rowb", boost=False)
            for cc_ in range(4):
                tr32(PB[7][:, cc_ * 32:(cc_ + 1) * 32], rowb[:32, cc_ * 128:(cc_ + 1) * 128], 32, ["rowb"])
            S.op("dve", lambda e: e.tensor_copy(out=csi[:, :, :, :].rearrange("p c s t -> p (c s t)"),
                                                in_=PB[7][:, 0:128]), reads=["pb7"], writes=["csi"], nofence=True)
        load_gu(0)
        load_gu(1)
        load_gu(2)
        load_wd(0)
        load_wd(1)
        def gu_chunk(f, sl, c0, c1, bg, bu, ss_):
            N = c1 - c0
            rd = ["XT%d" % j for j in overlapping(tlb, c0, c1)] + GUR[sl]
            for kc in range(8):
                S.op("pe", lambda e, kc=kc: e.matmul(
                    PB[bg][:, :N], lhsT=GU[:, sl, kc, 0:128], rhs=XT[:, kc, c0:c1], start=(kc == 0),
                    stop=(kc == 7)), reads=rd, writes=[PBn[bg]])
            for kc in range(8):
                S.op("pe", lambda e, kc=kc: e.matmul(
                    PB[bu][:, :N], lhsT=GU[:, sl, kc, 128:256], rhs=XT[:, kc, c0:c1], start=(kc == 0),
                    stop=(kc == 7)), reads=rd, writes=[PBn[bu]])
            S.op("act", lambda e: e.activation(out=sgate[:, ss_, :N], in_=PB[bg][:, :N], func=AF.Silu),
                 reads=[PBn[bg]], writes=["sgate%d" % ss_])
            S.op("dve", lambda e: e.tensor_tensor(
                out=HT[:, f, c0:c1], in0=PB[bu][:, :N], in1=sgate[:, ss_, :N], op=ALU.mult),
                reads=[PBn[bu], "sgate%d" % ss_], writes=["HT%d.%d" % (j, f) for j in overlapping(tlb, c0, c1)])

        kb = 0
        for f in range(NFF):
            for (c0, c1) in gch:
                bg, bu = ((0, 1), (2, 3), (4, 5))[kb % 3]
                gu_chunk(f, f % 3, c0, c1, bg, bu, kb % 2)
                kb += 1
            if f + 3 < NFF:
                load_gu(f + 3)
        if hf == 0:
            xstg = sgate[:, :, :].rearrange("p s n -> p (s n)")
            sres = ["sgate0", "sgate1"]
            for i_, (t, n, j, col) in enumerate(tiles_of(1)):
                src = xsamp if t == 17 else xp[128 * (t - 1):128 * t, :]
                S.op("sp", lambda e, src=src, n=n: e.dma_start(out=xstg[:n, :], in_=src), writes=sres, dma=True,
                     key="xstg")
                norm_stats(t, n, j, 0, i_ % 2, src=xstg[:n, :], src_res=sres)
                norm_apply(t, n, j, col, g1, 0, [0, 1], i_ % 2, on_dve=True, src=xstg[:n, :], src_res=sres)
        load_wd(2)
        kd = 0
        for cb in range(2):
            if cb == 1:
                load_wd(3)
            for (t, n, j, col) in tlb:
                bk = 6 + (kd % 2)
                kd += 1
                for f in range(NFF):
                    g = cb * 2 + f // 11
                    sl = g % 3
                    S.op("pe", lambda e, f=f, sl=sl, bk=bk, n=n, col=col: e.matmul(
                        PB[bk][:n, :], lhsT=HT[:, f, col:col + n], rhs=WD[:, sl, f % 11, :], start=(f == 0),
                        stop=(f == NFF - 1)), reads=["HT%d.%d" % (j, f), "WD%d" % sl], writes=[PBn[bk]])
                S.op("dve", lambda e, cb=cb, bk=bk, n=n, j=j: e.tensor_tensor(
                    out=X[:n, j, cb * 512:(cb + 1) * 512], in0=X[:n, j, cb * 512:(cb + 1) * 512],
                    in1=PB[bk][:n, :], op=ALU.add), reads=[XR(j)[cb], PBn[bk]], writes=[XR(j)[cb]], nofence=NF2)
                if cb == 1:
                    ys_ = kd % 2
                    S.op("act", lambda e, n=n, j=j, t=t, ys_=ys_: e.activation(out=YO[:n, ys_, :], in_=X[:n, j, :],
                                                                               func=AF.Square,
                                                                               accum_out=stat[:n, t, 6:7]),
                         reads=XR(j), writes=["YO%d" % ys_, "st%d.6" % t])
                    S.op("dve", lambda e, n=n, t=t: e.tensor_scalar(out=stat[:n, t, 7:8], in0=stat[:n, t, 6:7],
                                                                    scalar1=1.0 / D, scalar2=EPS, op0=ALU.mult,
                                                                    op1=ALU.add),
                         reads=["st%d.6" % t], writes=["st%d.7" % t], nofence=NF2)
                    S.op("pool", lambda e, n=n, t=t: e.tensor_tensor(out=stat[:n, t, 8:9], in0=stat[:n, t, 7:8],
                                                                     in1=neghalf[:n, 0:1], op=ALU.pow),
                         reads=["st%d.7" % t, "tabs"], writes=["st%d.8" % t], nofence=NF2)
                    S.op("dve", lambda e, n=n, j=j, t=t, ys_=ys_: e.scalar_tensor_tensor(
                        out=YO[:n, ys_, :], in0=X[:n, j, :], scalar=stat[:n, t, 8:9], in1=gf[:n, :],
                        op0=ALU.mult, op1=ALU.mult), reads=XR(j) + ["st%d.8" % t, "gf"], writes=["YO%d" % ys_])
                    dst = ys if t == 17 else yp[128 * (t - 1):128 * t, :]
                    S.op(YQ, lambda e, dst=dst, n=n, ys_=ys_: e.dma_start(out=dst, in_=YO[:n, ys_, :]),
                         reads=["YO%d" % ys_], dma=True, key="o_y%d" % ys_)
        out_keys.extend(["o_y0", "o_y1"])

    S.emit(final_wait_keys=sorted(set(out_keys)), schedule=SCHEDULE)
    print("[kernel] ops=%d sim_time_us=%.1f" % (len(S.ops), getattr(S, "sim_time", -1)))
    return nc


_CACHE = {}


def kernel(x_prompt, x_sample, state_conv, state_ret, meta_tokens, norm1_g, w_in, w_conv, ret_norm_g, w_out,
           norm2_g, w_gate, w_up, w_down, final_norm_g):
    f = lambda a: np.ascontiguousarray(np.asarray(a, dtype=np.float32))
    x_prompt, x_sample, state_conv, state_ret = f(x_prompt), f(x_sample), f(state_conv), f(state_ret)
    cs, tabs, identf, smT = _tables()
    wi, wo, wg_, wu_, wd_ = f(w_in)[0], f(w_out)[0], f(w_gate)[0], f(w_up)[0], f(w_down)[0]
    c = np.ascontiguousarray
    cwb = c(wi[:, :1536].reshape(8, 128, 3, 4, 128).transpose(3, 1, 0, 2, 4)).reshape(4, 128, 8 * 384)
    wretb = c(wi[:, 1536:].reshape(8, 128, 2048).transpose(1, 0, 2)).reshape(128, 8 * 2048)
    woutb = c(wo.reshape(8, 128, D).transpose(1, 0, 2)).reshape(128, 8 * D)
    gu = np.stack([wg_.reshape(8, 128, NFF, 128), wu_.reshape(8, 128, NFF, 128)], axis=3)
    gub = c(gu.transpose(2, 1, 0, 3, 4)).reshape(NFF, 128, 8 * 256)
    wdb = c(wd_.reshape(2, 11, 128, 2, 512).transpose(3, 0, 2, 1, 4)).reshape(4, 128, 11 * 512)
    shared = {
        "meta": f(meta_tokens), "w_conv": f(w_conv)[0], "rng": f(ret_norm_g)[0],
        "n1": f(norm1_g)[0], "n2": f(norm2_g)[0], "nf": f(final_norm_g),
        "cwb": cwb, "wretb": wretb, "woutb": woutb, "gub": gub, "wdb": wdb,
        "cs": cs, "tabs": tabs, "identf": identf, "smT": smT,
    }
    in_maps = []
    for c in range(8):
        m = dict(shared)
        m["xp"] = x_prompt[c]
        m["xsamp"] = x_sample[16 * c:16 * c + 16].reshape(64, D)
        m["sconv"] = state_conv[0, 16 * c:16 * c + 16]
        m["sret"] = state_ret[0, 16 * c:16 * c + 16]
        in_maps.append(m)
    if "nc" not in _CACHE:
        _CACHE["nc"] = build_nc()
    res = run_bass_kernel_spmd(_CACHE["nc"], in_maps, core_ids=list(range(8)))
    r = res.results
    y_prompt = np.stack([r[c]["yp"] for c in range(8)], 0)
    y_sample = np.concatenate([r[c]["ys"].reshape(16, 4, D) for c in range(8)], 0)
    ncp = np.stack([r[c]["ncp"] for c in range(8)], 0)[None]
    nrp = np.stack([r[c]["nrp"] for c in range(8)], 0)[None]
    ncs = np.concatenate([r[c]["ncs"] for c in range(8)], 0)[None]
    nrs = np.concatenate([r[c]["nrs"] for c in range(8)], 0)[None]
    return (y_prompt.astype(np.float32), y_sample.astype(np.float32), ncp.astype(np.float32),
            nrp.astype(np.float32), ncs.astype(np.float32), nrs.astype(np.float32))
```

```python
import contextlib
import os
import numpy as np
import concourse.bass as bass
import concourse.mybir as mybir
from concourse.bass_utils import run_bass_kernel_spmd

F32 = mybir.dt.float32
BF16 = mybir.dt.bfloat16
AF = mybir.ActivationFunctionType
ALU = mybir.AluOpType
AX = mybir.AxisListType

D = 1024
SEQ = 2048
NMETA = 16
DFF = 2816
NFF = DFF // 128
INC = 3584
EPS = 1e-6
GN_EPS = 1e-5
PAST = 16384
ENGS = ("pe", "act", "dve", "pool", "sp")
SCHEDULE = True
A3_AFTER_RET = False
PE_COLD = float(os.environ.get("MK_PE_COLD", "1.0"))
HOP = float(os.environ.get("MK_HOP", "0.15"))
PE_BASE = float(os.environ.get("MK_PE_BASE", "0.03"))
ACT_BASE = float(os.environ.get("MK_ACT_BASE", "0.25"))
ACT_RATE = float(os.environ.get("MK_ACT_RATE", "1200"))
POOL_BASE = float(os.environ.get("MK_POOL_BASE", "1.5"))
DMA_LAT = float(os.environ.get("MK_DMA_LAT", "2.0"))
LATE_PE = os.environ.get("MK_LATE_PE", "1") == "1"
NF2 = os.environ.get("MK_NF2", "1") == "1"
JITTER = float(os.environ.get("MK_JITTER", "0.0"))
_RNG = np.random.RandomState(int(os.environ.get("MK_SEED", "0")))
STRICT = os.environ.get("MK_STRICT", "1") == "1"


class _Op:
    __slots__ = ("eng", "fn", "dma", "key", "deps", "odeps", "signal", "count", "idx", "cost", "start")


class _Fake:
    def __init__(self):
        self.rec = None

    def __getattr__(self, name):
        def f(*a, **k):
            self.rec = (name, a, k)
            return self
        return f


def _prod(t):
    r = 1
    for x in t:
        r *= int(x)
    return r


class Sched:
    def __init__(self, nc):
        self.nc = nc
        self.ops = []
        self.last_w = {}
        self.readers = {}
        self.excl = set()
        self.dma_keys = {}
        self.last_dma = {}
        self.auto = None
        self.pe_scale = 1.0

    def _cost(self, eng, fn, dma):
        fk = _Fake()
        try:
            fn(fk)
            name, a, k = fk.rec
            out = k.get("out", a[0] if a else None)
            shp = tuple(out.shape)
            F = _prod(shp[1:]) if len(shp) > 1 else 1
            if dma:
                c = _prod(shp) * 4 / 300e3
                if k.get("allow_slow_non_contiguous"):
                    c += _prod(shp) * 0.004
                return c
            if eng == "pe":
                if name == "matmul":
                    M = _prod(tuple(k["lhsT"].shape)[1:])
                    N = _prod(tuple(k["rhs"].shape)[1:])
                    return PE_BASE + max(N / 2400.0, M / 1200.0)
                return 0.1
            if eng == "dve":
                return 0.13 + F / 960.0
            if eng == "act":
                return ACT_BASE + F / ACT_RATE
            if eng == "pool":
                return POOL_BASE + F / 600.0
        except Exception:
            pass
        return 0.5

    def op(self, eng, fn, reads=(), writes=(), dma=False, key=None, nofence=False):
        o = _Op()
        reads = list(reads)
        writes = list(writes)
        late = False
        if self.auto is not None and not nofence and self.auto not in writes:
            if eng == "pe" and LATE_PE:
                late = True
            else:
                reads.append(self.auto)
        o.eng, o.fn, o.dma, o.key = eng, fn, dma, key
        o.idx = len(self.ops)
        o.signal = False
        o.count = None
        o.cost = self._cost(eng, fn, dma) * (self.pe_scale if eng == "pe" else 1.0)
        if JITTER > 0:
            o.cost *= 1.0 + JITTER * (2.0 * _RNG.random() - 1.0)
        deps = {}
        for r in reads:
            w = self.last_w.get(r)
            if w is not None:
                deps[w] = True
            if r in self.excl:
                for x in self.readers.get(r, ()):
                    if self.ops[x].eng != eng:
                        deps.setdefault(x, False)
        for w_ in writes:
            w = self.last_w.get(w_)
            if w is not None:
                deps.setdefault(w, False)
            for x in self.readers.get(w_, ()):
                deps.setdefault(x, False)
        fin = []
        for d, raw in deps.items():
            Dd = self.ops[d]
            if Dd.dma:
                fin.append(d)
            elif Dd.eng == eng and not dma:
                if eng in ("act", "dve", "pool") and (raw or STRICT):
                    fin.append(d)
            else:
                fin.append(d)
        o.deps = fin
        o.odeps = set(deps.keys())
        if dma and key in self.last_dma:
            o.odeps.add(self.last_dma[key])
        for r in reads:
            self.readers.setdefault(r, []).append(o.idx)
        if late:
            self.readers.setdefault(self.auto, []).append(o.idx)
        for w_ in writes:
            self.last_w[w_] = o.idx
            self.readers[w_] = []
        if dma:
            self.dma_keys[key] = self.dma_keys.get(key, 0) + 16
            o.count = self.dma_keys[key]
            self.last_dma[key] = o.idx
        self.ops.append(o)
        return o

    def schedule(self, K=48):
        import heapq
        ops = self.ops
        n = len(ops)
        succ = [[] for _ in range(n)]
        npred = [0] * n
        for o in ops:
            npred[o.idx] = len(o.odeps)
            for d in o.odeps:
                succ[d].append(o.idx)
        ready_t = [0.0] * n
        fin_t = [0.0] * n
        rank = [0.0] * n
        for i in range(n - 1, -1, -1):
            m = 0.0
            for j in succ[i]:
                if rank[j] > m:
                    m = rank[j]
            rank[i] = ops[i].cost + m + (DMA_LAT if ops[i].dma else 0.0) + HOP
        cand = {e: [] for e in ENGS}
        for o in ops:
            if npred[o.idx] == 0:
                cand[o.eng].append(o.idx)
        free = {e: 0.0 for e in ENGS}
        dma_free = 0.0
        order = {e: [] for e in ENGS}
        done = 0
        while done < n:
            best = None
            for e in ENGS:
                h = cand[e]
                if not h:
                    continue
                T = free[e]
                rd = [i for i in h if ready_t[i] <= T]
                if rd:
                    pick = min(rd, key=lambda i: (-rank[i], i))
                else:
                    pick = min(h, key=lambda i: (ready_t[i], -rank[i], i))
                st = max(T, ready_t[pick])
                if best is None or st < best[0]:
                    best = (st, e, pick)
            st, e, i = best
            cand[e].remove(i)
            o = ops[i]
            o.start = st
            if o.dma:
                t0 = max(st, dma_free)
                dma_free = t0 + o.cost
                free[e] = st + (o.cost if e == "pool" else 0.1)
                fin_t[i] = dma_free + DMA_LAT
            else:
                free[e] = st + o.cost
                fin_t[i] = free[e]
            order[e].append(o)
            done += 1
            for j in succ[i]:
                f_ = fin_t[i] + (HOP if ops[j].eng != e else 0.0)
                if f_ > ready_t[j]:
                    ready_t[j] = f_
                npred[j] -= 1
                if npred[j] == 0:
                    cand[ops[j].eng].append(j)
        self.sim_time = max(fin_t) if n else 0.0
        return order

    def emit(self, final_wait_keys=(), schedule=True):
        nc = self.nc
        ops = self.ops
        for o in ops:
            for d in o.deps:
                if not ops[d].dma:
                    ops[d].signal = True
        if schedule:
            per_eng = self.schedule()
        else:
            per_eng = {e: [o for o in ops if o.eng == e] for e in ENGS}
        for e in ENGS:
            c = 0
            for o in per_eng[e]:
                if not o.dma and o.signal:
                    c += 1
                    o.count = c
        with contextlib.ExitStack() as st:
            esem = {e: st.enter_context(nc.semaphore("s_" + e)) for e in ENGS}
            dsem = {k: st.enter_context(nc.semaphore("d_%d" % i)) for i, k in enumerate(self.dma_keys)}
            block = st.enter_context(nc.Block())

            def run(engname, eng):
                waited = {}
                for o in per_eng[engname]:
                    need = {}
                    for d in o.deps:
                        Dd = ops[d]
                        s = ("d", Dd.key) if Dd.dma else ("e", Dd.eng)
                        if Dd.count > need.get(s, 0):
                            need[s] = Dd.count
                    for s, v in need.items():
                        if waited.get(s, 0) >= v:
                            continue
                        waited[s] = v
                        eng.wait_ge(dsem[s[1]] if s[0] == "d" else esem[s[1]], v)
                    ins = o.fn(eng)
                    if o.dma:
                        ins.then_inc(dsem[o.key], 16)
                    elif o.signal:
                        ins.then_inc(esem[engname], 1)
                if engname == "sp":
                    for k in final_wait_keys:
                        eng.wait_ge(dsem[k], self.dma_keys[k])

            block.sync(lambda e: run("sp", e))
            block.tensor(lambda e: run("pe", e))
            block.scalar(lambda e: run("act", e))
            block.vector(lambda e: run("dve", e))
            block.gpsimd(lambda e: run("pool", e))


def _tables():
    half = 32
    inv = 10000.0 ** (-np.arange(half, dtype=np.float64) / half)
    lg = np.log1p(-(2.0 ** (-5.0 - np.arange(8)))).astype(np.float32).astype(np.float64)
    cs = np.zeros((128, 18, 2, 32), np.float32)
    for t in range(18):
        i = np.arange(128)
        if t == 0:
            pos = i.astype(np.float64)
        elif t <= 16:
            pos = (16 + 128 * (t - 1) + i).astype(np.float64)
        else:
            pos = (PAST + (i % 4)).astype(np.float64)
        ang = pos[:, None] * inv[None, :]
        cs[:, t, 0] = np.cos(ang)
        cs[:, t, 1] = np.sin(ang)
    tabs = np.zeros((128, 336), np.float32)
    i = np.arange(128)
    for kind in range(2):
        il = i if kind == 0 else (i % 4)
        tabs[:, kind * 8:kind * 8 + 8] = np.exp(-(il[:, None] + 1.0) * lg[None, :]) * 0.125
        tabs[:, 16 + kind * 8:16 + kind * 8 + 8] = GN_EPS * np.exp(-2.0 * (il[:, None] + 1.0) * lg[None, :])
    for k, L in enumerate((128.0, 16.0, 4.0)):
        tabs[:, 32 + k * 8:32 + k * 8 + 8] = np.exp(L * lg)[None, :]
    tabs[:, 56:72] = (i[:, None] // 4 == np.arange(16)[None, :]).astype(np.float32)
    tabs[:, 72:80] = -0.5
    m0 = (i[None, :] >= i[:, None]).astype(np.float32)
    m1 = m0 * (i[None, :] // 4 == i[:, None] // 4)
    tabs[:, 80:208] = m0
    tabs[:, 208:336] = m1
    smT = np.zeros((64, 16, 64), np.float32)
    for s in range(16):
        smT[:, s, 4 * s:4 * s + 4] = 1.0
    return cs.reshape(128, 18 * 64), tabs, np.eye(128, dtype=np.float32), smT.reshape(64, 1024)


def build_nc():
    nc = bass.Bass("TRN2", target_bir_lowering=False)
    S = Sched(nc)

    def din(name, shape):
        return nc.dram_tensor(name, list(shape), F32, kind="ExternalInput").ap()

    def dout(name, shape):
        return nc.dram_tensor(name, list(shape), F32, kind="ExternalOutput").ap()

    xp = din("xp", [SEQ, D]); meta = din("meta", [NMETA, D]); xsamp = din("xsamp", [64, D])
    sconv = din("sconv", [16, 2, 512]); sret = din("sret", [16, 8, 64, 64])
    w_conv = din("w_conv", [3, 512]); rng_d = din("rng", [512])
    n1 = din("n1", [D]); n2 = din("n2", [D]); nf = din("nf", [D])
    cwb = din("cwb", [4, 128, 8 * 384])
    wretb = din("wretb", [128, 8 * 2048])
    woutb = din("woutb", [128, 8 * D])
    gub = din("gub", [NFF, 128, 8 * 256])
    wdb = din("wdb", [4, 128, 11 * 512])
    cs_d = din("cs", [128, 18 * 64]); tabs_d = din("tabs", [128, 336]); ident_d = din("identf", [128, 128])
    smT_d = din("smT", [64, 1024])
    yp = dout("yp", [SEQ, D]); ys = dout("ys", [64, D]); ncp = dout("ncp", [2, 512]); nrp = dout("nrp", [8, 64, 64])
    ncs = dout("ncs", [16, 2, 512]); nrs = dout("nrs", [16, 8, 64, 64])

    def A(name, shape, dt):
        return nc.alloc_sbuf_tensor("sb_" + name, shape, dt)

    X = A("X", [128, 9, D], F32)
    XT = A("XT", [128, 8, 1088], BF16)
    AL = A("AL", [128, 25088], BF16)
    WRET = AL[:, 0:16384].rearrange("p (k n) -> p k n", k=8)
    MIXT = AL[:, 16384:16384 + 8704].rearrange("p (k n) -> p k n", k=8)
    HT = AL[:, 0:NFF * 1088].rearrange("p (f n) -> p f n", f=NFF)
    AL2 = A("AL2", [128, 43584], BF16)
    _off = [0]

    def carve(nelem_bf16):
        o = _off[0]
        _off[0] += nelem_bf16
        assert _off[0] <= 43584, _off[0]
        return AL2[:, o:o + nelem_bf16]

    def carve_f32(nelem):
        return carve(2 * nelem).bitcast(F32)

    CW = carve(2 * 8 * 384).rearrange("p (s k n) -> p s k n", s=2, k=8)
    WOUT = carve(8 * D).rearrange("p (k n) -> p k n", k=8)
    r1 = _off[0]
    hsb = carve_f32(2 * 512).rearrange("p (s n) -> p s n", s=2)
    ubuf = carve_f32(520)
    cacc = carve_f32(512)
    ext = carve_f32(96).rearrange("p (s t) -> p s t", s=16)
    conv_end = _off[0]
    _off[0] = r1

    def ret_set():
        d = {}
        d["qrot"] = carve(512); d["ktil"] = carve(512); d["vbf"] = carve(512)
        d["sg"] = carve_f32(512)
        d["QT"] = carve(1024).rearrange("p (h n) -> p h n", h=8)
        d["KT"] = carve(1024).rearrange("p (h n) -> p h n", h=8)
        d["PT"] = carve(1024).rearrange("p (h n) -> p h n", h=8)
        d["osq"] = carve_f32(512)
        d["ro"] = carve(512)
        return d

    RS = [None, None]
    RS[1] = ret_set()
    _off[0] = max(_off[0], conv_end)
    RS[0] = ret_set()
    t1 = carve_f32(512); t2 = carve_f32(512)
    on = carve_f32(512)
    Stmp = carve_f32(512)
    Sbf = carve(512).rearrange("p (h e) -> p h e", h=8)
    QTm = carve(2 * 512).rearrange("p (s h n) -> p s h n", s=2, h=8)
    km = carve(2 * 512).rearrange("p (s n) -> p s n", s=2)
    S0b = carve_f32(3 * 512).rearrange("p (s h e) -> p s h e", s=3, h=8)
    S0bf = carve(2 * 512).rearrange("p (s h e) -> p s h e", s=2, h=8)
    S0t = carve_f32(2 * 512).rearrange("p (s h e) -> p s h e", s=2, h=8)
    So = carve_f32(2 * 512).rearrange("p (s h e) -> p s h e", s=2, h=8)
    endA = _off[0]
    _off[0] = 0
    GU = carve(3 * 8 * 256).rearrange("p (s k n) -> p s k n", s=3, k=8)
    WD = carve(3 * 11 * 512).rearrange("p (s f n) -> p s f n", s=3, f=11)
    sgate = carve_f32(2 * 512).rearrange("p (s n) -> p s n", s=2)
    YO = carve_f32(2 * D).rearrange("p (s n) -> p s n", s=2)
    gf = carve_f32(D)
    endB = _off[0]
    xs2 = A("xs", [128, 2, D], BF16)
    stat = A("stat", [128, 18, 12], F32)
    gst = A("gst", [128, 2, 48], F32)
    S32 = A("S32", [64, 8, 64], F32)
    ucar = A("ucar", [128, 4, 2], F32)
    cst = A("cst", [128, 4, 2], F32)
    cso = A("cso", [128, 4, 16, 2], F32)
    csi = A("csi", [128, 4, 16, 2], F32)
    idf = A("idf", [128, 128], F32)
    rowb = A("rowb", [32, 512], F32)
    rowc = A("rowc", [8, 3, 128], F32)
    cs_t = A("cs_t", [128, 18, 2, 32], F32)
    tabs = A("tabs", [128, 336], F32)
    ident = A("ident", [128, 128], BF16)
    smT = A("smT", [64, 16, 64], BF16)
    wc = A("wc", [128, 4, 3], F32)
    g1 = A("g1", [128, 8], F32); g2 = A("g2", [128, 8], F32)
    rngt = A("rngt", [128, 4], F32)
    fsc = A("fsc", [128, 8], F32)
    mk = A("mk", [128, 8], F32)
    kdec = tabs[:, 0:16].rearrange("p (k h) -> p k h", k=2)
    epsq = tabs[:, 16:32].rearrange("p (k h) -> p k h", k=2)
    gl = tabs[:, 32:56].rearrange("p (k h) -> p k h", k=3)
    seqmask = tabs[:, 56:72]
    neghalf = tabs[:, 72:80]
    maskt = tabs[:, 80:336].rearrange("p (k n) -> p k n", k=2)
    PB = [nc.alloc_psum_tensor("pb%d" % i, [128, 512], F32) for i in range(8)]
    PBn = ["pb%d" % i for i in range(8)]
    S.excl.update(PBn)

    def pbf(i):
        return PB[i][:, :].bitcast(BF16)

    def ld(q, dst, src, res, **kw):
        S.op(q, lambda e: e.dma_start(out=dst, in_=src, **kw), writes=[res], dma=True, key=res, nofence=True)

    ld("sp", tabs[:, :], tabs_d, "tabs")
    ld("pool", ident[:, :], ident_d, "ident")
    ld("sp", idf[:, :], ident_d, "idf")
    ld("sp", rowc[:, 0, :], n1.rearrange("(k p) -> k p", p=128), "rowc0")
    ld("sp", rowc[:, 1, :], n2.rearrange("(k p) -> k p", p=128), "rowc1")
    ld("sp", rowc[0:4, 2, :], rng_d.rearrange("(k p) -> k p", p=128), "rowc2")
    ld("sp", rowb[0:3, :], w_conv, "rowb")

    def tr32(out_ps, lhsT, kk, reads):
        S.op("pe", lambda e: e.matmul(out_ps, lhsT=lhsT, rhs=idf[:kk, :kk], start=True, stop=True),
             reads=list(reads) + ["idf"], writes=["pb7"], nofence=True)

    tr32(PB[7][:, 0:8], rowc[:8, 0, :], 8, ["rowc0"])
    tr32(PB[7][:, 8:16], rowc[:8, 1, :], 8, ["rowc1"])
    tr32(PB[7][:, 16:20], rowc[:4, 2, :], 4, ["rowc2"])
    for cc_ in range(4):
        tr32(PB[7][:, 20 + 3 * cc_:23 + 3 * cc_], rowb[:3, cc_ * 128:(cc_ + 1) * 128], 3, ["rowb"])
    S.op("dve", lambda e: e.tensor_copy(out=g1[:, :], in_=PB[7][:, 0:8]), reads=["pb7"], writes=["g1"], nofence=True)
    S.op("dve", lambda e: e.tensor_copy(out=g2[:, :], in_=PB[7][:, 8:16]), reads=["pb7"], writes=["g2"], nofence=True)
    S.op("dve", lambda e: e.tensor_copy(out=rngt[:, :], in_=PB[7][:, 16:20]), reads=["pb7"], writes=["rngt"],
         nofence=True)
    S.op("dve", lambda e: e.tensor_copy(out=wc[:, :, :].rearrange("p c j -> p (c j)"), in_=PB[7][:, 20:32]),
         reads=["pb7"], writes=["wc"], nofence=True)

    def late_tables():
        ld("sp", cs_t[:, :, :, :], cs_d.rearrange("p (t k d) -> p t k d", t=18, k=2), "cs_t")
        ld("pool", smT[:, :, :], smT_d.rearrange("p (s n) -> p s n", s=16), "smT")

    S.op("dve", lambda e: e.memset(S32[:, :, :], 0.0), writes=["S32"], nofence=True)
    S.op("dve", lambda e: e.memset(fsc[:, :], 0.0), writes=["PH"], nofence=True)
    S.auto = "PH"

    def fence():
        S.op("dve", lambda e: e.memset(fsc[:, :], 0.0), writes=["PH"])

    def tiles_of(hf):
        if hf == 0:
            return [(0, 16, 0, 0)] + [(t, 128, t, 16 + 128 * (t - 1)) for t in range(1, 9)]
        return [(t, 128, t - 9, 128 * (t - 9)) for t in range(9, 17)] + [(17, 64, 8, 1024)]

    def split(c0, c1, k):
        b = [c0 + (c1 - c0) * i // k for i in range(k + 1)]
        return [(b[i], b[i + 1]) for i in range(k)]

    def overlapping(tl, c0, c1):
        return [j for (t, n, j, col) in tl if col < c1 and col + n > c0]

    pt_rot = [0]

    def XR(j):
        return ["X%da" % j, "X%db" % j]

    def norm_stats(t, n, j, sc, xsl, nf=False, src=None, src_res=None):
        src = X[:n, j, :] if src is None else src
        src_res = XR(j) if src_res is None else src_res
        S.op("act", lambda e: e.activation(out=xs2[:n, xsl, :], in_=src, func=AF.Square,
                                           accum_out=stat[:n, t, sc:sc + 1]),
             reads=src_res, writes=["xs%d" % xsl, "st%d.%d" % (t, sc)], nofence=nf)
        S.op("dve", lambda e: e.tensor_scalar(out=stat[:n, t, sc + 1:sc + 2], in0=stat[:n, t, sc:sc + 1],
                                              scalar1=1.0 / D, scalar2=EPS, op0=ALU.mult, op1=ALU.add),
             reads=["st%d.%d" % (t, sc)], writes=["st%d.%d" % (t, sc + 1)], nofence=nf)
        S.op("pool", lambda e: e.tensor_tensor(out=stat[:n, t, sc + 2:sc + 3], in0=stat[:n, t, sc + 1:sc + 2],
                                               in1=neghalf[:n, 0:1], op=ALU.pow),
             reads=["st%d.%d" % (t, sc + 1), "tabs"], writes=["st%d.%d" % (t, sc + 2)], nofence=nf)

    def norm_apply(t, n, j, col, gain, sc, banks, xsl, nf=False, on_dve=False, src=None, src_res=None):
        src = X[:n, j, :] if src is None else src
        src_res = XR(j) if src_res is None else src_res
        bank = banks[pt_rot[0] % len(banks)]
        xs = xs2[:, xsl, :]
        xsn = "xs%d" % xsl
        pt_rot[0] += 1
        if on_dve:
            S.op("dve", lambda e: e.tensor_scalar(out=xs[:n, :], in0=src, scalar1=stat[:n, t, sc + 2:sc + 3],
                                                  scalar2=None, op0=ALU.mult),
                 reads=src_res + ["st%d.%d" % (t, sc + 2)], writes=[xsn], nofence=nf)
        else:
            S.op("act", lambda e: e.activation(out=xs[:n, :], in_=src, func=AF.Copy,
                                               scale=stat[:n, t, sc + 2:sc + 3]),
                 reads=src_res + ["st%d.%d" % (t, sc + 2)], writes=[xsn], nofence=nf)
        pv = pbf(bank).rearrange("p (k n) -> p k n", k=8)
        for kc in range(8):
            S.op("pe", lambda e, kc=kc: e.transpose(out=pv[:, kc, :n], in_=xs[:n, kc * 128:(kc + 1) * 128],
                                                    identity=ident[:n, :n]),
                 reads=[xsn, "ident"], writes=[PBn[bank]], nofence=nf)
        S.op("dve", lambda e: e.tensor_tensor(out=XT[:, :, col:col + n], in0=pv[:, :, :n],
                                              in1=gain[:, :].unsqueeze(2).to_broadcast([128, 8, n]), op=ALU.mult),
             reads=[PBn[bank], "g1", "g2"], writes=["XT%d" % j], nofence=nf)

    out_keys = []
    CWR = {0: ["GU0", "GU1a"], 1: ["GU1b", "GU2"]}
    GUR = {0: ["GU0"], 1: ["GU1a", "GU1b"], 2: ["GU2"]}

    for hf in range(2):
        tl = tiles_of(hf)
        if hf == 1:
            fence()
        for (t, n, j, col) in tl:
            if t == 0:
                src = meta
            elif t <= 16:
                src = xp[128 * (t - 1):128 * t, :]
            else:
                src = xsamp
            S.op("sp", lambda e, src=src, n=n, j=j: e.dma_start(out=X[:n, j, :], in_=src),
                 reads=(["X2a"] if (hf == 0 and j > 2 and os.environ.get("MK_XPRIO", "1") == "1") else []),
                 writes=XR(j), dma=True, key="X%d" % j, nofence=True)
        def load_cw(cc):
            sl = cc % 2
            S.op("pool", lambda e: e.dma_start(out=CW[:, sl, :, :].rearrange("p k n -> p (k n)"), in_=cwb[cc]),
                 reads=(["X6a"] if (hf == 0 and cc == 1) else []), writes=CWR[sl], dma=True, key="CW%d" % sl,
                 nofence=True)

        load_cw(0)
        load_cw(1)
        if hf == 0:
            late_tables()
        S.op("act", lambda e: e.copy(out=Sbf[:64, :, :], in_=S32[:, :, :]), reads=["S32"], writes=["Sbf"])
        if hf == 0:
            for i_, (t, n, j, col) in enumerate(tl):
                norm_stats(t, n, j, 0, i_ % 2, nf=True)
                if i_ >= 1:
                    (t_, n_, j_, col_) = tl[i_ - 1]
                    norm_apply(t_, n_, j_, col_, g1, 0, [0, 1], (i_ - 1) % 2, nf=True, on_dve=True)
            (t_, n_, j_, col_) = tl[-1]
            norm_apply(t_, n_, j_, col_, g1, 0, [0, 1], (len(tl) - 1) % 2, nf=True, on_dve=True)
        def load_wret(blk):
            S.op("pool", lambda e: e.dma_start(
                out=WRET[:, 2 * blk:2 * blk + 2, :].rearrange("p k n -> p (k n)"),
                in_=wretb[:, blk * 4096:(blk + 1) * 4096]), reads=(["X8a"] if hf == 0 else []), writes=["WRET"],
                dma=True, key="WRET")

        def load_wout(cb):
            S.op("pool", lambda e: e.dma_start(
                out=WOUT[:, 4 * cb:4 * cb + 4, :].rearrange("p k n -> p (k n)"),
                in_=woutb[:, cb * 4096:(cb + 1) * 4096]), writes=["WOUT"], dma=True, key="WOUT")

        if hf == 0:
            if os.environ.get("MK_UNEVEN", "0") == "1":
                pch = [(0, 144, "p"), (144, 592, "p"), (592, 1040, "p")]
            else:
                pch = [(a, b, "p") for (a, b) in split(0, 1040, 3)]
        else:
            pch = [(a, b, "p") for (a, b) in split(0, 1024, 3)] + [(1024, 1088, "s")]
        n_p = len([x for x in pch if x[2] == "p"])

        def conv_chunk(cc, ci, c0, c1, kind, sl, bks, hs):
            N = c1 - c0
            rd = ["XT%d" % j for j in overlapping(tl, c0, c1)] + CWR[sl]
            for br in range(3):
                for kc in range(8):
                    S.op("pe", lambda e, br=br, kc=kc: e.matmul(
                        PB[bks[br]][:, :N], lhsT=CW[:, sl, kc, br * 128:(br + 1) * 128], rhs=XT[:, kc, c0:c1],
                        start=(kc == 0), stop=(kc == 7)), reads=rd, writes=[PBn[bks[br]]])
            pb_, pc_, ph_ = bks
            S.op("act", lambda e: e.copy(out=hsb[:, hs, :N], in_=PB[ph_][:, :N]),
                 reads=[PBn[ph_]], writes=["hsb%d" % hs])
            mxw = ["MXc%d.%d" % (j, cc) for j in overlapping(tl, c0, c1)]
            if kind == "p":
                if ci == 0:
                    if hf == 0:
                        S.op("dve", lambda e: e.memset(ubuf[:, 0:2], 0.0), writes=["ubuf"])
                    else:
                        S.op("dve", lambda e: e.tensor_copy(out=ubuf[:, 0:2], in_=ucar[:, cc, :]),
                             reads=["ucar%d" % cc], writes=["ubuf"])
                S.op("dve", lambda e: e.tensor_tensor(
                    out=ubuf[:, 2:2 + N], in0=PB[pc_][:, :N], in1=hsb[:, hs, :N], op=ALU.mult),
                    reads=[PBn[pc_], "hsb%d" % hs], writes=["ubuf"])
                S.op("act", lambda e: e.activation(out=cacc[:, :N], in_=ubuf[:, 2:2 + N], func=AF.Copy,
                                                   scale=wc[:, cc, 2:3]),
                     reads=["ubuf", "wc"], writes=["cacc"])
                S.op("dve", lambda e: e.scalar_tensor_tensor(
                    out=cacc[:, :N], in0=ubuf[:, 1:1 + N], scalar=wc[:, cc, 1:2], in1=cacc[:, :N],
                    op0=ALU.mult, op1=ALU.add), reads=["ubuf", "wc", "cacc"], writes=["cacc"])
                S.op("dve", lambda e: e.scalar_tensor_tensor(
                    out=cacc[:, :N], in0=ubuf[:, 0:N], scalar=wc[:, cc, 0:1], in1=cacc[:, :N],
                    op0=ALU.mult, op1=ALU.add), reads=["ubuf", "wc", "cacc"], writes=["cacc"])
                S.op("dve", lambda e: e.tensor_tensor(
                    out=MIXT[:, cc, c0:c1], in0=PB[pb_][:, :N], in1=cacc[:, :N], op=ALU.mult),
                    reads=[PBn[pb_], "cacc"], writes=mxw)
                last_p = (ci == n_p - 1)
                if last_p and hf == 0:
                    S.op("dve", lambda e: e.tensor_copy(out=ucar[:, cc, :], in_=ubuf[:, N:N + 2]),
                         reads=["ubuf"], writes=["ucar%d" % cc])
                elif last_p:
                    S.op("dve", lambda e: e.tensor_copy(out=cst[:, cc, :], in_=ubuf[:, N:N + 2]),
                         reads=["ubuf"], writes=["cst%d" % cc])
                else:
                    S.op("dve", lambda e: e.tensor_copy(out=ubuf[:, 0:2], in_=ubuf[:, N:N + 2]),
                         reads=["ubuf"], writes=["ubuf"])
            else:
                S.op("dve", lambda e: e.tensor_copy(out=ext[:, :, 0:2], in_=csi[:, cc, :, :]), reads=["csi"],
                     writes=["ext"])
                pc3 = PB[pc_][:, :64].rearrange("p (s t) -> p s t", t=4)
                hs3 = hsb[:, hs, :64].rearrange("p (s t) -> p s t", t=4)
                ca3 = cacc[:, :64].rearrange("p (s t) -> p s t", t=4)
                S.op("dve", lambda e: e.tensor_tensor(out=ext[:, :, 2:6], in0=pc3, in1=hs3, op=ALU.mult),
                     reads=[PBn[pc_], "hsb%d" % hs], writes=["ext"])
                S.op("dve", lambda e: e.tensor_scalar(
                    out=ca3, in0=ext[:, :, 2:6], scalar1=wc[:, cc, 2:3], scalar2=None, op0=ALU.mult),
                    reads=["ext", "wc"], writes=["cacc"])
                S.op("dve", lambda e: e.scalar_tensor_tensor(
                    out=ca3, in0=ext[:, :, 1:5], scalar=wc[:, cc, 1:2], in1=ca3, op0=ALU.mult, op1=ALU.add),
                    reads=["ext", "wc", "cacc"], writes=["cacc"])
                S.op("dve", lambda e: e.scalar_tensor_tensor(
                    out=ca3, in0=ext[:, :, 0:4], scalar=wc[:, cc, 0:1], in1=ca3, op0=ALU.mult, op1=ALU.add),
                    reads=["ext", "wc", "cacc"], writes=["cacc"])
                S.op("dve", lambda e: e.tensor_tensor(
                    out=MIXT[:, cc, c0:c1], in0=PB[pb_][:, :64], in1=cacc[:, :64], op=ALU.mult),
                    reads=[PBn[pb_], "cacc"], writes=mxw)
                S.op("dve", lambda e: e.tensor_copy(out=cso[:, cc, :, :], in_=ext[:, :, 4:6]),
                     reads=["ext"], writes=["cso%d" % cc])

        cset = 0
        for cc in range(4):
            for ci, (c0, c1, kind) in enumerate(pch):
                conv_chunk(cc, ci, c0, c1, kind, cc % 2, (1, 2, 3) if cset % 2 == 0 else (4, 5, 6), cset % 2)
                cset += 1
            if cc + 2 < 4:
                load_cw(cc + 2)
            if cc == 0:
                load_wret(0)
                load_wret(1)
            elif cc == 1:
                load_wret(2)
                load_wret(3)
        if hf == 1:
            for cc in range(4):
                S.op("pe", lambda e, cc=cc: e.matmul(PB[7][:32, cc * 128:(cc + 1) * 128],
                                                     lhsT=cso[:, cc, :, :].rearrange("p s t -> p (s t)"), rhs=idf[:, :],
                                                     start=True, stop=True),
                     reads=["cso%d" % cc, "idf"], writes=["pb7"], nofence=True)
            S.op("dve", lambda e: e.tensor_copy(out=rowb[:32, :], in_=PB[7][:32, :]), reads=["pb7"], writes=["rowb"],
                 nofence=True)
            S.op("sp", lambda e: e.dma_start(out=ncs.rearrange("s t c -> (s t) c"), in_=rowb[:32, :]), reads=["rowb"],
                 dma=True, key="o_ncs", nofence=True)
            out_keys.append("o_ncs")
            for cc in range(4):
                S.op("pe", lambda e, cc=cc: e.matmul(PB[7][:2, cc * 128:(cc + 1) * 128], lhsT=cst[:, cc, :],
                                                     rhs=idf[:, :], start=True, stop=True),
                     reads=["cst%d" % cc, "idf"], writes=["pb7"], nofence=True)
            S.op("dve", lambda e: e.tensor_copy(out=rowb[:2, :], in_=PB[7][:2, :]), reads=["pb7"], writes=["rowb"],
                 nofence=True)
            S.op("sp", lambda e: e.dma_start(out=ncp, in_=rowb[:2, :]), reads=["rowb"], dma=True, key="o_ncp",
                 nofence=True)
            out_keys.append("o_ncp")

        fence()
        bsets = [(0, 1, 2, 3), (4, 5, 6, 7)]

        def alpha_gen(tile, p):
            (t, n, j, col) = tile
            bset = bsets[p]
            for blk in range(4):
                for kc in range(8):
                    S.op("pe", lambda e, blk=blk, kc=kc: e.matmul(
                        PB[bset[blk]][:n, :], lhsT=XT[:, kc, col:col + n], rhs=WRET[:, kc, blk * 512:(blk + 1) * 512],
                        start=(kc == 0), stop=(kc == 7)), reads=["XT%d" % j, "WRET"], writes=[PBn[bset[blk]]])
                    if kc % 4 == 3:
                        yield

        def alpha(tile, p):
            for _ in alpha_gen(tile, p):
                pass

        def step(g):
            if g is not None:
                next(g, None)

        def beta(tile, p):
            (t, n, j, col) = tile
            bA, bB, bC, bD = bsets[p]
            R = RS[p]
            kind = 1 if t == 17 else 0
            sfx = str(p)
            qrot, ktil, vbf, sg, QT, KT = R["qrot"], R["ktil"], R["vbf"], R["sg"], R["QT"], R["KT"]
            for h in range(8):
                S.op("act", lambda e, h=h: e.activation(out=vbf[:n, h * 64:(h + 1) * 64],
                                                        in_=PB[bC][:n, h * 64:(h + 1) * 64], func=AF.Copy,
                                                        scale=kdec[:n, kind, h:h + 1]),
                     reads=[PBn[bC], "tabs"], writes=["vbf%s.%d" % (sfx, h)])
            S.op("act", lambda e: e.activation(out=sg[:n, :], in_=PB[bD][:n, :], func=AF.Silu),
                 reads=[PBn[bD]], writes=["sg" + sfx])
            cosb = cs_t[:n, t, 0, :].unsqueeze(1).unsqueeze(1).to_broadcast([n, 8, 2, 32])
            sinb = cs_t[:n, t, 1, :].unsqueeze(1).to_broadcast([n, 8, 32])
            t14 = t1[:n, :].rearrange("p (h t d) -> p h t d", h=8, t=2)
            t24 = t2[:n, :].rearrange("p (h t d) -> p h t d", h=8, t=2)
            pq = pbf(bC).rearrange("p (h n) -> p h n", h=8)
            pk = pbf(bD).rearrange("p (h n) -> p h n", h=8)
            for which, bk in (("q", bA), ("k", bB)):
                p4 = PB[bk][:n, :].rearrange("p (h t d) -> p h t d", h=8, t=2)
                S.op("dve", lambda e, p4=p4: e.tensor_tensor(out=t14, in0=p4, in1=cosb, op=ALU.mult),
                     reads=[PBn[bk], "cs_t"], writes=["t1"])
                S.op("dve", lambda e, p4=p4: e.scalar_tensor_tensor(out=t24[:, :, 0, :], in0=p4[:, :, 1, :],
                                                                   scalar=-1.0, in1=sinb, op0=ALU.mult, op1=ALU.mult),
                     reads=[PBn[bk], "cs_t"], writes=["t2a"])
                S.op("dve", lambda e, p4=p4: e.tensor_tensor(out=t24[:, :, 1, :], in0=p4[:, :, 0, :], in1=sinb,
                                                            op=ALU.mult),
                     reads=[PBn[bk], "cs_t"], writes=["t2b"])
                if which == "q":
                    S.op("dve", lambda e: e.tensor_tensor(out=qrot[:n, :], in0=t1[:n, :], in1=t2[:n, :], op=ALU.add),
                         reads=["t1", "t2a", "t2b"], writes=["qrot" + sfx])
                    for h in range(8):
                        S.op("pe", lambda e, h=h: e.transpose(out=pq[:64, h, :n], in_=qrot[:n, h * 64:(h + 1) * 64],
                                                              identity=ident[:n, :n]),
                             reads=["qrot" + sfx, "ident"], writes=[PBn[bC]])
                    S.op("act", lambda e: e.copy(out=QT[:64, :, :n], in_=pq[:64, :, :n]), reads=[PBn[bC]],
                         writes=["QT" + sfx])
                else:
                    S.op("dve", lambda e: e.tensor_tensor(out=ktil[:n, :], in0=t1[:n, :], in1=t2[:n, :], op=ALU.add),
                         reads=["t1", "t2a", "t2b"], writes=["ktil" + sfx])
                    for h in range(8):
                        S.op("pe", lambda e, h=h: e.transpose(out=pk[:64, h, :n], in_=ktil[:n, h * 64:(h + 1) * 64],
                                                              identity=ident[:n, :n]),
                             reads=["ktil" + sfx, "ident"], writes=[PBn[bD]])
                    S.op("act", lambda e: e.copy(out=KT[:64, :, :n], in_=pk[:64, :, :n]), reads=[PBn[bD]],
                         writes=["KT" + sfx])
            for h in range(8):
                bk = bA if h < 4 else bB
                S.op("pe", lambda e, h=h, bk=bk: e.matmul(
                    PB[bk][:n, (h % 4) * 128:(h % 4) * 128 + n], lhsT=KT[:64, h, :n], rhs=QT[:64, h, :n],
                    start=True, stop=True), reads=["QT" + sfx, "KT" + sfx], writes=[PBn[bk]])

        def gamma1(tile, p, ag=None):
            (t, n, j, col) = tile
            bA, bB, bC, bD = bsets[p]
            R = RS[p]
            kind = 1 if t == 17 else 0
            sfx = str(p)
            ktil, vbf, QT, PT = R["ktil"], R["vbf"], R["QT"], R["PT"]
            for hb, bk in ((0, bA), (1, bB)):
                S.op("dve", lambda e, hb=hb, bk=bk: e.tensor_tensor(
                    out=PT[:n, hb * 4:hb * 4 + 4, :n],
                    in0=PB[bk][:n, :].rearrange("p (h i) -> p h i", h=4)[:, :, :n],
                    in1=maskt[:n, kind, :n].unsqueeze(1).to_broadcast([n, 4, n]), op=ALU.mult),
                    reads=[PBn[bk], "tabs"], writes=["PT%s.%d" % (sfx, hb)])
            if kind == 0:
                for h in range(8):
                    S.op("pe", lambda e, h=h: e.matmul(PB[bC][:n, h * 64:(h + 1) * 64], lhsT=PT[:n, h, :n],
                                                       rhs=vbf[:n, h * 64:(h + 1) * 64], start=True, stop=False),
                         reads=["PT%s.%d" % (sfx, h // 4), "vbf%s.%d" % (sfx, h)], writes=[PBn[bC]])
                    S.op("pe", lambda e, h=h: e.matmul(PB[bC][:n, h * 64:(h + 1) * 64], lhsT=QT[:64, h, :n],
                                                       rhs=Sbf[:64, h, :], start=False, stop=True),
                         reads=["QT" + sfx, "Sbf"], writes=[PBn[bC]])
                    step(ag)
                for h in range(8):
                    S.op("pe", lambda e, h=h: e.matmul(PB[bD][:64, h * 64:(h + 1) * 64],
                                                       lhsT=ktil[:n, h * 64:(h + 1) * 64],
                                                       rhs=vbf[:n, h * 64:(h + 1) * 64], start=True, stop=True),
                         reads=["ktil" + sfx, "vbf%s.%d" % (sfx, h)], writes=[PBn[bD]])
            else:
                for h in range(8):
                    S.op("pe", lambda e, h=h: e.matmul(PB[bC][:n, h * 64:(h + 1) * 64], lhsT=PT[:n, h, :n],
                                                       rhs=vbf[:n, h * 64:(h + 1) * 64], start=(h == 0), stop=False,
                                                       skip_group_check=True),
                         reads=["PT%s.%d" % (sfx, h // 4), "vbf%s.%d" % (sfx, h)], writes=[PBn[bC]])

                def s0_load(s):
                    s4 = s % 3
                    S.op("sp", lambda e: e.dma_start(out=S0b[:64, s4, :, :], in_=sret[s].rearrange("h d e -> d h e")),
                         writes=["S0b%d" % s4], dma=True, key="S0b%d" % s4)

                s0_load(0)
                s0_load(1)
                for s in range(16):
                    sl = s % 2
                    s4 = s % 3
                    if s + 2 < 16:
                        s0_load(s + 2)
                    S.op("act", lambda e, sl=sl, s4=s4: e.copy(out=S0bf[:64, sl, :, :], in_=S0b[:64, s4, :, :]),
                         reads=["S0b%d" % s4], writes=["S0bf%d" % sl])
                    S.op("dve", lambda e, s=s, sl=sl: e.tensor_tensor(
                        out=QTm[:64, sl, :, :64], in0=QT[:64, :, :64],
                        in1=smT[:64, s, :].unsqueeze(1).to_broadcast([64, 8, 64]), op=ALU.mult),
                        reads=["QT" + sfx, "smT"], writes=["QTm%d" % sl])
                    S.op("act", lambda e, s=s, sl=sl: e.activation(out=km[:64, sl, :], in_=ktil[:64, :], func=AF.Copy,
                                                                   scale=seqmask[:64, s:s + 1]),
                         reads=["ktil" + sfx, "tabs"], writes=["km%d" % sl])
                    for h in range(8):
                        S.op("pe", lambda e, h=h, sl=sl, s=s: e.matmul(
                            PB[bC][:n, h * 64:(h + 1) * 64], lhsT=QTm[:64, sl, h, :64], rhs=S0bf[:64, sl, h, :],
                            start=False, stop=(s == 15), skip_group_check=True),
                            reads=["QTm%d" % sl, "S0bf%d" % sl], writes=[PBn[bC]])
                    bk = bA if sl == 0 else bB
                    for h in range(8):
                        S.op("pe", lambda e, h=h, sl=sl, bk=bk: e.matmul(
                            PB[bk][:64, h * 64:(h + 1) * 64], lhsT=km[:64, sl, h * 64:(h + 1) * 64],
                            rhs=vbf[:64, h * 64:(h + 1) * 64], start=True, stop=True),
                            reads=["km%d" % sl, "vbf%s.%d" % (sfx, h)], writes=[PBn[bk]])
                    S.op("dve", lambda e, sl=sl, bk=bk, s4=s4: e.tensor_tensor(
                        out=S0t[:64, sl, :, :], in0=S0b[:64, s4, :, :],
                        in1=PB[bk][:64, :].rearrange("p (h e) -> p h e", h=8), op=ALU.add),
                        reads=["S0b%d" % s4, PBn[bk]], writes=["S0t%d" % sl])
                    S.op("pool", lambda e, sl=sl: e.tensor_tensor(
                        out=So[:64, sl, :, :], in0=S0t[:64, sl, :, :],
                        in1=gl[:64, 2, :].unsqueeze(2).to_broadcast([64, 8, 64]), op=ALU.mult),
                        reads=["S0t%d" % sl, "tabs"], writes=["So%d" % sl])
                    S.op("sp", lambda e, s=s, sl=sl: e.dma_start(out=nrs[s].rearrange("h d e -> d h e"),
                                                                 in_=So[:64, sl, :, :]),
                         reads=["So%d" % sl], dma=True, key="o_nrs%d" % sl)
                out_keys.extend(["o_nrs0", "o_nrs1"])

        def gamma2a(tile, p):
            (t, n, j, col) = tile
            bA, bB, bC, bD = bsets[p]
            R = RS[p]
            kind = 1 if t == 17 else 0
            sfx = str(p)
            sg, osq, ro = R["sg"], R["osq"], R["ro"]
            gs = p
            o3 = PB[bC][:n, :].rearrange("p (h e) -> p h e", h=8)
            G = lambda a, b: gst[:n, gs, a:b]
            gA, gB, gC, gD, gE = ["g%d%s" % (gs, c_) for c_ in "abcde"]
            S.op("dve", lambda e: e.tensor_reduce(out=G(0, 8), in_=o3, axis=AX.X, op=ALU.add),
                 reads=[PBn[bC]], writes=[gA])
            S.op("act", lambda e: e.activation(out=osq[:n, :], in_=PB[bC][:n, :], func=AF.Square),
                 reads=[PBn[bC]], writes=["osq" + sfx])
            S.op("dve", lambda e: e.tensor_reduce(out=G(8, 16), in_=osq[:n, :].rearrange("p (h e) -> p h e", h=8),
                                                  axis=AX.X, op=ALU.add), reads=["osq" + sfx], writes=[gB])
            S.op("dve", lambda e: e.tensor_scalar(out=G(16, 24), in0=G(0, 8), scalar1=1.0 / 64, scalar2=None,
                                                  op0=ALU.mult), reads=[gA], writes=[gC])
            S.op("dve", lambda e: e.tensor_tensor(out=G(24, 32), in0=G(16, 24), in1=G(16, 24), op=ALU.mult),
                 reads=[gC], writes=[gD])
            S.op("dve", lambda e: e.scalar_tensor_tensor(out=G(32, 40), in0=G(8, 16), scalar=1.0 / 64, in1=G(24, 32),
                                                         op0=ALU.mult, op1=ALU.subtract), reads=[gB, gD], writes=[gE])
            S.op("dve", lambda e: e.tensor_tensor(out=G(32, 40), in0=G(32, 40), in1=epsq[:n, kind, :], op=ALU.add),
                 reads=[gE, "tabs"], writes=[gE])
            S.op("pool", lambda e: e.tensor_tensor(out=G(40, 48), in0=G(32, 40), in1=neghalf[:n, :], op=ALU.pow),
                 reads=[gE, "tabs"], writes=["gsr%d" % gs])
            S.op("dve", lambda e: e.scalar_tensor_tensor(out=G(24, 32), in0=G(16, 24), scalar=-1.0, in1=G(40, 48),
                                                         op0=ALU.mult, op1=ALU.mult),
                 reads=[gC, gD, "gsr%d" % gs], writes=["gsn%d" % gs])
            for h in range(8):
                S.op("act", lambda e, h=h: e.activation(out=on[:n, h * 64:(h + 1) * 64],
                                                        in_=PB[bC][:n, h * 64:(h + 1) * 64], func=AF.Identity,
                                                        scale=gst[:n, gs, 40 + h:41 + h], bias=gst[:n, gs, 24 + h:25 + h]),
                     reads=[PBn[bC], "gsr%d" % gs, "gsn%d" % gs], writes=["on.%d" % h])
            S.op("dve", lambda e: e.tensor_tensor(out=ro[:n, :], in0=on[:n, :], in1=sg[:n, :], op=ALU.mult),
                 reads=["on.%d" % h_ for h_ in range(8)] + ["sg" + sfx], writes=["ro" + sfx])
            if kind == 0:
                gk = 1 if t == 0 else 0
                S.op("dve", lambda e: e.tensor_tensor(out=Stmp[:64, :], in0=S32[:, :, :].rearrange("p h e -> p (h e)"),
                                                      in1=PB[bD][:64, :], op=ALU.add),
                     reads=["S32", PBn[bD]], writes=["Stmp"])
                S.op("dve", lambda e: e.tensor_tensor(
                    out=S32[:, :, :], in0=Stmp[:64, :].rearrange("p (h e) -> p h e", h=8),
                    in1=gl[:64, gk, :].unsqueeze(2).to_broadcast([64, 8, 64]), op=ALU.mult),
                    reads=["Stmp", "tabs"], writes=["S32"])
                S.op("act", lambda e: e.copy(out=Sbf[:64, :, :], in_=S32[:, :, :]), reads=["S32"], writes=["Sbf"])
                if t == 16:
                    S.op("sp", lambda e: e.dma_start(out=nrp.rearrange("h d e -> d h e"), in_=S32[:, :, :]),
                         reads=["S32"], dma=True, key="o_nrp")
                    out_keys.append("o_nrp")
            pr = pbf(bD).rearrange("p (k n) -> p k n", k=8)
            for pair in range(4):
                S.op("pe", lambda e, pair=pair: e.transpose(out=pr[:, pair, :n],
                                                            in_=ro[:n, pair * 128:(pair + 1) * 128],
                                                            identity=ident[:n, :n]),
                     reads=["ro" + sfx, "ident"], writes=[PBn[bD]])

        def gamma2b(tile, p):
            (t, n, j, col) = tile
            bD = bsets[p][3]
            pr = pbf(bD).rearrange("p (k n) -> p k n", k=8)
            for pair in range(4):
                S.op("act", lambda e, pair=pair: e.activation(out=MIXT[:, 4 + pair, col:col + n], in_=pr[:, pair, :n],
                                                              func=AF.Copy, scale=rngt[:, pair:pair + 1]),
                     reads=[PBn[bD], "rngt"], writes=["MXr%d.%d" % (j, pair)])

        S.pe_scale = PE_COLD
        alpha(tl[0], 0)
        beta(tl[0], 0)
        for i, tile in enumerate(tl):
            p = i % 2
            ag = alpha_gen(tl[i + 1], 1 - p) if i + 1 < len(tl) else None
            gamma1(tile, p, ag)
            if ag is not None:
                for _ in ag:
                    pass
            gamma2a(tile, p)
            if i + 1 < len(tl):
                beta(tl[i + 1], 1 - p)
            gamma2b(tile, p)
            if i == len(tl) - 1:
                S.op("act", lambda e: e.copy(out=mk[:, 4:8], in_=tabs[:, 0:4]), reads=["MXr%d.3" % tile[2], "tabs"],
                     writes=["RETDONE"])
            if i == 1:
                load_wout(0)
                load_wout(1)

        tl3 = [x for x in tl if x[0] != 0]

        def a3_proj(tile, k3):
            (t, n, j, col) = tile
            bks = (0, 1) if k3 % 2 == 0 else (2, 3)
            for cb in range(2):
                for kc in range(8):
                    S.op("pe", lambda e, cb=cb, kc=kc: e.matmul(
                        PB[bks[cb]][:n, :], lhsT=MIXT[:, kc, col:col + n], rhs=WOUT[:, kc, cb * 512:(cb + 1) * 512],
                        start=(kc == 0), stop=(kc == 7)),
                        reads=["MXc%d.%d" % (j, c_) for c_ in range(4)] + ["MXr%d.%d" % (j, c_) for c_ in range(4)]
                        + ["WOUT"] + (["RETDONE"] if A3_AFTER_RET else []), writes=[PBn[bks[cb]]])

        def a3_add(tile, k3):
            (t, n, j, col) = tile
            bks = (0, 1) if k3 % 2 == 0 else (2, 3)
            for cb in range(2):
                S.op("dve", lambda e, cb=cb: e.tensor_tensor(
                    out=X[:n, j, cb * 512:(cb + 1) * 512], in0=X[:n, j, cb * 512:(cb + 1) * 512],
                    in1=PB[bks[cb]][:n, :], op=ALU.add), reads=[XR(j)[cb], PBn[bks[cb]]], writes=[XR(j)[cb]],
                    nofence=NF2)

        a3_proj(tl3[0], 0)
        for k3, tile in enumerate(tl3):
            (t, n, j, col) = tile
            if k3 + 1 < len(tl3):
                a3_proj(tl3[k3 + 1], k3 + 1)
            a3_add(tile, k3)
            norm_stats(t, n, j, 3, k3 % 2, nf=NF2)
            if k3 >= 1:
                (t_, n_, j_, col_) = tl3[k3 - 1]
                norm_apply(t_, n_, j_, col_, g2, 3, [4, 5], (k3 - 1) % 2, nf=NF2)
        (t_, n_, j_, col_) = tl3[-1]
        norm_apply(t_, n_, j_, col_, g2, 3, [4, 5], (len(tl3) - 1) % 2, nf=NF2)

        S.pe_scale = 1.0
        fence()
        tlb = [x for x in tl if x[0] != 0]
        if hf == 0:
            gch = split(16, 1040, 2)
        else:
            gch = split(0, 1088, 3)

        def load_wd(g):
            sl = g % 3
            S.op("pool", lambda e: e.dma_start(out=WD[:, sl, :, :].rearrange("p f n -> p (f n)"), in_=wdb[g]),
                 writes=["WD%d" % sl], dma=True, key="WD%d" % sl)

        def load_gu(f):
            sl = f % 3
            S.op("pool", lambda e: e.dma_start(out=GU[:, sl, :, :].rearrange("p k n -> p (k n)"), in_=gub[f]),
                 writes=GUR[sl], dma=True, key="GU%d" % sl, nofence=True)

        S.op("sp", lambda e: e.dma_start(out=gf[:, :], in_=nf.partition_broadcast(128)), writes=["gf"], dma=True,
             key="gf")
        if hf == 0:
            ld("sp", rowb[:, :], sconv.rearrange("s t c -> (s t) c"), "rowb")
            for cc_ in range(4):
                tr32(PB[7][:, cc_ * 32:(cc_ + 1) * 32], rowb[:32, cc_ * 128:(cc_ + 1) * 128], 32, ["rowb"])
            S.op("dve", lambda e: e.tensor_copy(out=csi[:, :, :, :].rearrange("p c s t -> p (c s t)"),
                                                in_=PB[7][:, 0:128]), reads=["pb7"], writes=["csi"], nofence=True)
        load_gu(0)
        load_gu(1)
        load_gu(2)
        load_wd(0)
        load_wd(1)
        def gu_chunk(f, sl, c0, c1, bg, bu, ss_):
            N = c1 - c0
            rd = ["XT%d" % j for j in overlapping(tlb, c0, c1)] + GUR[sl]
            for kc in range(8):
                S.op("pe", lambda e, kc=kc: e.matmul(
                    PB[bg][:, :N], lhsT=GU[:, sl, kc, 0:128], rhs=XT[:, kc, c0:c1], start=(kc == 0),
                    stop=(kc == 7)), reads=rd, writes=[PBn[bg]])
            for kc in range(8):
                S.op("pe", lambda e, kc=kc: e.matmul(
                    PB[bu][:, :N], lhsT=GU[:, sl, kc, 128:256], rhs=XT[:, kc, c0:c1], start=(kc == 0),
                    stop=(kc == 7)), reads=rd, writes=[PBn[bu]])
            S.op("act", lambda e: e.activation(out=sgate[:, ss_, :N], in_=PB[bg][:, :N], func=AF.Silu),
                 reads=[PBn[bg]], writes=["sgate%d" % ss_])
            S.op("dve", lambda e: e.tensor_tensor(
                out=HT[:, f, c0:c1], in0=PB[bu][:, :N], in1=sgate[:, ss_, :N], op=ALU.mult),
                reads=[PBn[bu], "sgate%d" % ss_], writes=["HT%d.%d" % (j, f) for j in overlapping(tlb, c0, c1)])

        kb = 0
        for f in range(NFF):
            for (c0, c1) in gch:
                bg, bu = ((0, 1), (2, 3), (4, 5))[kb % 3]
                gu_chunk(f, f % 3, c0, c1, bg, bu, kb % 2)
                kb += 1
            if f + 3 < NFF:
                load_gu(f + 3)
        if hf == 0:
            xstg = sgate[:, :, :].rearrange("p s n -> p (s n)")
            sres = ["sgate0", "sgate1"]
            for i_, (t, n, j, col) in enumerate(tiles_of(1)):
                src = xsamp if t == 17 else xp[128 * (t - 1):128 * t, :]
                S.op("sp", lambda e, src=src, n=n: e.dma_start(out=xstg[:n, :], in_=src), writes=sres, dma=True,
                     key="xstg")
                norm_stats(t, n, j, 0, i_ % 2, src=xstg[:n, :], src_res=sres)
                norm_apply(t, n, j, col, g1, 0, [0, 1], i_ % 2, on_dve=True, src=xstg[:n, :], src_res=sres)
        load_wd(2)
        kd = 0
        for cb in range(2):
            if cb == 1:
                load_wd(3)
            for (t, n, j, col) in tlb:
                bk = 6 + (kd % 2)
                kd += 1
                for f in range(NFF):
                    g = cb * 2 + f // 11
                    sl = g % 3
                    S.op("pe", lambda e, f=f, sl=sl, bk=bk, n=n, col=col: e.matmul(
                        PB[bk][:n, :], lhsT=HT[:, f, col:col + n], rhs=WD[:, sl, f % 11, :], start=(f == 0),
                        stop=(f == NFF - 1)), reads=["HT%d.%d" % (j, f), "WD%d" % sl], writes=[PBn[bk]])
                S.op("dve", lambda e, cb=cb, bk=bk, n=n, j=j: e.tensor_tensor(
                    out=X[:n, j, cb * 512:(cb + 1) * 512], in0=X[:n, j, cb * 512:(cb + 1) * 512],
                    in1=PB[bk][:n, :], op=ALU.add), reads=[XR(j)[cb], PBn[bk]], writes=[XR(j)[cb]], nofence=NF2)
                if cb == 1:
                    ys_ = kd % 2
                    S.op("act", lambda e, n=n, j=j, t=t, ys_=ys_: e.activation(out=YO[:n, ys_, :], in_=X[:n, j, :],
                                                                               func=AF.Square,
                                                                               accum_out=stat[:n, t, 6:7]),
                         reads=XR(j), writes=["YO%d" % ys_, "st%d.6" % t])
                    S.op("dve", lambda e, n=n, t=t: e.tensor_scalar(out=stat[:n, t, 7:8], in0=stat[:n, t, 6:7],
                                                                    scalar1=1.0 / D, scalar2=EPS, op0=ALU.mult,
                                                                    op1=ALU.add),
                         reads=["st%d.6" % t], writes=["st%d.7" % t], nofence=NF2)
                    S.op("pool", lambda e, n=n, t=t: e.tensor_tensor(out=stat[:n, t, 8:9], in0=stat[:n, t, 7:8],
                                                                     in1=neghalf[:n, 0:1], op=ALU.pow),
                         reads=["st%d.7" % t, "tabs"], writes=["st%d.8" % t], nofence=NF2)
                    S.op("dve", lambda e, n=n, j=j, t=t, ys_=ys_: e.scalar_tensor_tensor(
                        out=YO[:n, ys_, :], in0=X[:n, j, :], scalar=stat[:n, t, 8:9], in1=gf[:n, :],
                        op0=ALU.mult, op1=ALU.mult), reads=XR(j) + ["st%d.8" % t, "gf"], writes=["YO%d" % ys_])
                    dst = ys if t == 17 else yp[128 * (t - 1):128 * t, :]
                    S.op("sp", lambda e, dst=dst, n=n, ys_=ys_: e.dma_start(out=dst, in_=YO[:n, ys_, :]),
                         reads=["YO%d" % ys_], dma=True, key="o_y%d" % ys_)
        out_keys.extend(["o_y0", "o_y1"])

    S.emit(final_wait_keys=sorted(set(out_keys)), schedule=SCHEDULE)
    print("[kernel] ops=%d sim_time_us=%.1f" % (len(S.ops), getattr(S, "sim_time", -1)))
    return nc


_CACHE = {}


def kernel(x_prompt, x_sample, state_conv, state_ret, meta_tokens, norm1_g, w_in, w_conv, ret_norm_g, w_out,
           norm2_g, w_gate, w_up, w_down, final_norm_g):
    f = lambda a: np.ascontiguousarray(np.asarray(a, dtype=np.float32))
    x_prompt, x_sample, state_conv, state_ret = f(x_prompt), f(x_sample), f(state_conv), f(state_ret)
    cs, tabs, identf, smT = _tables()
    wi, wo, wg_, wu_, wd_ = f(w_in)[0], f(w_out)[0], f(w_gate)[0], f(w_up)[0], f(w_down)[0]
    c = np.ascontiguousarray
    cwb = c(wi[:, :1536].reshape(8, 128, 3, 4, 128).transpose(3, 1, 0, 2, 4)).reshape(4, 128, 8 * 384)
    wretb = c(wi[:, 1536:].reshape(8, 128, 2048).transpose(1, 0, 2)).reshape(128, 8 * 2048)
    woutb = c(wo.reshape(8, 128, D).transpose(1, 0, 2)).reshape(128, 8 * D)
    gu = np.stack([wg_.reshape(8, 128, NFF, 128), wu_.reshape(8, 128, NFF, 128)], axis=3)
    gub = c(gu.transpose(2, 1, 0, 3, 4)).reshape(NFF, 128, 8 * 256)
    wdb = c(wd_.reshape(2, 11, 128, 2, 512).transpose(3, 0, 2, 1, 4)).reshape(4, 128, 11 * 512)
    shared = {
        "meta": f(meta_tokens), "w_conv": f(w_conv)[0], "rng": f(ret_norm_g)[0],
        "n1": f(norm1_g)[0], "n2": f(norm2_g)[0], "nf": f(final_norm_g),
        "cwb": cwb, "wretb": wretb, "woutb": woutb, "gub": gub, "wdb": wdb,
        "cs": cs, "tabs": tabs, "identf": identf, "smT": smT,
    }
    in_maps = []
    for c in range(8):
        m = dict(shared)
        m["xp"] = x_prompt[c]
        m["xsamp"] = x_sample[16 * c:16 * c + 16].reshape(64, D)
        m["sconv"] = state_conv[0, 16 * c:16 * c + 16]
        m["sret"] = state_ret[0, 16 * c:16 * c + 16]
        in_maps.append(m)
    if "nc" not in _CACHE:
        _CACHE["nc"] = build_nc()
    res = run_bass_kernel_spmd(_CACHE["nc"], in_maps, core_ids=list(range(8)))
    r = res.results
    y_prompt = np.stack([r[c]["yp"] for c in range(8)], 0)
    y_sample = np.concatenate([r[c]["ys"].reshape(16, 4, D) for c in range(8)], 0)
    ncp = np.stack([r[c]["ncp"] for c in range(8)], 0)[None]
    nrp = np.stack([r[c]["nrp"] for c in range(8)], 0)[None]
    ncs = np.concatenate([r[c]["ncs"] for c in range(8)], 0)[None]
    nrs = np.concatenate([r[c]["nrs"] for c in range(8)], 0)[None]
    return (y_prompt.astype(np.float32), y_sample.astype(np.float32), ncp.astype(np.float32),
            nrp.astype(np.float32), ncs.astype(np.float32), nrs.astype(np.float32))
```

```python
import contextlib
import os
import numpy as np
import concourse.bass as bass
import concourse.mybir as mybir
from concourse.bass_utils import run_bass_kernel_spmd

F32 = mybir.dt.float32
BF16 = mybir.dt.bfloat16
AF = mybir.ActivationFunctionType
ALU = mybir.AluOpType
AX = mybir.AxisListType

D = 1024
SEQ = 2048
NMETA = 16
DFF = 2816
NFF = DFF // 128
INC = 3584
EPS = 1e-6
GN_EPS = 1e-5
PAST = 16384
ENGS = ("pe", "act", "dve", "pool", "sp")
SCHEDULE = True
A3_AFTER_RET = False
PE_COLD = float(os.environ.get("MK_PE_COLD", "1.0"))
HOP = float(os.environ.get("MK_HOP", "0.15"))
PE_BASE = float(os.environ.get("MK_PE_BASE", "0.03"))
ACT_BASE = float(os.environ.get("MK_ACT_BASE", "0.25"))
ACT_RATE = float(os.environ.get("MK_ACT_RATE", "1200"))
POOL_BASE = float(os.environ.get("MK_POOL_BASE", "1.5"))
DMA_LAT = float(os.environ.get("MK_DMA_LAT", "2.0"))
LATE_PE = os.environ.get("MK_LATE_PE", "1") == "1"
NF2 = os.environ.get("MK_NF2", "1") == "1"
JITTER = float(os.environ.get("MK_JITTER", "0.0"))
_RNG = np.random.RandomState(int(os.environ.get("MK_SEED", "0")))
STRICT = os.environ.get("MK_STRICT", "1") == "1"


class _Op:
    __slots__ = ("eng", "fn", "dma", "key", "deps", "odeps", "signal", "count", "idx", "cost", "start")


class _Fake:
    def __init__(self):
        self.rec = None

    def __getattr__(self, name):
        def f(*a, **k):
            self.rec = (name, a, k)
            return self
        return f


def _prod(t):
    r = 1
    for x in t:
        r *= int(x)
    return r


class Sched:
    def __init__(self, nc):
        self.nc = nc
        self.ops = []
        self.last_w = {}
        self.readers = {}
        self.excl = set()
        self.dma_keys = {}
        self.last_dma = {}
        self.auto = None
        self.pe_scale = 1.0

    def _cost(self, eng, fn, dma):
        fk = _Fake()
        try:
            fn(fk)
            name, a, k = fk.rec
            out = k.get("out", a[0] if a else None)
            shp = tuple(out.shape)
            F = _prod(shp[1:]) if len(shp) > 1 else 1
            if dma:
                c = _prod(shp) * 4 / 300e3
                if k.get("allow_slow_non_contiguous"):
                    c += _prod(shp) * 0.004
                return c
            if eng == "pe":
                if name == "matmul":
                    M = _prod(tuple(k["lhsT"].shape)[1:])
                    N = _prod(tuple(k["rhs"].shape)[1:])
                    return PE_BASE + max(N / 2400.0, M / 1200.0)
                return 0.1
            if eng == "dve":
                return 0.13 + F / 960.0
            if eng == "act":
                return ACT_BASE + F / ACT_RATE
            if eng == "pool":
                return POOL_BASE + F / 600.0
        except Exception:
            pass
        return 0.5

    def op(self, eng, fn, reads=(), writes=(), dma=False, key=None, nofence=False):
        o = _Op()
        reads = list(reads)
        writes = list(writes)
        late = False
        if self.auto is not None and not nofence and self.auto not in writes:
            if eng == "pe" and LATE_PE:
                late = True
            else:
                reads.append(self.auto)
        o.eng, o.fn, o.dma, o.key = eng, fn, dma, key
        o.idx = len(self.ops)
        o.signal = False
        o.count = None
        o.cost = self._cost(eng, fn, dma) * (self.pe_scale if eng == "pe" else 1.0)
        if JITTER > 0:
            o.cost *= 1.0 + JITTER * (2.0 * _RNG.random() - 1.0)
        deps = {}
        for r in reads:
            w = self.last_w.get(r)
            if w is not None:
                deps[w] = True
            if r in self.excl:
                for x in self.readers.get(r, ()):
                    if self.ops[x].eng != eng:
                        deps.setdefault(x, False)
        for w_ in writes:
            w = self.last_w.get(w_)
            if w is not None:
                deps.setdefault(w, False)
            for x in self.readers.get(w_, ()):
                deps.setdefault(x, False)
        fin = []
        for d, raw in deps.items():
            Dd = self.ops[d]
            if Dd.dma:
                fin.append(d)
            elif Dd.eng == eng and not dma:
                if eng in ("act", "dve", "pool") and (raw or STRICT):
                    fin.append(d)
            else:
                fin.append(d)
        o.deps = fin
        o.odeps = set(deps.keys())
        if dma and key in self.last_dma:
            o.odeps.add(self.last_dma[key])
        for r in reads:
            self.readers.setdefault(r, []).append(o.idx)
        if late:
            self.readers.setdefault(self.auto, []).append(o.idx)
        for w_ in writes:
            self.last_w[w_] = o.idx
            self.readers[w_] = []
        if dma:
            self.dma_keys[key] = self.dma_keys.get(key, 0) + 16
            o.count = self.dma_keys[key]
            self.last_dma[key] = o.idx
        self.ops.append(o)
        return o

    def schedule(self, K=48):
        import heapq
        ops = self.ops
        n = len(ops)
        succ = [[] for _ in range(n)]
        npred = [0] * n
        for o in ops:
            npred[o.idx] = len(o.odeps)
            for d in o.odeps:
                succ[d].append(o.idx)
        ready_t = [0.0] * n
        fin_t = [0.0] * n
        rank = [0.0] * n
        for i in range(n - 1, -1, -1):
            m = 0.0
            for j in succ[i]:
                if rank[j] > m:
                    m = rank[j]
            rank[i] = ops[i].cost + m + (DMA_LAT if ops[i].dma else 0.0) + HOP
        cand = {e: [] for e in ENGS}
        for o in ops:
            if npred[o.idx] == 0:
                cand[o.eng].append(o.idx)
        free = {e: 0.0 for e in ENGS}
        dma_free = 0.0
        order = {e: [] for e in ENGS}
        done = 0
        while done < n:
            best = None
            for e in ENGS:
                h = cand[e]
                if not h:
                    continue
                T = free[e]
                rd = [i for i in h if ready_t[i] <= T]
                if rd:
                    pick = min(rd, key=lambda i: (-rank[i], i))
                else:
                    pick = min(h, key=lambda i: (ready_t[i], -rank[i], i))
                st = max(T, ready_t[pick])
                if best is None or st < best[0]:
                    best = (st, e, pick)
            st, e, i = best
            cand[e].remove(i)
            o = ops[i]
            o.start = st
            if o.dma:
                t0 = max(st, dma_free)
                dma_free = t0 + o.cost
                free[e] = st + (o.cost if e == "pool" else 0.1)
                fin_t[i] = dma_free + DMA_LAT
            else:
                free[e] = st + o.cost
                fin_t[i] = free[e]
            order[e].append(o)
            done += 1
            for j in succ[i]:
                f_ = fin_t[i] + (HOP if ops[j].eng != e else 0.0)
                if f_ > ready_t[j]:
                    ready_t[j] = f_
                npred[j] -= 1
                if npred[j] == 0:
                    cand[ops[j].eng].append(j)
        self.sim_time = max(fin_t) if n else 0.0
        return order

    def emit(self, final_wait_keys=(), schedule=True):
        nc = self.nc
        ops = self.ops
        for o in ops:
            for d in o.deps:
                if not ops[d].dma:
                    ops[d].signal = True
        if schedule:
            per_eng = self.schedule()
        else:
            per_eng = {e: [o for o in ops if o.eng == e] for e in ENGS}
        for e in ENGS:
            c = 0
            for o in per_eng[e]:
                if not o.dma and o.signal:
                    c += 1
                    o.count = c
        with contextlib.ExitStack() as st:
            esem = {e: st.enter_context(nc.semaphore("s_" + e)) for e in ENGS}
            dsem = {k: st.enter_context(nc.semaphore("d_%d" % i)) for i, k in enumerate(self.dma_keys)}
            block = st.enter_context(nc.Block())

            def run(engname, eng):
                waited = {}
                for o in per_eng[engname]:
                    need = {}
                    for d in o.deps:
                        Dd = ops[d]
                        s = ("d", Dd.key) if Dd.dma else ("e", Dd.eng)
                        if Dd.count > need.get(s, 0):
                            need[s] = Dd.count
                    for s, v in need.items():
                        if waited.get(s, 0) >= v:
                            continue
                        waited[s] = v
                        eng.wait_ge(dsem[s[1]] if s[0] == "d" else esem[s[1]], v)
                    ins = o.fn(eng)
                    if o.dma:
                        ins.then_inc(dsem[o.key], 16)
                    elif o.signal:
                        ins.then_inc(esem[engname], 1)
                if engname == "sp":
                    for k in final_wait_keys:
                        eng.wait_ge(dsem[k], self.dma_keys[k])

            block.sync(lambda e: run("sp", e))
            block.tensor(lambda e: run("pe", e))
            block.scalar(lambda e: run("act", e))
            block.vector(lambda e: run("dve", e))
            block.gpsimd(lambda e: run("pool", e))


def _tables():
    half = 32
    inv = 10000.0 ** (-np.arange(half, dtype=np.float64) / half)
    lg = np.log1p(-(2.0 ** (-5.0 - np.arange(8)))).astype(np.float32).astype(np.float64)
    cs = np.zeros((128, 18, 2, 32), np.float32)
    for t in range(18):
        i = np.arange(128)
        if t == 0:
            pos = i.astype(np.float64)
        elif t <= 16:
            pos = (16 + 128 * (t - 1) + i).astype(np.float64)
        else:
            pos = (PAST + (i % 4)).astype(np.float64)
        ang = pos[:, None] * inv[None, :]
        cs[:, t, 0] = np.cos(ang)
        cs[:, t, 1] = np.sin(ang)
    tabs = np.zeros((128, 336), np.float32)
    i = np.arange(128)
    for kind in range(2):
        il = i if kind == 0 else (i % 4)
        tabs[:, kind * 8:kind * 8 + 8] = np.exp(-(il[:, None] + 1.0) * lg[None, :]) * 0.125
        tabs[:, 16 + kind * 8:16 + kind * 8 + 8] = GN_EPS * np.exp(-2.0 * (il[:, None] + 1.0) * lg[None, :])
    for k, L in enumerate((128.0, 16.0, 4.0)):
        tabs[:, 32 + k * 8:32 + k * 8 + 8] = np.exp(L * lg)[None, :]
    tabs[:, 56:72] = (i[:, None] // 4 == np.arange(16)[None, :]).astype(np.float32)
    tabs[:, 72:80] = -0.5
    m0 = (i[None, :] >= i[:, None]).astype(np.float32)
    m1 = m0 * (i[None, :] // 4 == i[:, None] // 4)
    tabs[:, 80:208] = m0
    tabs[:, 208:336] = m1
    smT = np.zeros((64, 16, 64), np.float32)
    for s in range(16):
        smT[:, s, 4 * s:4 * s + 4] = 1.0
    return cs.reshape(128, 18 * 64), tabs, np.eye(128, dtype=np.float32), smT.reshape(64, 1024)


def build_nc():
    nc = bass.Bass("TRN2", target_bir_lowering=False)
    S = Sched(nc)

    def din(name, shape):
        return nc.dram_tensor(name, list(shape), F32, kind="ExternalInput").ap()

    def dout(name, shape):
        return nc.dram_tensor(name, list(shape), F32, kind="ExternalOutput").ap()

    xp = din("xp", [SEQ, D]); meta = din("meta", [NMETA, D]); xsamp = din("xsamp", [64, D])
    sconv = din("sconv", [16, 2, 512]); sret = din("sret", [16, 8, 64, 64])
    w_conv = din("w_conv", [3, 512]); rng_d = din("rng", [512])
    n1 = din("n1", [D]); n2 = din("n2", [D]); nf = din("nf", [D])
    cwb = din("cwb", [4, 128, 8 * 384])
    wretb = din("wretb", [128, 8 * 2048])
    woutb = din("woutb", [128, 8 * D])
    gub = din("gub", [NFF, 128, 8 * 256])
    wdb = din("wdb", [4, 128, 11 * 512])
    cs_d = din("cs", [128, 18 * 64]); tabs_d = din("tabs", [128, 336]); ident_d = din("identf", [128, 128])
    smT_d = din("smT", [64, 1024])
    yp = dout("yp", [SEQ, D]); ys = dout("ys", [64, D]); ncp = dout("ncp", [2, 512]); nrp = dout("nrp", [8, 64, 64])
    ncs = dout("ncs", [16, 2, 512]); nrs = dout("nrs", [16, 8, 64, 64])

    def A(name, shape, dt):
        return nc.alloc_sbuf_tensor("sb_" + name, shape, dt)

    X = A("X", [128, 9, D], F32)
    XT = A("XT", [128, 8, 1088], BF16)
    AL = A("AL", [128, 25088], BF16)
    WRET = AL[:, 0:16384].rearrange("p (k n) -> p k n", k=8)
    MIXT = AL[:, 16384:16384 + 8704].rearrange("p (k n) -> p k n", k=8)
    HT = AL[:, 0:NFF * 1088].rearrange("p (f n) -> p f n", f=NFF)
    AL2 = A("AL2", [128, 43584], BF16)
    _off = [0]

    def carve(nelem_bf16):
        o = _off[0]
        _off[0] += nelem_bf16
        assert _off[0] <= 43584, _off[0]
        return AL2[:, o:o + nelem_bf16]

    def carve_f32(nelem):
        return carve(2 * nelem).bitcast(F32)

    CW = carve(2 * 8 * 384).rearrange("p (s k n) -> p s k n", s=2, k=8)
    WOUT = carve(8 * D).rearrange("p (k n) -> p k n", k=8)
    r1 = _off[0]
    hsb = carve_f32(2 * 512).rearrange("p (s n) -> p s n", s=2)
    ubuf = carve_f32(520)
    cacc = carve_f32(512)
    ext = carve_f32(96).rearrange("p (s t) -> p s t", s=16)
    conv_end = _off[0]
    _off[0] = r1

    def ret_set():
        d = {}
        d["qrot"] = carve(512); d["ktil"] = carve(512); d["vbf"] = carve(512)
        d["sg"] = carve_f32(512)
        d["QT"] = carve(1024).rearrange("p (h n) -> p h n", h=8)
        d["KT"] = carve(1024).rearrange("p (h n) -> p h n", h=8)
        d["PT"] = carve(1024).rearrange("p (h n) -> p h n", h=8)
        d["osq"] = carve_f32(512)
        d["ro"] = carve(512)
        return d

    RS = [None, None]
    RS[1] = ret_set()
    _off[0] = max(_off[0], conv_end)
    RS[0] = ret_set()
    t1 = carve_f32(512); t2 = carve_f32(512)
    on = carve_f32(512)
    Stmp = carve_f32(512)
    Sbf = carve(512).rearrange("p (h e) -> p h e", h=8)
    QTm = carve(2 * 512).rearrange("p (s h n) -> p s h n", s=2, h=8)
    km = carve(2 * 512).rearrange("p (s n) -> p s n", s=2)
    S0b = carve_f32(3 * 512).rearrange("p (s h e) -> p s h e", s=3, h=8)
    S0bf = carve(2 * 512).rearrange("p (s h e) -> p s h e", s=2, h=8)
    S0t = carve_f32(2 * 512).rearrange("p (s h e) -> p s h e", s=2, h=8)
    So = carve_f32(2 * 512).rearrange("p (s h e) -> p s h e", s=2, h=8)
    endA = _off[0]
    _off[0] = 0
    GU = carve(3 * 8 * 256).rearrange("p (s k n) -> p s k n", s=3, k=8)
    WD = carve(3 * 11 * 512).rearrange("p (s f n) -> p s f n", s=3, f=11)
    sgate = carve_f32(2 * 512).rearrange("p (s n) -> p s n", s=2)
    YO = carve_f32(2 * D).rearrange("p (s n) -> p s n", s=2)
    gf = carve_f32(D)
    endB = _off[0]
    xs2 = A("xs", [128, 2, D], BF16)
    stat = A("stat", [128, 18, 12], F32)
    gst = A("gst", [128, 2, 48], F32)
    S32 = A("S32", [64, 8, 64], F32)
    ucar = A("ucar", [128, 4, 2], F32)
    cst = A("cst", [128, 4, 2], F32)
    cso = A("cso", [128, 4, 16, 2], F32)
    csi = A("csi", [128, 4, 16, 2], F32)
    idf = A("idf", [128, 128], F32)
    rowb = A("rowb", [32, 512], F32)
    rowc = A("rowc", [8, 3, 128], F32)
    cs_t = A("cs_t", [128, 18, 2, 32], F32)
    tabs = A("tabs", [128, 336], F32)
    ident = A("ident", [128, 128], BF16)
    smT = A("smT", [64, 16, 64], BF16)
    wc = A("wc", [128, 4, 3], F32)
    g1 = A("g1", [128, 8], F32); g2 = A("g2", [128, 8], F32)
    rngt = A("rngt", [128, 4], F32)
    fsc = A("fsc", [128, 8], F32)
    mk = A("mk", [128, 8], F32)
    kdec = tabs[:, 0:16].rearrange("p (k h) -> p k h", k=2)
    epsq = tabs[:, 16:32].rearrange("p (k h) -> p k h", k=2)
    gl = tabs[:, 32:56].rearrange("p (k h) -> p k h", k=3)
    seqmask = tabs[:, 56:72]
    neghalf = tabs[:, 72:80]
    maskt = tabs[:, 80:336].rearrange("p (k n) -> p k n", k=2)
    PB = [nc.alloc_psum_tensor("pb%d" % i, [128, 512], F32) for i in range(8)]
    PBn = ["pb%d" % i for i in range(8)]
    S.excl.update(PBn)

    def pbf(i):
        return PB[i][:, :].bitcast(BF16)

    def ld(q, dst, src, res, **kw):
        S.op(q, lambda e: e.dma_start(out=dst, in_=src, **kw), writes=[res], dma=True, key=res, nofence=True)

    ld("sp", tabs[:, :], tabs_d, "tabs")
    ld("pool", ident[:, :], ident_d, "ident")
    ld("sp", idf[:, :], ident_d, "idf")
    ld("sp", rowc[:, 0, :], n1.rearrange("(k p) -> k p", p=128), "rowc0")
    ld("sp", rowc[:, 1, :], n2.rearrange("(k p) -> k p", p=128), "rowc1")
    ld("sp", rowc[0:4, 2, :], rng_d.rearrange("(k p) -> k p", p=128), "rowc2")
    ld("sp", rowb[0:3, :], w_conv, "rowb")

    def tr32(out_ps, lhsT, kk, reads):
        S.op("pe", lambda e: e.matmul(out_ps, lhsT=lhsT, rhs=idf[:kk, :kk], start=True, stop=True),
             reads=list(reads) + ["idf"], writes=["pb7"], nofence=True)

    tr32(PB[7][:, 0:8], rowc[:8, 0, :], 8, ["rowc0"])
    tr32(PB[7][:, 8:16], rowc[:8, 1, :], 8, ["rowc1"])
    tr32(PB[7][:, 16:20], rowc[:4, 2, :], 4, ["rowc2"])
    for cc_ in range(4):
        tr32(PB[7][:, 20 + 3 * cc_:23 + 3 * cc_], rowb[:3, cc_ * 128:(cc_ + 1) * 128], 3, ["rowb"])
    S.op("dve", lambda e: e.tensor_copy(out=g1[:, :], in_=PB[7][:, 0:8]), reads=["pb7"], writes=["g1"], nofence=True)
    S.op("dve", lambda e: e.tensor_copy(out=g2[:, :], in_=PB[7][:, 8:16]), reads=["pb7"], writes=["g2"], nofence=True)
    S.op("dve", lambda e: e.tensor_copy(out=rngt[:, :], in_=PB[7][:, 16:20]), reads=["pb7"], writes=["rngt"],
         nofence=True)
    S.op("dve", lambda e: e.tensor_copy(out=wc[:, :, :].rearrange("p c j -> p (c j)"), in_=PB[7][:, 20:32]),
         reads=["pb7"], writes=["wc"], nofence=True)

    def late_tables():
        ld("sp", cs_t[:, :, :, :], cs_d.rearrange("p (t k d) -> p t k d", t=18, k=2), "cs_t")
        ld("pool", smT[:, :, :], smT_d.rearrange("p (s n) -> p s n", s=16), "smT")

    S.op("dve", lambda e: e.memset(S32[:, :, :], 0.0), writes=["S32"], nofence=True)
    S.op("dve", lambda e: e.memset(fsc[:, :], 0.0), writes=["PH"], nofence=True)
    S.auto = "PH"

    def fence():
        S.op("dve", lambda e: e.memset(fsc[:, :], 0.0), writes=["PH"])

    def tiles_of(hf):
        if hf == 0:
            return [(0, 16, 0, 0)] + [(t, 128, t, 16 + 128 * (t - 1)) for t in range(1, 9)]
        return [(t, 128, t - 9, 128 * (t - 9)) for t in range(9, 17)] + [(17, 64, 8, 1024)]

    def split(c0, c1, k):
        b = [c0 + (c1 - c0) * i // k for i in range(k + 1)]
        return [(b[i], b[i + 1]) for i in range(k)]

    def overlapping(tl, c0, c1):
        return [j for (t, n, j, col) in tl if col < c1 and col + n > c0]

    pt_rot = [0]

    def XR(j):
        return ["X%da" % j, "X%db" % j]

    def norm_stats(t, n, j, sc, xsl, nf=False, src=None, src_res=None):
        src = X[:n, j, :] if src is None else src
        src_res = XR(j) if src_res is None else src_res
        S.op("act", lambda e: e.activation(out=xs2[:n, xsl, :], in_=src, func=AF.Square,
                                           accum_out=stat[:n, t, sc:sc + 1]),
             reads=src_res, writes=["xs%d" % xsl, "st%d.%d" % (t, sc)], nofence=nf)
        S.op("dve", lambda e: e.tensor_scalar(out=stat[:n, t, sc + 1:sc + 2], in0=stat[:n, t, sc:sc + 1],
                                              scalar1=1.0 / D, scalar2=EPS, op0=ALU.mult, op1=ALU.add),
             reads=["st%d.%d" % (t, sc)], writes=["st%d.%d" % (t, sc + 1)], nofence=nf)
        S.op("pool", lambda e: e.tensor_tensor(out=stat[:n, t, sc + 2:sc + 3], in0=stat[:n, t, sc + 1:sc + 2],
                                               in1=neghalf[:n, 0:1], op=ALU.pow),
             reads=["st%d.%d" % (t, sc + 1), "tabs"], writes=["st%d.%d" % (t, sc + 2)], nofence=nf)

    def norm_apply(t, n, j, col, gain, sc, banks, xsl, nf=False, on_dve=False, src=None, src_res=None):
        src = X[:n, j, :] if src is None else src
        src_res = XR(j) if src_res is None else src_res
        bank = banks[pt_rot[0] % len(banks)]
        xs = xs2[:, xsl, :]
        xsn = "xs%d" % xsl
        pt_rot[0] += 1
        if on_dve:
            S.op("dve", lambda e: e.tensor_scalar(out=xs[:n, :], in0=src, scalar1=stat[:n, t, sc + 2:sc + 3],
                                                  scalar2=None, op0=ALU.mult),
                 reads=src_res + ["st%d.%d" % (t, sc + 2)], writes=[xsn], nofence=nf)
        else:
            S.op("act", lambda e: e.activation(out=xs[:n, :], in_=src, func=AF.Copy,
                                               scale=stat[:n, t, sc + 2:sc + 3]),
                 reads=src_res + ["st%d.%d" % (t, sc + 2)], writes=[xsn], nofence=nf)
        pv = pbf(bank).rearrange("p (k n) -> p k n", k=8)
        for kc in range(8):
            S.op("pe", lambda e, kc=kc: e.transpose(out=pv[:, kc, :n], in_=xs[:n, kc * 128:(kc + 1) * 128],
                                                    identity=ident[:n, :n]),
                 reads=[xsn, "ident"], writes=[PBn[bank]], nofence=nf)
        S.op("dve", lambda e: e.tensor_tensor(out=XT[:, :, col:col + n], in0=pv[:, :, :n],
                                              in1=gain[:, :].unsqueeze(2).to_broadcast([128, 8, n]), op=ALU.mult),
             reads=[PBn[bank], "g1", "g2"], writes=["XT%d" % j], nofence=nf)

    out_keys = []
    CWR = {0: ["GU0", "GU1a"], 1: ["GU1b", "GU2"]}
    GUR = {0: ["GU0"], 1: ["GU1a", "GU1b"], 2: ["GU2"]}

    for hf in range(2):
        tl = tiles_of(hf)
        if hf == 1:
            fence()
        for (t, n, j, col) in tl:
            if t == 0:
                src = meta
            elif t <= 16:
                src = xp[128 * (t - 1):128 * t, :]
            else:
                src = xsamp
            S.op("sp", lambda e, src=src, n=n, j=j: e.dma_start(out=X[:n, j, :], in_=src),
                 reads=(["X2a"] if (hf == 0 and j > 2 and os.environ.get("MK_XPRIO", "1") == "1") else []),
                 writes=XR(j), dma=True, key="X%d" % j, nofence=True)
        def load_cw(cc):
            sl = cc % 2
            S.op("pool", lambda e: e.dma_start(out=CW[:, sl, :, :].rearrange("p k n -> p (k n)"), in_=cwb[cc]),
                 reads=(["X6a"] if (hf == 0 and cc == 1) else []), writes=CWR[sl], dma=True, key="CW%d" % sl,
                 nofence=True)

        load_cw(0)
        load_cw(1)
        if hf == 0:
            late_tables()
        S.op("act", lambda e: e.copy(out=Sbf[:64, :, :], in_=S32[:, :, :]), reads=["S32"], writes=["Sbf"])
        if hf == 0:
            for i_, (t, n, j, col) in enumerate(tl):
                norm_stats(t, n, j, 0, i_ % 2, nf=True)
                if i_ >= 1:
                    (t_, n_, j_, col_) = tl[i_ - 1]
                    norm_apply(t_, n_, j_, col_, g1, 0, [0, 1], (i_ - 1) % 2, nf=True, on_dve=True)
            (t_, n_, j_, col_) = tl[-1]
            norm_apply(t_, n_, j_, col_, g1, 0, [0, 1], (len(tl) - 1) % 2, nf=True, on_dve=True)
        def load_wret(blk):
            S.op("pool", lambda e: e.dma_start(
                out=WRET[:, 2 * blk:2 * blk + 2, :].rearrange("p k n -> p (k n)"),
                in_=wretb[:, blk * 4096:(blk + 1) * 4096]), reads=(["X8a"] if hf == 0 else []),
                writes=["WRET%d" % blk], dma=True, key="WRET%d" % blk)

        def load_wout(cb):
            S.op("pool", lambda e: e.dma_start(
                out=WOUT[:, 4 * cb:4 * cb + 4, :].rearrange("p k n -> p (k n)"),
                in_=woutb[:, cb * 4096:(cb + 1) * 4096]), writes=["WOUT%d" % cb], dma=True, key="WOUT%d" % cb)

        if hf == 0:
            if os.environ.get("MK_UNEVEN", "0") == "1":
                pch = [(0, 144, "p"), (144, 592, "p"), (592, 1040, "p")]
            else:
                pch = [(a, b, "p") for (a, b) in split(0, 1040, 3)]
        else:
            pch = [(a, b, "p") for (a, b) in split(0, 1024, 3)] + [(1024, 1088, "s")]
        n_p = len([x for x in pch if x[2] == "p"])

        def conv_chunk(cc, ci, c0, c1, kind, sl, bks, hs):
            N = c1 - c0
            rd = ["XT%d" % j for j in overlapping(tl, c0, c1)] + CWR[sl]
            for br in range(3):
                for kc in range(8):
                    S.op("pe", lambda e, br=br, kc=kc: e.matmul(
                        PB[bks[br]][:, :N], lhsT=CW[:, sl, kc, br * 128:(br + 1) * 128], rhs=XT[:, kc, c0:c1],
                        start=(kc == 0), stop=(kc == 7)), reads=rd, writes=[PBn[bks[br]]])
            pb_, pc_, ph_ = bks
            S.op("act", lambda e: e.copy(out=hsb[:, hs, :N], in_=PB[ph_][:, :N]),
                 reads=[PBn[ph_]], writes=["hsb%d" % hs])
            mxw = ["MXc%d.%d" % (j, cc) for j in overlapping(tl, c0, c1)]
            if kind == "p":
                if ci == 0:
                    if hf == 0:
                        S.op("dve", lambda e: e.memset(ubuf[:, 0:2], 0.0), writes=["ubuf"])
                    else:
                        S.op("dve", lambda e: e.tensor_copy(out=ubuf[:, 0:2], in_=ucar[:, cc, :]),
                             reads=["ucar%d" % cc], writes=["ubuf"])
                S.op("dve", lambda e: e.tensor_tensor(
                    out=ubuf[:, 2:2 + N], in0=PB[pc_][:, :N], in1=hsb[:, hs, :N], op=ALU.mult),
                    reads=[PBn[pc_], "hsb%d" % hs], writes=["ubuf"])
                S.op("act", lambda e: e.activation(out=cacc[:, :N], in_=ubuf[:, 2:2 + N], func=AF.Copy,
                                                   scale=wc[:, cc, 2:3]),
                     reads=["ubuf", "wc"], writes=["cacc"])
                S.op("dve", lambda e: e.scalar_tensor_tensor(
                    out=cacc[:, :N], in0=ubuf[:, 1:1 + N], scalar=wc[:, cc, 1:2], in1=cacc[:, :N],
                    op0=ALU.mult, op1=ALU.add), reads=["ubuf", "wc", "cacc"], writes=["cacc"])
                S.op("dve", lambda e: e.scalar_tensor_tensor(
                    out=cacc[:, :N], in0=ubuf[:, 0:N], scalar=wc[:, cc, 0:1], in1=cacc[:, :N],
                    op0=ALU.mult, op1=ALU.add), reads=["ubuf", "wc", "cacc"], writes=["cacc"])
                S.op("dve", lambda e: e.tensor_tensor(
                    out=MIXT[:, cc, c0:c1], in0=PB[pb_][:, :N], in1=cacc[:, :N], op=ALU.mult),
                    reads=[PBn[pb_], "cacc"], writes=mxw)
                last_p = (ci == n_p - 1)
                if last_p and hf == 0:
                    S.op("dve", lambda e: e.tensor_copy(out=ucar[:, cc, :], in_=ubuf[:, N:N + 2]),
                         reads=["ubuf"], writes=["ucar%d" % cc])
                elif last_p:
                    S.op("dve", lambda e: e.tensor_copy(out=cst[:, cc, :], in_=ubuf[:, N:N + 2]),
                         reads=["ubuf"], writes=["cst%d" % cc])
                else:
                    S.op("dve", lambda e: e.tensor_copy(out=ubuf[:, 0:2], in_=ubuf[:, N:N + 2]),
                         reads=["ubuf"], writes=["ubuf"])
            else:
                S.op("dve", lambda e: e.tensor_copy(out=ext[:, :, 0:2], in_=csi[:, cc, :, :]), reads=["csi"],
                     writes=["ext"])
                pc3 = PB[pc_][:, :64].rearrange("p (s t) -> p s t", t=4)
                hs3 = hsb[:, hs, :64].rearrange("p (s t) -> p s t", t=4)
                ca3 = cacc[:, :64].rearrange("p (s t) -> p s t", t=4)
                S.op("dve", lambda e: e.tensor_tensor(out=ext[:, :, 2:6], in0=pc3, in1=hs3, op=ALU.mult),
                     reads=[PBn[pc_], "hsb%d" % hs], writes=["ext"])
                S.op("dve", lambda e: e.tensor_scalar(
                    out=ca3, in0=ext[:, :, 2:6], scalar1=wc[:, cc, 2:3], scalar2=None, op0=ALU.mult),
                    reads=["ext", "wc"], writes=["cacc"])
                S.op("dve", lambda e: e.scalar_tensor_tensor(
                    out=ca3, in0=ext[:, :, 1:5], scalar=wc[:, cc, 1:2], in1=ca3, op0=ALU.mult, op1=ALU.add),
                    reads=["ext", "wc", "cacc"], writes=["cacc"])
                S.op("dve", lambda e: e.scalar_tensor_tensor(
                    out=ca3, in0=ext[:, :, 0:4], scalar=wc[:, cc, 0:1], in1=ca3, op0=ALU.mult, op1=ALU.add),
                    reads=["ext", "wc", "cacc"], writes=["cacc"])
                S.op("dve", lambda e: e.tensor_tensor(
                    out=MIXT[:, cc, c0:c1], in0=PB[pb_][:, :64], in1=cacc[:, :64], op=ALU.mult),
                    reads=[PBn[pb_], "cacc"], writes=mxw)
                S.op("dve", lambda e: e.tensor_copy(out=cso[:, cc, :, :], in_=ext[:, :, 4:6]),
                     reads=["ext"], writes=["cso%d" % cc])

        cset = 0
        for cc in range(4):
            for ci, (c0, c1, kind) in enumerate(pch):
                conv_chunk(cc, ci, c0, c1, kind, cc % 2, (1, 2, 3) if cset % 2 == 0 else (4, 5, 6), cset % 2)
                cset += 1
            if cc + 2 < 4:
                load_cw(cc + 2)
            if cc == 0:
                load_wret(0)
                load_wret(1)
            elif cc == 1:
                load_wret(2)
                load_wret(3)
        if hf == 1:
            for cc in range(4):
                S.op("pe", lambda e, cc=cc: e.matmul(PB[7][:32, cc * 128:(cc + 1) * 128],
                                                     lhsT=cso[:, cc, :, :].rearrange("p s t -> p (s t)"), rhs=idf[:, :],
                                                     start=True, stop=True),
                     reads=["cso%d" % cc, "idf"], writes=["pb7"], nofence=True)
            S.op("dve", lambda e: e.tensor_copy(out=rowb[:32, :], in_=PB[7][:32, :]), reads=["pb7"], writes=["rowb"],
                 nofence=True)
            S.op("sp", lambda e: e.dma_start(out=ncs.rearrange("s t c -> (s t) c"), in_=rowb[:32, :]), reads=["rowb"],
                 dma=True, key="o_ncs", nofence=True)
            out_keys.append("o_ncs")
            for cc in range(4):
                S.op("pe", lambda e, cc=cc: e.matmul(PB[7][:2, cc * 128:(cc + 1) * 128], lhsT=cst[:, cc, :],
                                                     rhs=idf[:, :], start=True, stop=True),
                     reads=["cst%d" % cc, "idf"], writes=["pb7"], nofence=True)
            S.op("dve", lambda e: e.tensor_copy(out=rowb[:2, :], in_=PB[7][:2, :]), reads=["pb7"], writes=["rowb"],
                 nofence=True)
            S.op("sp", lambda e: e.dma_start(out=ncp, in_=rowb[:2, :]), reads=["rowb"], dma=True, key="o_ncp",
                 nofence=True)
            out_keys.append("o_ncp")

        fence()
        bsets = [(0, 1, 2, 3), (4, 5, 6, 7)]

        def alpha_gen(tile, p):
            (t, n, j, col) = tile
            bset = bsets[p]
            for blk in range(4):
                for kc in range(8):
                    S.op("pe", lambda e, blk=blk, kc=kc: e.matmul(
                        PB[bset[blk]][:n, :], lhsT=XT[:, kc, col:col + n], rhs=WRET[:, kc, blk * 512:(blk + 1) * 512],
                        start=(kc == 0), stop=(kc == 7)), reads=["XT%d" % j, "WRET%d" % (kc // 2)], writes=[PBn[bset[blk]]])
                    if kc % 4 == 3:
                        yield

        def alpha(tile, p):
            for _ in alpha_gen(tile, p):
                pass

        def step(g):
            if g is not None:
                next(g, None)

        def beta(tile, p):
            (t, n, j, col) = tile
            bA, bB, bC, bD = bsets[p]
            R = RS[p]
            kind = 1 if t == 17 else 0
            sfx = str(p)
            qrot, ktil, vbf, sg, QT, KT = R["qrot"], R["ktil"], R["vbf"], R["sg"], R["QT"], R["KT"]
            for h in range(8):
                S.op("act", lambda e, h=h: e.activation(out=vbf[:n, h * 64:(h + 1) * 64],
                                                        in_=PB[bC][:n, h * 64:(h + 1) * 64], func=AF.Copy,
                                                        scale=kdec[:n, kind, h:h + 1]),
                     reads=[PBn[bC], "tabs"], writes=["vbf%s.%d" % (sfx, h)])
            S.op("act", lambda e: e.activation(out=sg[:n, :], in_=PB[bD][:n, :], func=AF.Silu),
                 reads=[PBn[bD]], writes=["sg" + sfx])
            cosb = cs_t[:n, t, 0, :].unsqueeze(1).unsqueeze(1).to_broadcast([n, 8, 2, 32])
            sinb = cs_t[:n, t, 1, :].unsqueeze(1).to_broadcast([n, 8, 32])
            t14 = t1[:n, :].rearrange("p (h t d) -> p h t d", h=8, t=2)
            t24 = t2[:n, :].rearrange("p (h t d) -> p h t d", h=8, t=2)
            pq = pbf(bC).rearrange("p (h n) -> p h n", h=8)
            pk = pbf(bD).rearrange("p (h n) -> p h n", h=8)
            for which, bk in (("q", bA), ("k", bB)):
                p4 = PB[bk][:n, :].rearrange("p (h t d) -> p h t d", h=8, t=2)
                S.op("dve", lambda e, p4=p4: e.tensor_tensor(out=t14, in0=p4, in1=cosb, op=ALU.mult),
                     reads=[PBn[bk], "cs_t"], writes=["t1"])
                S.op("dve", lambda e, p4=p4: e.scalar_tensor_tensor(out=t24[:, :, 0, :], in0=p4[:, :, 1, :],
                                                                   scalar=-1.0, in1=sinb, op0=ALU.mult, op1=ALU.mult),
                     reads=[PBn[bk], "cs_t"], writes=["t2a"])
                S.op("dve", lambda e, p4=p4: e.tensor_tensor(out=t24[:, :, 1, :], in0=p4[:, :, 0, :], in1=sinb,
                                                            op=ALU.mult),
                     reads=[PBn[bk], "cs_t"], writes=["t2b"])
                if which == "q":
                    S.op("dve", lambda e: e.tensor_tensor(out=qrot[:n, :], in0=t1[:n, :], in1=t2[:n, :], op=ALU.add),
                         reads=["t1", "t2a", "t2b"], writes=["qrot" + sfx])
                    for h in range(8):
                        S.op("pe", lambda e, h=h: e.transpose(out=pq[:64, h, :n], in_=qrot[:n, h * 64:(h + 1) * 64],
                                                              identity=ident[:n, :n]),
                             reads=["qrot" + sfx, "ident"], writes=[PBn[bC]])
                    S.op("act", lambda e: e.copy(out=QT[:64, :, :n], in_=pq[:64, :, :n]), reads=[PBn[bC]],
                         writes=["QT" + sfx])
                else:
                    S.op("dve", lambda e: e.tensor_tensor(out=ktil[:n, :], in0=t1[:n, :], in1=t2[:n, :], op=ALU.add),
                         reads=["t1", "t2a", "t2b"], writes=["ktil" + sfx])
                    for h in range(8):
                        S.op("pe", lambda e, h=h: e.transpose(out=pk[:64, h, :n], in_=ktil[:n, h * 64:(h + 1) * 64],
                                                              identity=ident[:n, :n]),
                             reads=["ktil" + sfx, "ident"], writes=[PBn[bD]])
                    S.op("act", lambda e: e.copy(out=KT[:64, :, :n], in_=pk[:64, :, :n]), reads=[PBn[bD]],
                         writes=["KT" + sfx])
            for h in range(8):
                bk = bA if h < 4 else bB
                S.op("pe", lambda e, h=h, bk=bk: e.matmul(
                    PB[bk][:n, (h % 4) * 128:(h % 4) * 128 + n], lhsT=KT[:64, h, :n], rhs=QT[:64, h, :n],
                    start=True, stop=True), reads=["QT" + sfx, "KT" + sfx], writes=[PBn[bk]])

        def gamma1(tile, p, ag=None):
            (t, n, j, col) = tile
            bA, bB, bC, bD = bsets[p]
            R = RS[p]
            kind = 1 if t == 17 else 0
            sfx = str(p)
            ktil, vbf, QT, PT = R["ktil"], R["vbf"], R["QT"], R["PT"]
            for hb, bk in ((0, bA), (1, bB)):
                S.op("dve", lambda e, hb=hb, bk=bk: e.tensor_tensor(
                    out=PT[:n, hb * 4:hb * 4 + 4, :n],
                    in0=PB[bk][:n, :].rearrange("p (h i) -> p h i", h=4)[:, :, :n],
                    in1=maskt[:n, kind, :n].unsqueeze(1).to_broadcast([n, 4, n]), op=ALU.mult),
                    reads=[PBn[bk], "tabs"], writes=["PT%s.%d" % (sfx, hb)])
            if kind == 0:
                for h in range(8):
                    S.op("pe", lambda e, h=h: e.matmul(PB[bC][:n, h * 64:(h + 1) * 64], lhsT=PT[:n, h, :n],
                                                       rhs=vbf[:n, h * 64:(h + 1) * 64], start=True, stop=False),
                         reads=["PT%s.%d" % (sfx, h // 4), "vbf%s.%d" % (sfx, h)], writes=[PBn[bC]])
                    S.op("pe", lambda e, h=h: e.matmul(PB[bC][:n, h * 64:(h + 1) * 64], lhsT=QT[:64, h, :n],
                                                       rhs=Sbf[:64, h, :], start=False, stop=True),
                         reads=["QT" + sfx, "Sbf"], writes=[PBn[bC]])
                    step(ag)
                for h in range(8):
                    S.op("pe", lambda e, h=h: e.matmul(PB[bD][:64, h * 64:(h + 1) * 64],
                                                       lhsT=ktil[:n, h * 64:(h + 1) * 64],
                                                       rhs=vbf[:n, h * 64:(h + 1) * 64], start=True, stop=True),
                         reads=["ktil" + sfx, "vbf%s.%d" % (sfx, h)], writes=[PBn[bD]])
            else:
                for h in range(8):
                    S.op("pe", lambda e, h=h: e.matmul(PB[bC][:n, h * 64:(h + 1) * 64], lhsT=PT[:n, h, :n],
                                                       rhs=vbf[:n, h * 64:(h + 1) * 64], start=(h == 0), stop=False,
                                                       skip_group_check=True),
                         reads=["PT%s.%d" % (sfx, h // 4), "vbf%s.%d" % (sfx, h)], writes=[PBn[bC]])

                def s0_load(s):
                    s4 = s % 3
                    S.op("sp", lambda e: e.dma_start(out=S0b[:64, s4, :, :], in_=sret[s].rearrange("h d e -> d h e")),
                         writes=["S0b%d" % s4], dma=True, key="S0b%d" % s4)

                s0_load(0)
                s0_load(1)
                for s in range(16):
                    sl = s % 2
                    s4 = s % 3
                    if s + 2 < 16:
                        s0_load(s + 2)
                    S.op("act", lambda e, sl=sl, s4=s4: e.copy(out=S0bf[:64, sl, :, :], in_=S0b[:64, s4, :, :]),
                         reads=["S0b%d" % s4], writes=["S0bf%d" % sl])
                    S.op("dve", lambda e, s=s, sl=sl: e.tensor_tensor(
                        out=QTm[:64, sl, :, :64], in0=QT[:64, :, :64],
                        in1=smT[:64, s, :].unsqueeze(1).to_broadcast([64, 8, 64]), op=ALU.mult),
                        reads=["QT" + sfx, "smT"], writes=["QTm%d" % sl])
                    S.op("act", lambda e, s=s, sl=sl: e.activation(out=km[:64, sl, :], in_=ktil[:64, :], func=AF.Copy,
                                                                   scale=seqmask[:64, s:s + 1]),
                         reads=["ktil" + sfx, "tabs"], writes=["km%d" % sl])
                    for h in range(8):
                        S.op("pe", lambda e, h=h, sl=sl, s=s: e.matmul(
                            PB[bC][:n, h * 64:(h + 1) * 64], lhsT=QTm[:64, sl, h, :64], rhs=S0bf[:64, sl, h, :],
                            start=False, stop=(s == 15), skip_group_check=True),
                            reads=["QTm%d" % sl, "S0bf%d" % sl], writes=[PBn[bC]])
                    bk = bA if sl == 0 else bB
                    for h in range(8):
                        S.op("pe", lambda e, h=h, sl=sl, bk=bk: e.matmul(
                            PB[bk][:64, h * 64:(h + 1) * 64], lhsT=km[:64, sl, h * 64:(h + 1) * 64],
                            rhs=vbf[:64, h * 64:(h + 1) * 64], start=True, stop=True),
                            reads=["km%d" % sl, "vbf%s.%d" % (sfx, h)], writes=[PBn[bk]])
                    S.op("dve", lambda e, sl=sl, bk=bk, s4=s4: e.tensor_tensor(
                        out=S0t[:64, sl, :, :], in0=S0b[:64, s4, :, :],
                        in1=PB[bk][:64, :].rearrange("p (h e) -> p h e", h=8), op=ALU.add),
                        reads=["S0b%d" % s4, PBn[bk]], writes=["S0t%d" % sl])
                    S.op("pool", lambda e, sl=sl: e.tensor_tensor(
                        out=So[:64, sl, :, :], in0=S0t[:64, sl, :, :],
                        in1=gl[:64, 2, :].unsqueeze(2).to_broadcast([64, 8, 64]), op=ALU.mult),
                        reads=["S0t%d" % sl, "tabs"], writes=["So%d" % sl])
                    S.op("sp", lambda e, s=s, sl=sl: e.dma_start(out=nrs[s].rearrange("h d e -> d h e"),
                                                                 in_=So[:64, sl, :, :]),
                         reads=["So%d" % sl], dma=True, key="o_nrs%d" % sl)
                out_keys.extend(["o_nrs0", "o_nrs1"])

        def gamma2a(tile, p):
            (t, n, j, col) = tile
            bA, bB, bC, bD = bsets[p]
            R = RS[p]
            kind = 1 if t == 17 else 0
            sfx = str(p)
            sg, osq, ro = R["sg"], R["osq"], R["ro"]
            gs = p
            o3 = PB[bC][:n, :].rearrange("p (h e) -> p h e", h=8)
            G = lambda a, b: gst[:n, gs, a:b]
            gA, gB, gC, gD, gE = ["g%d%s" % (gs, c_) for c_ in "abcde"]
            S.op("dve", lambda e: e.tensor_reduce(out=G(0, 8), in_=o3, axis=AX.X, op=ALU.add),
                 reads=[PBn[bC]], writes=[gA])
            S.op("act", lambda e: e.activation(out=osq[:n, :], in_=PB[bC][:n, :], func=AF.Square),
                 reads=[PBn[bC]], writes=["osq" + sfx])
            S.op("dve", lambda e: e.tensor_reduce(out=G(8, 16), in_=osq[:n, :].rearrange("p (h e) -> p h e", h=8),
                                                  axis=AX.X, op=ALU.add), reads=["osq" + sfx], writes=[gB])
            S.op("dve", lambda e: e.tensor_scalar(out=G(16, 24), in0=G(0, 8), scalar1=1.0 / 64, scalar2=None,
                                                  op0=ALU.mult), reads=[gA], writes=[gC])
            S.op("dve", lambda e: e.tensor_tensor(out=G(24, 32), in0=G(16, 24), in1=G(16, 24), op=ALU.mult),
                 reads=[gC], writes=[gD])
            S.op("dve", lambda e: e.scalar_tensor_tensor(out=G(32, 40), in0=G(8, 16), scalar=1.0 / 64, in1=G(24, 32),
                                                         op0=ALU.mult, op1=ALU.subtract), reads=[gB, gD], writes=[gE])
            S.op("dve", lambda e: e.tensor_tensor(out=G(32, 40), in0=G(32, 40), in1=epsq[:n, kind, :], op=ALU.add),
                 reads=[gE, "tabs"], writes=[gE])
            S.op("pool", lambda e: e.tensor_tensor(out=G(40, 48), in0=G(32, 40), in1=neghalf[:n, :], op=ALU.pow),
                 reads=[gE, "tabs"], writes=["gsr%d" % gs])
            S.op("dve", lambda e: e.scalar_tensor_tensor(out=G(24, 32), in0=G(16, 24), scalar=-1.0, in1=G(40, 48),
                                                         op0=ALU.mult, op1=ALU.mult),
                 reads=[gC, gD, "gsr%d" % gs], writes=["gsn%d" % gs])
            for h in range(8):
                S.op("act", lambda e, h=h: e.activation(out=on[:n, h * 64:(h + 1) * 64],
                                                        in_=PB[bC][:n, h * 64:(h + 1) * 64], func=AF.Identity,
                                                        scale=gst[:n, gs, 40 + h:41 + h], bias=gst[:n, gs, 24 + h:25 + h]),
                     reads=[PBn[bC], "gsr%d" % gs, "gsn%d" % gs], writes=["on.%d" % h])
            S.op("dve", lambda e: e.tensor_tensor(out=ro[:n, :], in0=on[:n, :], in1=sg[:n, :], op=ALU.mult),
                 reads=["on.%d" % h_ for h_ in range(8)] + ["sg" + sfx], writes=["ro" + sfx])
            if kind == 0:
                gk = 1 if t == 0 else 0
                S.op("dve", lambda e: e.tensor_tensor(out=Stmp[:64, :], in0=S32[:, :, :].rearrange("p h e -> p (h e)"),
                                                      in1=PB[bD][:64, :], op=ALU.add),
                     reads=["S32", PBn[bD]], writes=["Stmp"])
                S.op("dve", lambda e: e.tensor_tensor(
                    out=S32[:, :, :], in0=Stmp[:64, :].rearrange("p (h e) -> p h e", h=8),
                    in1=gl[:64, gk, :].unsqueeze(2).to_broadcast([64, 8, 64]), op=ALU.mult),
                    reads=["Stmp", "tabs"], writes=["S32"])
                S.op("act", lambda e: e.copy(out=Sbf[:64, :, :], in_=S32[:, :, :]), reads=["S32"], writes=["Sbf"])
                if t == 16:
                    S.op("sp", lambda e: e.dma_start(out=nrp.rearrange("h d e -> d h e"), in_=S32[:, :, :]),
                         reads=["S32"], dma=True, key="o_nrp")
                    out_keys.append("o_nrp")
            pr = pbf(bD).rearrange("p (k n) -> p k n", k=8)
            for pair in range(4):
                S.op("pe", lambda e, pair=pair: e.transpose(out=pr[:, pair, :n],
                                                            in_=ro[:n, pair * 128:(pair + 1) * 128],
                                                            identity=ident[:n, :n]),
                     reads=["ro" + sfx, "ident"], writes=[PBn[bD]])

        def gamma2b(tile, p):
            (t, n, j, col) = tile
            bD = bsets[p][3]
            pr = pbf(bD).rearrange("p (k n) -> p k n", k=8)
            for pair in range(4):
                S.op("act", lambda e, pair=pair: e.activation(out=MIXT[:, 4 + pair, col:col + n], in_=pr[:, pair, :n],
                                                              func=AF.Copy, scale=rngt[:, pair:pair + 1]),
                     reads=[PBn[bD], "rngt"], writes=["MXr%d.%d" % (j, pair)])

        S.pe_scale = PE_COLD
        alpha(tl[0], 0)
        beta(tl[0], 0)
        for i, tile in enumerate(tl):
            p = i % 2
            ag = alpha_gen(tl[i + 1], 1 - p) if i + 1 < len(tl) else None
            gamma1(tile, p, ag)
            if ag is not None:
                for _ in ag:
                    pass
            gamma2a(tile, p)
            if i + 1 < len(tl):
                beta(tl[i + 1], 1 - p)
            gamma2b(tile, p)
            if i == len(tl) - 1:
                S.op("act", lambda e: e.copy(out=mk[:, 4:8], in_=tabs[:, 0:4]), reads=["MXr%d.3" % tile[2], "tabs"],
                     writes=["RETDONE"])
            if i == 1:
                load_wout(0)
                load_wout(1)

        tl3 = [x for x in tl if x[0] != 0]

        def a3_proj(tile, k3):
            (t, n, j, col) = tile
            bks = (0, 1) if k3 % 2 == 0 else (2, 3)
            for cb in range(2):
                for kc in range(8):
                    S.op("pe", lambda e, cb=cb, kc=kc: e.matmul(
                        PB[bks[cb]][:n, :], lhsT=MIXT[:, kc, col:col + n], rhs=WOUT[:, kc, cb * 512:(cb + 1) * 512],
                        start=(kc == 0), stop=(kc == 7)),
                        reads=["MXc%d.%d" % (j, c_) for c_ in range(4)] + ["MXr%d.%d" % (j, c_) for c_ in range(4)]
                        + ["WOUT%d" % (kc // 4)] + (["RETDONE"] if A3_AFTER_RET else []), writes=[PBn[bks[cb]]])

        def a3_add(tile, k3):
            (t, n, j, col) = tile
            bks = (0, 1) if k3 % 2 == 0 else (2, 3)
            for cb in range(2):
                S.op("dve", lambda e, cb=cb: e.tensor_tensor(
                    out=X[:n, j, cb * 512:(cb + 1) * 512], in0=X[:n, j, cb * 512:(cb + 1) * 512],
                    in1=PB[bks[cb]][:n, :], op=ALU.add), reads=[XR(j)[cb], PBn[bks[cb]]], writes=[XR(j)[cb]],
                    nofence=NF2)

        a3_proj(tl3[0], 0)
        for k3, tile in enumerate(tl3):
            (t, n, j, col) = tile
            if k3 + 1 < len(tl3):
                a3_proj(tl3[k3 + 1], k3 + 1)
            a3_add(tile, k3)
            norm_stats(t, n, j, 3, k3 % 2, nf=NF2)
            if k3 >= 1:
                (t_, n_, j_, col_) = tl3[k3 - 1]
                norm_apply(t_, n_, j_, col_, g2, 3, [4, 5], (k3 - 1) % 2, nf=NF2)
        (t_, n_, j_, col_) = tl3[-1]
        norm_apply(t_, n_, j_, col_, g2, 3, [4, 5], (len(tl3) - 1) % 2, nf=NF2)

        S.pe_scale = 1.0
        fence()
        tlb = [x for x in tl if x[0] != 0]
        if hf == 0:
            gch = split(16, 1040, 2)
        else:
            gch = split(0, 1088, 3)

        def load_wd(g):
            sl = g % 3
            S.op("pool", lambda e: e.dma_start(out=WD[:, sl, :, :].rearrange("p f n -> p (f n)"), in_=wdb[g]),
                 writes=["WD%d" % sl], dma=True, key="WD%d" % sl)

        def load_gu(f):
            sl = f % 3
            S.op("pool", lambda e: e.dma_start(out=GU[:, sl, :, :].rearrange("p k n -> p (k n)"), in_=gub[f]),
                 writes=GUR[sl], dma=True, key="GU%d" % sl, nofence=True)

        S.op("sp", lambda e: e.dma_start(out=gf[:, :], in_=nf.partition_broadcast(128)), writes=["gf"], dma=True,
             key="gf")
        if hf == 0:
            ld("sp", rowb[:, :], sconv.rearrange("s t c -> (s t) c"), "rowb")
            for cc_ in range(4):
                tr32(PB[7][:, cc_ * 32:(cc_ + 1) * 32], rowb[:32, cc_ * 128:(cc_ + 1) * 128], 32, ["rowb"])
            S.op("dve", lambda e: e.tensor_copy(out=csi[:, :, :, :].rearrange("p c s t -> p (c s t)"),
                                                in_=PB[7][:, 0:128]), reads=["pb7"], writes=["csi"], nofence=True)
        load_gu(0)
        load_gu(1)
        load_gu(2)
        load_wd(0)
        load_wd(1)
        def gu_chunk(f, sl, c0, c1, bg, bu, ss_):
            N = c1 - c0
            rd = ["XT%d" % j for j in overlapping(tlb, c0, c1)] + GUR[sl]
            for kc in range(8):
                S.op("pe", lambda e, kc=kc: e.matmul(
                    PB[bg][:, :N], lhsT=GU[:, sl, kc, 0:128], rhs=XT[:, kc, c0:c1], start=(kc == 0),
                    stop=(kc == 7)), reads=rd, writes=[PBn[bg]])
            for kc in range(8):
                S.op("pe", lambda e, kc=kc: e.matmul(
                    PB[bu][:, :N], lhsT=GU[:, sl, kc, 128:256], rhs=XT[:, kc, c0:c1], start=(kc == 0),
                    stop=(kc == 7)), reads=rd, writes=[PBn[bu]])
            S.op("act", lambda e: e.activation(out=sgate[:, ss_, :N], in_=PB[bg][:, :N], func=AF.Silu),
                 reads=[PBn[bg]], writes=["sgate%d" % ss_])
            S.op("dve", lambda e: e.tensor_tensor(
                out=HT[:, f, c0:c1], in0=PB[bu][:, :N], in1=sgate[:, ss_, :N], op=ALU.mult),
                reads=[PBn[bu], "sgate%d" % ss_], writes=["HT%d.%d" % (j, f) for j in overlapping(tlb, c0, c1)])

        kb = 0
        for f in range(NFF):
            for (c0, c1) in gch:
                bg, bu = ((0, 1), (2, 3), (4, 5))[kb % 3]
                gu_chunk(f, f % 3, c0, c1, bg, bu, kb % 2)
                kb += 1
            if f + 3 < NFF:
                load_gu(f + 3)
        if hf == 0:
            xstg = sgate[:, :, :].rearrange("p s n -> p (s n)")
            sres = ["sgate0", "sgate1"]
            for i_, (t, n, j, col) in enumerate(tiles_of(1)):
                src = xsamp if t == 17 else xp[128 * (t - 1):128 * t, :]
                S.op("sp", lambda e, src=src, n=n: e.dma_start(out=xstg[:n, :], in_=src), writes=sres, dma=True,
                     key="xstg")
                norm_stats(t, n, j, 0, i_ % 2, src=xstg[:n, :], src_res=sres)
                norm_apply(t, n, j, col, g1, 0, [0, 1], i_ % 2, on_dve=True, src=xstg[:n, :], src_res=sres)
        load_wd(2)
        kd = 0
        for cb in range(2):
            if cb == 1:
                load_wd(3)
            for (t, n, j, col) in tlb:
                bk = 6 + (kd % 2)
                kd += 1
                for f in range(NFF):
                    g = cb * 2 + f // 11
                    sl = g % 3
                    S.op("pe", lambda e, f=f, sl=sl, bk=bk, n=n, col=col: e.matmul(
                        PB[bk][:n, :], lhsT=HT[:, f, col:col + n], rhs=WD[:, sl, f % 11, :], start=(f == 0),
                        stop=(f == NFF - 1)), reads=["HT%d.%d" % (j, f), "WD%d" % sl], writes=[PBn[bk]])
                S.op("dve", lambda e, cb=cb, bk=bk, n=n, j=j: e.tensor_tensor(
                    out=X[:n, j, cb * 512:(cb + 1) * 512], in0=X[:n, j, cb * 512:(cb + 1) * 512],
                    in1=PB[bk][:n, :], op=ALU.add), reads=[XR(j)[cb], PBn[bk]], writes=[XR(j)[cb]], nofence=NF2)
                if cb == 1:
                    ys_ = kd % 2
                    S.op("act", lambda e, n=n, j=j, t=t, ys_=ys_: e.activation(out=YO[:n, ys_, :], in_=X[:n, j, :],
                                                                               func=AF.Square,
                                                                               accum_out=stat[:n, t, 6:7]),
                         reads=XR(j), writes=["YO%d" % ys_, "st%d.6" % t])
                    S.op("dve", lambda e, n=n, t=t: e.tensor_scalar(out=stat[:n, t, 7:8], in0=stat[:n, t, 6:7],
                                                                    scalar1=1.0 / D, scalar2=EPS, op0=ALU.mult,
                                                                    op1=ALU.add),
                         reads=["st%d.6" % t], writes=["st%d.7" % t], nofence=NF2)
                    S.op("pool", lambda e, n=n, t=t: e.tensor_tensor(out=stat[:n, t, 8:9], in0=stat[:n, t, 7:8],
                                                                     in1=neghalf[:n, 0:1], op=ALU.pow),
                         reads=["st%d.7" % t, "tabs"], writes=["st%d.8" % t], nofence=NF2)
                    S.op("dve", lambda e, n=n, j=j, t=t, ys_=ys_: e.scalar_tensor_tensor(
                        out=YO[:n, ys_, :], in0=X[:n, j, :], scalar=stat[:n, t, 8:9], in1=gf[:n, :],
                        op0=ALU.mult, op1=ALU.mult), reads=XR(j) + ["st%d.8" % t, "gf"], writes=["YO%d" % ys_])
                    dst = ys if t == 17 else yp[128 * (t - 1):128 * t, :]
                    S.op("sp", lambda e, dst=dst, n=n, ys_=ys_: e.dma_start(out=dst, in_=YO[:n, ys_, :]),
                         reads=["YO%d" % ys_], dma=True, key="o_y%d" % ys_)
        out_keys.extend(["o_y0", "o_y1"])

    S.emit(final_wait_keys=sorted(set(out_keys)), schedule=SCHEDULE)
    print("[kernel] ops=%d sim_time_us=%.1f" % (len(S.ops), getattr(S, "sim_time", -1)))
    return nc


_CACHE = {}


def kernel(x_prompt, x_sample, state_conv, state_ret, meta_tokens, norm1_g, w_in, w_conv, ret_norm_g, w_out,
           norm2_g, w_gate, w_up, w_down, final_norm_g):
    f = lambda a: np.ascontiguousarray(np.asarray(a, dtype=np.float32))
    x_prompt, x_sample, state_conv, state_ret = f(x_prompt), f(x_sample), f(state_conv), f(state_ret)
    cs, tabs, identf, smT = _tables()
    wi, wo, wg_, wu_, wd_ = f(w_in)[0], f(w_out)[0], f(w_gate)[0], f(w_up)[0], f(w_down)[0]
    c = np.ascontiguousarray
    cwb = c(wi[:, :1536].reshape(8, 128, 3, 4, 128).transpose(3, 1, 0, 2, 4)).reshape(4, 128, 8 * 384)
    wretb = c(wi[:, 1536:].reshape(8, 128, 2048).transpose(1, 0, 2)).reshape(128, 8 * 2048)
    woutb = c(wo.reshape(8, 128, D).transpose(1, 0, 2)).reshape(128, 8 * D)
    gu = np.stack([wg_.reshape(8, 128, NFF, 128), wu_.reshape(8, 128, NFF, 128)], axis=3)
    gub = c(gu.transpose(2, 1, 0, 3, 4)).reshape(NFF, 128, 8 * 256)
    wdb = c(wd_.reshape(2, 11, 128, 2, 512).transpose(3, 0, 2, 1, 4)).reshape(4, 128, 11 * 512)
    shared = {
        "meta": f(meta_tokens), "w_conv": f(w_conv)[0], "rng": f(ret_norm_g)[0],
        "n1": f(norm1_g)[0], "n2": f(norm2_g)[0], "nf": f(final_norm_g),
        "cwb": cwb, "wretb": wretb, "woutb": woutb, "gub": gub, "wdb": wdb,
        "cs": cs, "tabs": tabs, "identf": identf, "smT": smT,
    }
    in_maps = []
    for c in range(8):
        m = dict(shared)
        m["xp"] = x_prompt[c]
        m["xsamp"] = x_sample[16 * c:16 * c + 16].reshape(64, D)
        m["sconv"] = state_conv[0, 16 * c:16 * c + 16]
        m["sret"] = state_ret[0, 16 * c:16 * c + 16]
        in_maps.append(m)
    if "nc" not in _CACHE:
        _CACHE["nc"] = build_nc()
    res = run_bass_kernel_spmd(_CACHE["nc"], in_maps, core_ids=list(range(8)))
    r = res.results
    y_prompt = np.stack([r[c]["yp"] for c in range(8)], 0)
    y_sample = np.concatenate([r[c]["ys"].reshape(16, 4, D) for c in range(8)], 0)
    ncp = np.stack([r[c]["ncp"] for c in range(8)], 0)[None]
    nrp = np.stack([r[c]["nrp"] for c in range(8)], 0)[None]
    ncs = np.concatenate([r[c]["ncs"] for c in range(8)], 0)[None]
    nrs = np.concatenate([r[c]["nrs"] for c in range(8)], 0)[None]
    return (y_prompt.astype(np.float32), y_sample.astype(np.float32), ncp.astype(np.float32),
            nrp.astype(np.float32), ncs.astype(np.float32), nrs.astype(np.float32))
```
